# Optimizing a Trainium2 kernel written in Bass

```python
import math
import jax, jax.numpy as jnp
from jax import lax
import numpy as np

D_MODEL = 1024
BATCH = 4
SEQ = 4096
DEPTH = 1
DEC_BATCH = 32
DEC_SEQ = 1
PAST_LEN = 16384
PAGE_SIZE = 128

SSM_WIDTH = D_MODEL
SSM_GROUP = 16
SSM_GROUPS = SSM_WIDTH // SSM_GROUP
SSM_STATE = 64
SSM_CHUNK = 128
DT_MIN = 0.001
DT_MAX = 0.1
HEAD_DIM = 64
HEADS_PER_GROUP = 4
DIL_WINDOWS = (128, 512, 2048)
DIL_RATES = (1, 4, 16)
N_DIL_GROUPS = 3
ATTN_HEADS = N_DIL_GROUPS * HEADS_PER_GROUP
ATTN_WIDTH = ATTN_HEADS * HEAD_DIM
SLOT_WIDTH = HEADS_PER_GROUP * HEAD_DIM
Q_BLOCK = 128
ALIBI_MAX_EXP = 8.0
D_FF = 2816
RMS_EPS = 1e-6
IN_WIDTH = SSM_WIDTH + 3 * ATTN_WIDTH + 2 * D_MODEL

kernel_name = "hybrid_s5_dilated_attn_macaron_step"

F32 = jnp.float32


def _rmsnorm(x, w):
    x32 = x.astype(F32)
    y = x32 * lax.rsqrt(jnp.mean(x32 * x32, axis=-1, keepdims=True) + RMS_EPS)
    return (y * w.astype(F32)).astype(x.dtype)


def _swiglu(h, w_gate, w_up, w_down):
    return (jax.nn.silu(h @ w_gate) * (h @ w_up)) @ w_down


def _cmul(ar, ai, br, bi):
    return ar * br - ai * bi, ar * bi + ai * br


def _scan_op(e1, e2):
    a1r, a1i, b1r, b1i = e1
    a2r, a2i, b2r, b2i = e2
    ar, ai = _cmul(a2r, a2i, a1r, a1i)
    br, bi = _cmul(a2r, a2i, b1r, b1i)
    return ar, ai, br + b2r, bi + b2i


def _s5_discretise(lambda_re, lambda_im, b_re, b_im, log_dt):
    lr = jnp.minimum(lambda_re.astype(F32), -1e-4)
    li = lambda_im.astype(F32)
    dt = jnp.exp(log_dt.astype(F32))[:, None]
    mag = jnp.exp(lr * dt)
    abar_re = mag * jnp.cos(li * dt)
    abar_im = mag * jnp.sin(li * dt)
    nr = abar_re - 1.0
    ni = abar_im
    den = lr * lr + li * li
    fr = (nr * lr + ni * li) / den
    fi = (ni * lr - nr * li) / den
    bbar_re, bbar_im = _cmul(fr[..., None], fi[..., None], b_re.astype(F32), b_im.astype(F32))
    return abar_re, abar_im, bbar_re, bbar_im


def _s5_scan(u, h0_re, h0_im, abar_re, abar_im, bbar_re, bbar_im, c_re, c_im):
    bt, length = u.shape[:2]
    chunk = SSM_CHUNK if length % SSM_CHUNK == 0 else length
    n_chunks = length // chunk
    uc = u.reshape(bt, n_chunks, chunk, SSM_GROUPS, SSM_GROUP).swapaxes(0, 1)
    a_shape = (bt, chunk, SSM_GROUPS, SSM_STATE)
    ar = jnp.broadcast_to(abar_re, a_shape)
    ai = jnp.broadcast_to(abar_im, a_shape)
    c_re32 = c_re.astype(F32)
    c_im32 = c_im.astype(F32)

    def step(carry, u_blk):
        hr, hi = carry
        br = jnp.einsum('btgc,gpc->btgp', u_blk, bbar_re)
        bi = jnp.einsum('btgc,gpc->btgp', u_blk, bbar_im)
        pr, pi, sr, si = lax.associative_scan(_scan_op, (ar, ai, br, bi), axis=1)
        xr = pr * hr[:, None] - pi * hi[:, None] + sr
        xi = pr * hi[:, None] + pi * hr[:, None] + si
        y = jnp.einsum('btgp,gkp->btgk', xr, c_re32) - jnp.einsum('btgp,gkp->btgk', xi, c_im32)
        return (xr[:, -1], xi[:, -1]), y

    (hr, hi), ys = lax.scan(step, (h0_re, h0_im), uc)
    return ys.swapaxes(0, 1).reshape(bt, length, SSM_WIDTH), hr, hi


def _dilated_attention(q, kv, q_pos, window, dilation, slopes):
    dist = jnp.arange(window // dilation + 1, dtype=jnp.int32) * dilation
    penalty = slopes[:, None] * dist.astype(F32)[None, :]

    def attend(q_blk, pos_blk):
        idx = pos_blk[:, None] - dist[None, :]
        valid = idx >= 0
        kv_g = jnp.take(kv, jnp.maximum(idx, 0), axis=1)
        s = jnp.einsum('bqhd,bqkhd->bhqk', q_blk, kv_g[:, :, :, 0], preferred_element_type=F32)
        s = s - penalty[None, :, None, :]
        s = jnp.where(valid[None, None], s, -jnp.inf)
        m = jnp.max(s, axis=-1, keepdims=True)
        p = jnp.exp(s - m)
        den = jnp.sum(p, axis=-1, keepdims=True)
        o = jnp.einsum('bhqk,bqkhd->bqhd', (p / den).astype(kv.dtype), kv_g[:, :, :, 1])
        lse = (m + jnp.log(den))[..., 0].transpose(0, 2, 1)
        return o, lse

    bt, lq = q.shape[:2]
    if lq % Q_BLOCK == 0 and lq > Q_BLOCK:
        nb = lq // Q_BLOCK
        qb = q.reshape(bt, nb, Q_BLOCK, HEADS_PER_GROUP, HEAD_DIM).swapaxes(0, 1)
        pb = q_pos.reshape(nb, Q_BLOCK)
        o, lse = lax.map(lambda a: attend(a[0], a[1]), (qb, pb))
        o = o.swapaxes(0, 1).reshape(bt, lq, HEADS_PER_GROUP, HEAD_DIM)
        lse = lse.swapaxes(0, 1).reshape(bt, lq, HEADS_PER_GROUP)
        return o, lse
    return attend(q, q_pos)


def _layer(x, h0_re, h0_im, kv_past, keep_rows, p, slopes):
    bt, length = x.shape[:2]
    x = x + 0.5 * _swiglu(_rmsnorm(x, p['ffn1_norm']), p['ffn1_w_gate'], p['ffn1_w_up'], p['ffn1_w_down'])
    h = _rmsnorm(x, p['mix_norm'])
    z = h @ p['w_in']
    cuts = [SSM_WIDTH, SSM_WIDTH + ATTN_WIDTH, SSM_WIDTH + 2 * ATTN_WIDTH,
            SSM_WIDTH + 3 * ATTN_WIDTH, SSM_WIDTH + 3 * ATTN_WIDTH + D_MODEL]
    u_a, q, k, v, g_a, g_b = jnp.split(z, cuts, axis=-1)

    abar_re, abar_im, bbar_re, bbar_im = _s5_discretise(
        p['ssm_lambda_re'], p['ssm_lambda_im'], p['ssm_b_re'], p['ssm_b_im'], p['ssm_log_dt'])
    u32 = u_a.astype(F32)
    y_ssm, hr, hi = _s5_scan(u32, h0_re.astype(F32), h0_im.astype(F32),
                             abar_re, abar_im, bbar_re, bbar_im, p['ssm_c_re'], p['ssm_c_im'])
    y_ssm = jax.nn.gelu(y_ssm + p['ssm_d'].astype(F32) * u32).astype(x.dtype)
    y_a = y_ssm * jax.nn.sigmoid(y_ssm @ p['w_glu'])

    hs = (bt, length, N_DIL_GROUPS, HEADS_PER_GROUP, HEAD_DIM)
    q = _rmsnorm(q.reshape(hs), p['q_gain']) * (HEAD_DIM ** -0.5)
    k = _rmsnorm(k.reshape(hs), p['k_gain'])
    kv_new = jnp.stack([k, v.reshape(hs)], axis=3)
    outs, lses, new_kv = [], [], []
    for g in range(N_DIL_GROUPS):
        kv_g = kv_new[:, :, g]
        if kv_past is None:
            kv_all = kv_g
            offset = 0
        else:
            kv_all = jnp.concatenate([kv_past[g].astype(kv_g.dtype), kv_g], axis=1)
            offset = kv_past[g].shape[1]
        pos = offset + jnp.arange(length, dtype=jnp.int32)
        o, lse = _dilated_attention(q[:, :, g], kv_all, pos, DIL_WINDOWS[g], DIL_RATES[g], slopes[g])
        outs.append(o)
        lses.append(lse)
        new_kv.append(kv_all[:, kv_all.shape[1] - keep_rows[g]:])
    mix_w = jax.nn.softmax(jnp.stack(lses, axis=0), axis=0)
    o_b = jnp.sum(mix_w[..., None] * jnp.stack(outs, axis=0).astype(F32), axis=0)
    o_b = o_b.astype(x.dtype).reshape(bt, length, SLOT_WIDTH)

    merged = jax.nn.sigmoid(g_a) * (y_a @ p['w_proj_a']) + jax.nn.sigmoid(g_b) * (o_b @ p['w_proj_b'])
    x = x + merged @ p['w_out']
    x = x + 0.5 * _swiglu(_rmsnorm(x, p['ffn2_norm']), p['ffn2_w_gate'], p['ffn2_w_up'], p['ffn2_w_down'])
    return x, hr.astype(h0_re.dtype), hi.astype(h0_im.dtype), new_kv


def setup_inputs(seed: int = 0) -> dict:
    key = jax.random.key(seed)
    ks = iter(jax.random.split(key, 40))
    nrm = lambda shape, s: jax.random.normal(next(ks), shape, F32) * s
    L = DEPTH
    inp = {}
    inp['x_prompt'] = nrm((BATCH, SEQ, D_MODEL), 1.0)
    inp['x_sample'] = nrm((DEC_BATCH, DEC_SEQ, D_MODEL), 1.0)
    inp['state_ssm_re'] = nrm((L, DEC_BATCH, SSM_GROUPS, SSM_STATE), 0.5)
    inp['state_ssm_im'] = nrm((L, DEC_BATCH, SSM_GROUPS, SSM_STATE), 0.5)
    for w in DIL_WINDOWS:
        inp['cache_kv_w%d' % w] = nrm((L, DEC_BATCH, min(w, PAST_LEN), 2, HEADS_PER_GROUP, HEAD_DIM), 1.0)
    inp['ffn1_norm'] = 1.0 + nrm((L, D_MODEL), 0.02)
    inp['ffn1_w_gate'] = nrm((L, D_MODEL, D_FF), D_MODEL ** -0.5)
    inp['ffn1_w_up'] = nrm((L, D_MODEL, D_FF), D_MODEL ** -0.5)
    inp['ffn1_w_down'] = nrm((L, D_FF, D_MODEL), D_FF ** -0.5)
    inp['mix_norm'] = 1.0 + nrm((L, D_MODEL), 0.02)
    inp['w_in'] = nrm((L, D_MODEL, IN_WIDTH), D_MODEL ** -0.5)
    inp['ssm_lambda_re'] = -0.5 + nrm((L, SSM_GROUPS, SSM_STATE), 0.01)
    inp['ssm_lambda_im'] = jnp.pi * jnp.arange(SSM_STATE, dtype=F32) + nrm((L, SSM_GROUPS, SSM_STATE), 0.01)
    inp['ssm_b_re'] = nrm((L, SSM_GROUPS, SSM_STATE, SSM_GROUP), (2 * SSM_GROUP) ** -0.5)
    inp['ssm_b_im'] = nrm((L, SSM_GROUPS, SSM_STATE, SSM_GROUP), (2 * SSM_GROUP) ** -0.5)
    inp['ssm_c_re'] = nrm((L, SSM_GROUPS, SSM_GROUP, SSM_STATE), SSM_STATE ** -0.5)
    inp['ssm_c_im'] = nrm((L, SSM_GROUPS, SSM_GROUP, SSM_STATE), SSM_STATE ** -0.5)
    inp['ssm_d'] = nrm((L, SSM_WIDTH), 1.0)
    inp['ssm_log_dt'] = jax.random.uniform(next(ks), (L, SSM_GROUPS), F32,
                                           minval=math.log(DT_MIN), maxval=math.log(DT_MAX))
    inp['w_glu'] = nrm((L, SSM_WIDTH, SSM_WIDTH), SSM_WIDTH ** -0.5)
    inp['q_gain'] = 1.0 + nrm((L, HEAD_DIM), 0.02)
    inp['k_gain'] = 1.0 + nrm((L, HEAD_DIM), 0.02)
    inp['w_proj_a'] = nrm((L, SSM_WIDTH, D_MODEL), SSM_WIDTH ** -0.5)
    inp['w_proj_b'] = nrm((L, SLOT_WIDTH, D_MODEL), SLOT_WIDTH ** -0.5)
    inp['w_out'] = nrm((L, D_MODEL, D_MODEL), D_MODEL ** -0.5)
    inp['ffn2_norm'] = 1.0 + nrm((L, D_MODEL), 0.02)
    inp['ffn2_w_gate'] = nrm((L, D_MODEL, D_FF), D_MODEL ** -0.5)
    inp['ffn2_w_up'] = nrm((L, D_MODEL, D_FF), D_MODEL ** -0.5)
    inp['ffn2_w_down'] = nrm((L, D_FF, D_MODEL), D_FF ** -0.5)
    return inp


def reference(x_prompt, x_sample, state_ssm_re, state_ssm_im, cache_kv_w128, cache_kv_w512, cache_kv_w2048,
              ffn1_norm, ffn1_w_gate, ffn1_w_up, ffn1_w_down, mix_norm, w_in,
              ssm_lambda_re, ssm_lambda_im, ssm_b_re, ssm_b_im, ssm_c_re, ssm_c_im, ssm_d, ssm_log_dt,
              w_glu, q_gain, k_gain, w_proj_a, w_proj_b, w_out,
              ffn2_norm, ffn2_w_gate, ffn2_w_up, ffn2_w_down):
    slopes = jnp.exp2(-ALIBI_MAX_EXP * jnp.arange(1, ATTN_HEADS + 1, dtype=F32) / ATTN_HEADS)
    slopes = slopes.reshape(N_DIL_GROUPS, HEADS_PER_GROUP)
    yp, ys = x_prompt, x_sample
    p_re, p_im, s_re, s_im = [], [], [], []
    p_kv = [[], [], []]
    s_kv = [[], [], []]
    for l in range(DEPTH):
        p = dict(ffn1_norm=ffn1_norm[l], ffn1_w_gate=ffn1_w_gate[l], ffn1_w_up=ffn1_w_up[l],
                 ffn1_w_down=ffn1_w_down[l], mix_norm=mix_norm[l], w_in=w_in[l],
                 ssm_lambda_re=ssm_lambda_re[l], ssm_lambda_im=ssm_lambda_im[l],
                 ssm_b_re=ssm_b_re[l], ssm_b_im=ssm_b_im[l], ssm_c_re=ssm_c_re[l], ssm_c_im=ssm_c_im[l],
                 ssm_d=ssm_d[l], ssm_log_dt=ssm_log_dt[l], w_glu=w_glu[l], q_gain=q_gain[l], k_gain=k_gain[l],
                 w_proj_a=w_proj_a[l], w_proj_b=w_proj_b[l], w_out=w_out[l],
                 ffn2_norm=ffn2_norm[l], ffn2_w_gate=ffn2_w_gate[l], ffn2_w_up=ffn2_w_up[l],
                 ffn2_w_down=ffn2_w_down[l])
        h0 = jnp.zeros((yp.shape[0], SSM_GROUPS, SSM_STATE), yp.dtype)
        keep_p = tuple(min(w, yp.shape[1]) for w in DIL_WINDOWS)
        yp, hr, hi, kvp = _layer(yp, h0, h0, None, keep_p, p, slopes)
        past = (cache_kv_w128[l], cache_kv_w512[l], cache_kv_w2048[l])
        keep_s = tuple(c.shape[1] for c in past)
        ys, sr, si, kvs = _layer(ys, state_ssm_re[l], state_ssm_im[l], past, keep_s, p, slopes)
        p_re.append(hr)
        p_im.append(hi)
        s_re.append(sr)
        s_im.append(si)
        for g in range(N_DIL_GROUPS):
            p_kv[g].append(kvp[g])
            s_kv[g].append(kvs[g])
    return (yp, ys,
            jnp.stack(p_re), jnp.stack(p_im),
            jnp.stack(p_kv[0]), jnp.stack(p_kv[1]), jnp.stack(p_kv[2]),
            jnp.stack(s_re), jnp.stack(s_im),
            jnp.stack(s_kv[0]), jnp.stack(s_kv[1]), jnp.stack(s_kv[2]))
```

```python
import contextlib
import math
import numpy as np
import concourse.bass as bass
import concourse.mybir as mybir
from concourse.bass_utils import run_bass_kernel_spmd

F32 = mybir.dt.float32
BF16 = mybir.dt.bfloat16
ALU = mybir.AluOpType
AF = mybir.ActivationFunctionType
AX = mybir.AxisListType

ENGS = ["pe", "act", "dve", "pool", "sp"]
T = 512
NPF = 11
TC = 32
NCH = T // TC
DIL = (1, 4, 16)
WIN = (128, 512, 2048)
SLOPES = [2.0 ** (-8.0 * i / 12.0) for i in range(1, 13)]
MASKV = 1.0e5


class Sched:
    def __init__(self, nc, stack, n_dma_slots=40):
        self.nc = nc
        self.ops = {e: [] for e in ENGS}
        self.sem = {e: stack.enter_context(nc.semaphore("s_" + e)) for e in ENGS if e != "sp"}
        self.cnt = {e: 0 for e in ENGS}
        self.dsem = [stack.enter_context(nc.semaphore("d%d" % i)) for i in range(n_dma_slots)]
        self.dcnt = [0] * n_dma_slots
        self.dnext = 0
        self.dnext_sw = 0
        self.waited = {e: {} for e in ENGS}
        self.lastw = {}
        self.readers = {}

    def _semof(self, pk):
        return self.sem[pk[1]] if pk[0] == "e" else self.dsem[pk[1]]

    def _need(self, eng, waits, pk, v):
        if pk == ("e", eng) and eng == "pe":
            return
        if self.waited[eng].get(pk, 0) >= v:
            return
        self.waited[eng][pk] = v
        waits[pk] = v

    def _deps(self, eng, reads, writes):
        waits = {}
        for r in reads:
            lw = self.lastw.get(r)
            if lw is not None:
                self._need(eng, waits, lw[0], lw[1])
        for w in writes:
            lw = self.lastw.get(w)
            if lw is not None:
                self._need(eng, waits, lw[0], lw[1])
            for pk, v in self.readers.get(w, {}).items():
                self._need(eng, waits, pk, v)
        return waits

    def _record(self, pk, val, reads, writes):
        for r in reads:
            self.readers.setdefault(r, {})[pk] = val
        for w in writes:
            self.lastw[w] = (pk, val)
            self.readers[w] = {}

    def op(self, eng, fn, reads=(), writes=()):
        waits = self._deps(eng, reads, writes)
        self.cnt[eng] += 1
        self.ops[eng].append((list(waits.items()), fn, ("e", eng), 1))
        self._record(("e", eng), self.cnt[eng], reads, writes)

    def dma(self, q, out, in_, reads=(), writes=()):
        half = len(self.dsem) // 2
        if q == "pool":
            slot = half + self.dnext_sw
            self.dnext_sw = (self.dnext_sw + 1) % (len(self.dsem) - half)
        else:
            slot = self.dnext
            self.dnext = (self.dnext + 1) % half
        waits = self._deps(q, reads, writes)
        k = self.dcnt[slot] + 1
        self.dcnt[slot] = k
        if k > 1:
            self._need(q, waits, ("d", slot), 16 * (k - 1))
        fn = lambda e, out=out, in_=in_: e.dma_start(out=out, in_=in_)
        self.ops[q].append((list(waits.items()), fn, ("d", slot), 16))
        self._record(("d", slot), 16 * k, reads, writes)

    def barrier(self):
        for e in ENGS:
            waits = {}
            for slot in range(len(self.dsem)):
                if self.dcnt[slot]:
                    self._need(e, waits, ("d", slot), 16 * self.dcnt[slot])
            for p in ENGS:
                if p != "sp" and p != e and self.cnt[p]:
                    self._need(e, waits, ("e", p), self.cnt[p])
            if waits:
                self.ops[e].append((list(waits.items()), None, None, 0))

    def finish(self):
        waits = {}
        for slot in range(len(self.dsem)):
            if self.dcnt[slot]:
                self._need("sp", waits, ("d", slot), 16 * self.dcnt[slot])
        for e in ENGS:
            if e != "sp" and self.cnt[e]:
                self._need("sp", waits, ("e", e), self.cnt[e])
        self.ops["sp"].append((list(waits.items()), None, None, 0))

    def emit(self):
        nc = self.nc
        with nc.Block() as block:
            def run(e, name):
                for waits, fn, pk, inc in self.ops[name]:
                    for wpk, v in waits:
                        e.wait_ge(self._semof(wpk), v)
                    if fn is not None:
                        fn(e).then_inc(self._semof(pk), inc)

            @block.tensor
            def _(e):
                run(e, "pe")

            @block.scalar
            def _(e):
                run(e, "act")

            @block.vector
            def _(e):
                run(e, "dve")

            @block.gpsimd
            def _(e):
                run(e, "pool")

            @block.sync
            def _(e):
                run(e, "sp")


def lay_w8(w, piece=256):
    n = w.shape[1] // piece
    return np.ascontiguousarray(w.reshape(8, 128, n, piece).transpose(2, 1, 0, 3))


def lay_wk(w, nk, piece=256):
    n = w.shape[1] // piece
    return np.ascontiguousarray(w.reshape(nk, 128, n, piece).transpose(2, 1, 0, 3))


def lay_wd(w):
    return np.ascontiguousarray(w.reshape(NPF, 2, 128, 2, 512).transpose(3, 0, 2, 1, 4))


def lay_vec(v):
    return np.ascontiguousarray(v.reshape(8, 128).T)


def lay_gp(a):
    return np.ascontiguousarray(a.reshape(32, 2, 64).transpose(1, 2, 0).reshape(128, 32))


def host_consts(half_is_first):
    c = {}
    c["ident"] = np.eye(128, dtype=np.float32)
    ob = np.zeros((128, 128), np.float32)
    ob[:64, :64] = 1.0 / 64
    ob[64:, 64:] = 1.0 / 64
    c["onesblk"] = ob
    k = np.arange(128)[:, None].astype(np.float32)
    q = np.arange(128)[None, :].astype(np.float32)
    cur = np.where(k <= q, q - k, MASKV).astype(np.float32)
    prev = np.where(k >= q, q + 128 - k, MASKV).astype(np.float32)
    prevh = np.full((128, 128), MASKV, np.float32) if half_is_first else prev
    c["dm"] = np.ascontiguousarray(np.stack([cur, prev, prevh], axis=1))
    m = np.ones((128, 2 * T), np.float32)
    m[:, ::TC] = 0.0
    c["scanmask"] = m
    pen = np.zeros((128, 12), np.float32)
    for g in range(3):
        for h in range(4):
            pen[:, 4 * g + h] = -8.0 * SLOPES[4 * g + h] * (WIN[g] - DIL[g] * np.arange(128))
    c["pens"] = pen
    sel = np.zeros((4, 4, 128), np.float32)
    for b in range(4):
        sel[b, b, :] = 1.0
    c["sel4"] = sel
    selc = np.zeros((128, 4, 4), np.float32)
    for b in range(4):
        selc[:, b, b] = 1.0
    c["selc"] = selc
    return c


def prep_shared(inp):
    L = 0
    d = {}
    for nm, key in (("1", "ffn1"), ("2", "ffn2")):
        d["wg" + nm] = lay_w8(inp[key + "_w_gate"][L])
        d["wu" + nm] = lay_w8(inp[key + "_w_up"][L])
        d["wd" + nm] = lay_wd(inp[key + "_w_down"][L])
    w_in = inp["w_in"][L]
    d["w_u"] = lay_w8(w_in[:, 0:1024])
    d["w_q"] = lay_w8(w_in[:, 1024:1792])
    d["w_k"] = lay_w8(w_in[:, 1792:2560])
    wk = w_in[:, 1792:2560].reshape(1024, 3, 256)
    wv = w_in[:, 2560:3328].reshape(1024, 3, 256)
    wkv = np.concatenate([wk, wv], axis=2).reshape(1024, 3 * 512)
    d["w_kv"] = lay_w8(wkv, piece=512)
    wq = w_in[:, 1024:1792].reshape(1024, 3, 256)
    d["w_qs"] = lay_w8(np.ascontiguousarray(wq.reshape(1024, 768)), piece=256)
    d["w_ga"] = lay_w8(w_in[:, 3328:4352])
    d["w_gb"] = lay_w8(w_in[:, 4352:5376])
    d["w_glu"] = lay_w8(inp["w_glu"][L])
    d["w_pa"] = lay_w8(inp["w_proj_a"][L])
    d["w_out"] = lay_w8(inp["w_out"][L])
    d["w_pb"] = lay_wk(inp["w_proj_b"][L], 2)
    vecs = np.zeros((128, 40), np.float32)
    vecs[:, 0:8] = lay_vec(inp["ffn1_norm"][L])
    vecs[:, 8:16] = lay_vec(inp["mix_norm"][L])
    vecs[:, 16:24] = lay_vec(inp["ffn2_norm"][L])
    vecs[:, 24:32] = lay_vec(inp["ssm_d"][L])
    vecs[:, 32] = np.tile(inp["q_gain"][L], 2)
    vecs[:, 33] = np.tile(inp["k_gain"][L], 2)
    d["vecs"] = vecs
    d["gain_bc"] = np.ascontiguousarray(np.stack([np.tile(inp["q_gain"][L][None, :], (128, 4)),
                                                  np.tile(inp["k_gain"][L][None, :], (128, 4))], axis=1))
    sc = np.stack([lay_gp(inp["ssm_lambda_re"][L]), lay_gp(inp["ssm_lambda_im"][L]),
                   lay_gp(np.tile(inp["ssm_log_dt"][L][:, None], (1, 64)))], axis=1)
    d["ssm_sc"] = np.ascontiguousarray(sc)
    B = np.stack([inp["ssm_b_re"][L], inp["ssm_b_im"][L]], axis=0)
    B = B.reshape(2, 32, 2, 64, 16).transpose(2, 3, 0, 1, 4).reshape(128, 2, 32, 16)
    d["ssm_B"] = np.ascontiguousarray(B)
    C = np.stack([inp["ssm_c_re"][L], inp["ssm_c_im"][L]], axis=0)
    CT = np.zeros((128, 2, 32, 128), np.float32)
    for P in range(32):
        qq = P % 4
        for g2 in range(2):
            g = 2 * P + g2
            for ri in range(2):
                CT[g2 * 64:(g2 + 1) * 64, ri, P, 32 * qq + 16 * g2: 32 * qq + 16 * g2 + 16] = C[ri, g].T
    d["ssm_CT"] = CT
    return d


class K:
    def __init__(self, n_halo, n_own, sample, dbg=None, level=9):
        self.level = level
        self.pipe_halo = True
        self.pre_ffn1 = set()
        self.n_halo, self.n_own, self.sample = n_halo, n_own, sample
        self.NT = n_halo + n_own
        self.dbg = dbg or []
        self.nc = bass.Bass("TRN2", target_bir_lowering=False)
        self.din = {}
        self.dout = {}

    def di(self, name, shape):
        self.din[name] = self.nc.dram_tensor(name, list(shape), F32, kind="ExternalInput").ap()
        return self.din[name]

    def do(self, name, shape):
        self.dout[name] = self.nc.dram_tensor(name, list(shape), F32, kind="ExternalOutput").ap()
        return self.dout[name]

    def sb(self, name, shape, dt):
        return self.st.enter_context(self.nc.sbuf_tensor(name, list(shape), dt))

    def build(self):
        nc = self.nc
        NT = self.NT
        di, do = self.di, self.do
        di("xT", [1024, NT * T])
        for nm in ("1", "2"):
            di("wg" + nm, [NPF, 128, 8, 256]); di("wu" + nm, [NPF, 128, 8, 256]); di("wd" + nm, [2, NPF, 128, 2, 512])
        for nm in ("w_u", "w_ga", "w_gb", "w_glu", "w_pa", "w_out"):
            di(nm, [4, 128, 8, 256])
        di("w_q", [3, 128, 8, 256]); di("w_k", [3, 128, 8, 256]); di("w_qs", [3, 128, 8, 256])
        di("w_kv", [3, 128, 8, 512])
        di("w_pb", [4, 128, 2, 256])
        di("vecs", [128, 40]); di("gain_bc", [128, 2, 256])
        di("ssm_sc", [128, 3, 32]); di("ssm_B", [128, 2, 32, 16]); di("ssm_CT", [128, 2, 32, 128])
        di("ident", [128, 128]); di("onesblk", [128, 128]); di("dm", [128, 3, 128]); di("scanmask", [128, 2 * T])
        do("yT", [1024, self.n_own * T])
        do("pst", [128, 2, 32])
        do("pkv", [3, self.n_own * T, 512])
        if self.sample:
            di("xsT", [1024, 4]); di("h0", [128, 2, 32, 4])
            di("c0", [4, 128, 512]); di("c1", [4, 512, 512]); di("c2", [4, 2048, 512])
            di("pens", [128, 12]); di("sel4", [4, 4, 128]); di("selc", [128, 4, 4])
            do("ysT", [1024, 4]); do("sst", [128, 2, 32, 4])
            do("skv0", [4, 128, 512]); do("skv1", [4, 512, 512]); do("skv2", [4, 2048, 512])
        for name, shape in self.dbg:
            do(name, shape)
        self.bt_d = nc.dram_tensor("bt_scr", [8, 128, 4, 2, 128], BF16, kind="Internal").ap()

        with contextlib.ExitStack() as st:
            self.st = st
            self.S = Sched(nc, st)
            self.alloc()
            self.setup()
            if self.pipe_halo and self.n_halo > 0:
                self.halo_phase()
                for ti in range(self.n_halo, NT):
                    self.tile(ti)
            else:
                for ti in range(NT):
                    self.tile(ti)
            if self.sample:
                self.sample_pass()
            self.S.finish()
            self.S.emit()
        return nc

    def alloc(self):
        sb = self.sb
        nc = self.nc
        NT = self.NT
        self.ps = self.st.enter_context(nc.psum_tensor("ps", [128, 8, 512], F32))
        self.xT = sb("xT_s", [128, 8, T], F32)
        self.hT = sb("hT_s", [128, 8, T], BF16)
        self.wA = [sb("wA%d" % i, [128, 8, 256], BF16) for i in range(4)]
        self.wD = [sb("wD%d" % i, [128, 2, 512], BF16) for i in range(3)]
        self.wKV = sb("wKV", [128, 8, 512], BF16)
        self.kT = [sb("kT0", [128, 2, 2 * T], BF16), sb("kT1", [128, 2, 2 * T], BF16), sb("kT2", [128, 2, ((NT + 3) // 4) * 4 * T], BF16)]
        self.V = [sb("V0", [128, 8, 4, 66], BF16), sb("V1", [128, 8, 4, 66], BF16), sb("V2", [128, 32, 4, 66], BF16)]
        self.tab = sb("tab_s", [128, 4, 32, TC], F32)
        self.apw = sb("apw_s", [128, 10, 2, 32], F32)
        self.ainv = sb("ainv", [128, 2, 32], F32)
        self.Kc = sb("Kcarry", [128, 2, 32], F32)
        self.vecs = sb("vecs_s", [128, 40], F32)
        self.gain_bc = sb("gain_bc_s", [128, 2, 256], F32)
        self.ident = sb("ident_b", [128, 128], BF16)
        self.identf = sb("ident_f", [128, 128], F32)
        self.onesblk = sb("onesblk_b", [128, 128], BF16)
        self.ones = sb("ones_b", [128, 128], BF16)
        self.ones1 = sb("ones1_f", [128, 64], F32)
        self.dm = sb("dm_s", [128, 3, 128], F32)
        self.scanmask = sb("scanmask_s", [128, 2 * T], F32)
        self.diagD = sb("diagD", [128, 8, 128], BF16)
        self.eps = sb("eps_s", [128, 1], F32)
        self.negpi = sb("negpi_s", [128, 1], F32)
        self.rstd = sb("rstd_s", [128, T], F32)
        self.sqb = sb("sqb_s", [128, T], BF16)
        SCR = 18304
        self.scr = sb("scr", [128, SCR], F32)

    def carve(self):
        scr = self.scr
        off = [0]

        def f32(n):
            v = scr[:, off[0]:off[0] + n]
            off[0] += n
            return v

        def b16(n):
            w = (n + 1) // 2
            v = scr[:, off[0]:off[0] + w].bitcast(BF16)
            off[0] += w
            return v
        return f32, b16, off

    def setup(self):
        S, nc = self.S, self.nc
        d = self.din
        S.dma("sp", self.vecs[:, :], d["vecs"], writes=["vecs"])
        S.dma("sp", self.gain_bc[:, :, :], d["gain_bc"], writes=["gain_bc"])
        S.dma("sp", self.identf[:, :], d["ident"], writes=["identf"])
        S.dma("sp", self.dm[:, :, :], d["dm"], writes=["dm"])
        S.dma("sp", self.scanmask[:, :], d["scanmask"], writes=["scanmask"])
        S.dma("pool", self.ident[:, :], d["ident"], writes=["ident"])
        S.dma("pool", self.onesblk[:, :], d["onesblk"], writes=["onesblk"])
        S.op("pool", lambda e: e.memset(self.ones[:, :], 1.0 / 1024.0), writes=["ones"])
        S.op("pool", lambda e: e.memset(self.ones1[:, :], 1.0), writes=["ones1"])
        S.op("pool", lambda e: e.memset(self.eps[:, :], 1e-6), writes=["eps"])
        S.op("pool", lambda e: e.memset(self.negpi[:, :], -math.pi), writes=["negpi"])
        S.op("pool", lambda e: e.memset(self.Kc[:, :, :], 0.0), writes=["Kc"])
        for g in range(3):
            S.op("pool", lambda e, g=g: e.memset(self.kT[g][:, :, :], 0.0), writes=[("kT", g)])
            S.op("pool", lambda e, g=g: e.memset(self.V[g][:, :, :, :], 0.0), writes=[("V", g)])
            S.op("pool", lambda e, g=g: e.memset(self.V[g][:, :, :, 64:65], 1.0), writes=[("V", g)])
        for c in range(8):
            S.op("dve", lambda e, c=c: e.tensor_scalar(out=self.diagD[:, c, :], in0=self.identf[:, :], scalar1=self.vecs[:, 24 + c:25 + c],
                                                       scalar2=None, op0=ALU.mult),
                 reads=["identf", "vecs"], writes=["diagD"])
        if self.pipe_halo and self.n_halo > 0:
            rec = []
            real = self.S

            class _Rec:
                def op(self_, *a, **k):
                    rec.append(("op", a, k))

                def dma(self_, *a, **k):
                    rec.append(("dma", a, k))

                def barrier(self_):
                    rec.append(("barrier", (), {}))
            self.S = _Rec()
            self.ssm_setup()
            self.S = real
            self.setup_rec = rec
        else:
            self.ssm_setup()

    def ssm_setup(self):
        S = self.S
        d = self.din
        f32, b16, off = self.carve()
        off[0] = 4096 if (self.pipe_halo and self.n_halo > 0) else 0
        sc = f32(96).rearrange("p (a b) -> p a b", a=3)
        Bm = f32(2 * 32 * 16).rearrange("p (r a c) -> p r a c", r=2, a=32)
        tmp = [f32(32) for _ in range(12)]
        Bb = f32(2 * 32 * 16).rearrange("p (r a c) -> p r a c", r=2, a=32)
        big = [f32(32 * 16) for _ in range(2)]
        xpad = [f32(2 * 128).rearrange("p (r c) -> p r c", r=2) for _ in range(4)]
        btb = [b16(4 * 2 * 128).rearrange("p (a r c) -> p a r c", a=4, r=2) for _ in range(2)]
        S.dma("sp", sc, d["ssm_sc"], writes=["sc"])
        S.dma("sp", Bm, d["ssm_B"], writes=["Bm"])
        lr, li, dt, mag, ang, cs, sn, a_re, a_im, t0, t1, t2 = tmp
        V = lambda fn, r, w: S.op("dve", fn, reads=r, writes=w)
        AC = lambda fn, r, w: S.op("act", fn, reads=r, writes=w)
        A = lambda fn, r, w: S.op("act", fn, reads=r, writes=w)
        TT = lambda o, a, b, op: (lambda e: e.tensor_tensor(out=o, in0=a, in1=b, op=op))
        V(lambda e: e.tensor_scalar_min(out=lr, in0=sc[:, 0, :], scalar1=-1e-4), ["sc"], ["lr"])
        V(lambda e: e.tensor_copy(out=li, in_=sc[:, 1, :]), ["sc"], ["li"])
        A(lambda e: e.activation(out=dt, in_=sc[:, 2, :], func=AF.Exp), ["sc"], ["dt"])
        V(TT(mag, lr, dt, ALU.mult), ["lr", "dt"], ["mag"])
        A(lambda e: e.activation(out=mag, in_=mag, func=AF.Exp), ["mag"], ["mag"])
        V(TT(ang, li, dt, ALU.mult), ["li", "dt"], ["ang"])
        ki = f32(32).bitcast(mybir.dt.int32)
        kf = f32(32)
        for dst, shift, tg in ((sn, 0.0, "sn"), (cs, 0.5 * math.pi, "cs")):
            V(lambda e, dst=dst, shift=shift: e.tensor_scalar(out=dst, in0=ang, scalar1=shift, scalar2=None, op0=ALU.add), ["ang"], [tg])
            V(lambda e, dst=dst: e.tensor_scalar(out=kf, in0=dst, scalar1=1.0 / (2.0 * math.pi), scalar2=None, op0=ALU.mult), [tg], ["kf"])
            V(lambda e: e.tensor_copy(out=ki, in_=kf), ["kf"], ["ki"])
            V(lambda e: e.tensor_copy(out=kf, in_=ki), ["ki"], ["kf"])
            V(lambda e, dst=dst: e.scalar_tensor_tensor(out=dst, in0=kf, scalar=-2.0 * math.pi, in1=dst, op0=ALU.mult, op1=ALU.add), ["kf", tg], [tg])
            V(lambda e, dst=dst: e.tensor_scalar(out=kf, in0=dst, scalar1=math.pi, scalar2=-2.0 * math.pi, op0=ALU.is_gt, op1=ALU.mult), [tg], ["kf"])
            V(TT(dst, dst, kf, ALU.add), [tg, "kf"], [tg])
            V(lambda e, dst=dst: e.tensor_scalar(out=kf, in0=dst, scalar1=-math.pi, scalar2=2.0 * math.pi, op0=ALU.is_lt, op1=ALU.mult), [tg], ["kf"])
            V(TT(dst, dst, kf, ALU.add), [tg, "kf"], [tg])
            A(lambda e, dst=dst: e.activation(out=dst, in_=dst, func=AF.Sin), [tg], [tg])
        V(TT(a_re, mag, cs, ALU.mult), ["mag", "cs"], ["a_re"])
        V(TT(a_im, mag, sn, ALU.mult), ["mag", "sn"], ["a_im"])
        nr, den, fr, fi = cs, sn, mag, ang
        V(lambda e: e.tensor_scalar_add(out=nr, in0=a_re, scalar1=-1.0), ["a_re", "cs"], ["nr"])
        V(TT(t0, lr, lr, ALU.mult), ["lr"], ["t0"])
        V(TT(t1, li, li, ALU.mult), ["li"], ["t1"])
        V(TT(den, t0, t1, ALU.add), ["t0", "t1", "sn"], ["den"])
        V(lambda e: e.reciprocal(out=den, in_=den), ["den"], ["den"])
        V(TT(t0, nr, lr, ALU.mult), ["nr", "lr"], ["t0"])
        V(TT(t1, a_im, li, ALU.mult), ["a_im", "li"], ["t1"])
        V(TT(t0, t0, t1, ALU.add), ["t0", "t1"], ["t0"])
        V(TT(fr, t0, den, ALU.mult), ["t0", "den", "mag"], ["fr"])
        V(TT(t0, a_im, lr, ALU.mult), ["a_im", "lr"], ["t0"])
        V(TT(t1, nr, li, ALU.mult), ["nr", "li"], ["t1"])
        V(TT(t0, t0, t1, ALU.subtract), ["t0", "t1"], ["t0"])
        V(TT(fi, t0, den, ALU.mult), ["t0", "den", "ang"], ["fi"])
        frb = fr.unsqueeze(2).to_broadcast([128, 32, 16])
        fib = fi.unsqueeze(2).to_broadcast([128, 32, 16])
        b0 = big[0].rearrange("p (a c) -> p a c", a=32)
        b1 = big[1].rearrange("p (a c) -> p a c", a=32)
        V(TT(b0, Bm[:, 0, :, :], frb, ALU.mult), ["Bm", "fr"], ["b0"])
        V(TT(b1, Bm[:, 1, :, :], fib, ALU.mult), ["Bm", "fi"], ["b1"])
        V(TT(Bb[:, 0, :, :], b0, b1, ALU.subtract), ["b0", "b1"], ["Bb"])
        V(TT(b0, Bm[:, 1, :, :], frb, ALU.mult), ["Bm", "fr"], ["b0"])
        V(TT(b1, Bm[:, 0, :, :], fib, ALU.mult), ["Bm", "fi"], ["b1"])
        V(TT(Bb[:, 1, :, :], b0, b1, ALU.add), ["b0", "b1"], ["Bb"])
        for qq in range(4):
            S.op("pool", lambda e, qq=qq: e.memset(xpad[qq], 0.0), writes=[("xpad", qq)])
        for P in range(32):
            ch, qq = P // 4, P % 4
            xp = xpad[qq]
            for ri in range(2):
                for g2 in range(2):
                    V(lambda e, xp=xp, ri=ri, g2=g2, P=P, qq=qq: e.tensor_copy(
                        out=xp[g2 * 64:(g2 + 1) * 64, ri, 32 * qq + 16 * g2: 32 * qq + 16 * g2 + 16],
                        in_=Bb[g2 * 64:(g2 + 1) * 64, ri, P, :]), ["Bb"], [("xpad", qq)])
            bb = btb[ch % 2]
            for ri in range(2):
                pb = 6 + (P * 2 + ri) % 2
                S.op("pe", lambda e, xp=xp, ri=ri, pb=pb: e.matmul(self.ps[:, pb, 0:128], lhsT=xp[:, ri, :], rhs=self.identf[:, :],
                                                                     start=True, stop=True),
                     reads=[("xpad", qq), "identf"], writes=[("ps", pb)])
                A(lambda e, bb=bb, qq=qq, ri=ri, pb=pb: e.copy(out=bb[:, qq, ri, :], in_=self.ps[:, pb, 0:128]),
                  [("ps", pb)], [("btb", ch % 2)])
            if qq == 3:
                S.dma("sp", self.bt_d[ch], bb, reads=[("btb", ch % 2)], writes=["bt_d"])
        apw = self.apw
        V(lambda e: e.tensor_copy(out=apw[:, 0, 0, :], in_=a_re), ["a_re"], ["apw"])
        V(lambda e: e.tensor_copy(out=apw[:, 0, 1, :], in_=a_im), ["a_im"], ["apw"])
        for s in range(1, 10):
            pr, pi_, nr_, ni_ = apw[:, s - 1, 0, :], apw[:, s - 1, 1, :], apw[:, s, 0, :], apw[:, s, 1, :]
            V(TT(t0, pr, pr, ALU.mult), ["apw"], ["t0"])
            V(TT(t1, pi_, pi_, ALU.mult), ["apw"], ["t1"])
            V(TT(nr_, t0, t1, ALU.subtract), ["t0", "t1"], ["apw"])
            V(TT(t0, pr, pi_, ALU.mult), ["apw"], ["t0"])
            V(lambda e, ni_=ni_: e.tensor_scalar(out=ni_, in0=t0, scalar1=2.0, scalar2=None, op0=ALU.mult), ["t0"], ["apw"])
        V(TT(t0, a_re, a_re, ALU.mult), ["a_re"], ["t0"])
        V(TT(t1, a_im, a_im, ALU.mult), ["a_im"], ["t1"])
        V(TT(t0, t0, t1, ALU.add), ["t0", "t1"], ["t0"])
        V(lambda e: e.reciprocal(out=t0, in_=t0), ["t0"], ["t0"])
        V(TT(self.ainv[:, 0, :], a_re, t0, ALU.mult), ["a_re", "t0"], ["ainv"])
        V(lambda e: e.scalar_tensor_tensor(out=self.ainv[:, 1, :], in0=a_im, scalar=-1.0, in1=t0, op0=ALU.mult, op1=ALU.mult),
          ["a_im", "t0"], ["ainv"])
        tab = self.tab
        ivr, ivi = lr, li
        V(lambda e: e.tensor_copy(out=ivr, in_=self.ainv[:, 0, :]), ["ainv", "lr"], ["ivr"])
        V(lambda e: e.tensor_copy(out=ivi, in_=self.ainv[:, 1, :]), ["ainv", "li"], ["ivi"])
        for base, getp in ((0, lambda s: (apw[:, s, 0, :], apw[:, s, 1, :])), (2, None)):
            S.op("pool", lambda e, base=base: e.memset(tab[:, base, :, 0:1], 1.0), writes=["tab"])
            S.op("pool", lambda e, base=base: e.memset(tab[:, base + 1, :, 0:1], 0.0), writes=["tab"])
            s = 0
            n = 1
            while n < TC:
                if getp is not None:
                    mr, mi = getp(s)
                else:
                    mr, mi = ivr, ivi
                mrb = mr.unsqueeze(2).to_broadcast([128, 32, n])
                mib = mi.unsqueeze(2).to_broadcast([128, 32, n])
                sr, si = tab[:, base, :, 0:n], tab[:, base + 1, :, 0:n]
                orr, oi = tab[:, base, :, n:2 * n], tab[:, base + 1, :, n:2 * n]
                bb0 = big[0].rearrange("p (a c) -> p a c", a=32)[:, :, 0:n]
                bb1 = big[1].rearrange("p (a c) -> p a c", a=32)[:, :, 0:n]
                V(TT(bb0, sr, mrb, ALU.mult), ["tab", "apw", "ivr"], ["b0"])
                V(TT(bb1, si, mib, ALU.mult), ["tab", "apw", "ivi"], ["b1"])
                V(TT(orr, bb0, bb1, ALU.subtract), ["b0", "b1"], ["tab"])
                V(TT(bb0, sr, mib, ALU.mult), ["tab", "apw", "ivi"], ["b0"])
                V(TT(bb1, si, mrb, ALU.mult), ["tab", "apw", "ivr"], ["b1"])
                V(TT(oi, bb0, bb1, ALU.add), ["b0", "b1"], ["tab"])
                if getp is None:
                    V(TT(t0, ivr, ivr, ALU.mult), ["ivr"], ["t0"])
                    V(TT(t1, ivi, ivi, ALU.mult), ["ivi"], ["t1"])
                    V(TT(t2, ivr, ivi, ALU.mult), ["ivr", "ivi"], ["t2"])
                    V(TT(ivr, t0, t1, ALU.subtract), ["t0", "t1"], ["ivr"])
                    V(lambda e: e.tensor_scalar(out=ivi, in0=t2, scalar1=2.0, scalar2=None, op0=ALU.mult), ["t2"], ["ivi"])
                n *= 2
                s += 1
        if self.has_dbg("apw0"):
            S.dma("sp", self.dout["apw0"], self.apw[:, 0, :, :], reads=["apw"])
        if self.has_dbg("tab"):
            S.dma("sp", self.dout["tab"], self.tab[:, :, :, :], reads=["tab"])
        if self.has_dbg("btd"):
            S.dma("pool", self.dout["btd"], self.bt_d, reads=["bt_d"])
        S.barrier()

    def has_dbg(self, name):
        return any(nm == name for nm, _ in self.dbg)

    def rmsnorm(self, col0, Tn):
        S = self.S
        xT, hT = self.xT, self.hT
        for c in range(8):
            S.op("act", lambda e, c=c: e.activation(out=hT[:, c, :Tn], in_=xT[:, c, :Tn], func=AF.Square),
                 reads=[("xT", c)], writes=[("hT", c)])
        for c in range(8):
            S.op("pe", lambda e, c=c: e.matmul(self.ps[:, 0, :Tn], lhsT=self.ones[:, :], rhs=hT[:, c, :Tn], start=(c == 0), stop=(c == 7)),
                 reads=[("hT", c), "ones"], writes=[("ps", 0)])
        S.op("act", lambda e: e.activation(out=self.rstd[:, :Tn], in_=self.ps[:, 0, :Tn], func=AF.Sqrt, bias=self.eps[:, 0:1]),
             reads=[("ps", 0), "eps"], writes=["rstd"])
        S.op("dve", lambda e: e.reciprocal(out=self.rstd[:, :Tn], in_=self.rstd[:, :Tn]), reads=["rstd"], writes=["rstd"])
        for c in range(8):
            S.op("dve", lambda e, c=c: e.scalar_tensor_tensor(out=hT[:, c, :Tn], in0=xT[:, c, :Tn], scalar=self.vecs[:, col0 + c:col0 + c + 1],
                                                           in1=self.rstd[:, :Tn], op0=ALU.mult, op1=ALU.mult),
                 reads=[("xT", c), "rstd", "vecs"], writes=[("hT", c)])

    def ffn_g(self, nm, Tn, hid, sg, dbase=4):
        S = self.S
        d = self.din
        xT, hT, ps = self.xT, self.hT, self.ps
        wg_d, wu_d, wd_d = d["wg" + nm], d["wu" + nm], d["wd" + nm]
        for pc in range(NPF):
            bg, bu = (0, 1) if pc % 2 == 0 else (2, 0)
            bg = (2 * pc) % 4
            bu = (2 * pc + 1) % 4
            S.dma("pool", self.wA[bg][:, :, :], wg_d[pc], writes=[("wA", bg)])
            S.dma("pool", self.wA[bu][:, :, :], wu_d[pc], writes=[("wA", bu)])
            for j in range(2):
                hc = 2 * pc + j
                pb = (hc % 2) * 2
                for k in range(8):
                    S.op("pe", lambda e, k=k, bg=bg, j=j, pb=pb: e.matmul(ps[:, pb, :Tn], lhsT=self.wA[bg][:, k, j * 128:(j + 1) * 128],
                                                                         rhs=hT[:, k, :Tn], start=(k == 0), stop=(k == 7)),
                         reads=[("wA", bg), ("hT", k)], writes=[("ps", pb)])
                for k in range(8):
                    S.op("pe", lambda e, k=k, bu=bu, j=j, pb=pb: e.matmul(ps[:, pb + 1, :Tn], lhsT=self.wA[bu][:, k, j * 128:(j + 1) * 128],
                                                                         rhs=hT[:, k, :Tn], start=(k == 0), stop=(k == 7)),
                         reads=[("wA", bu), ("hT", k)], writes=[("ps", pb + 1)])
                sbi = hc % 2
                S.op("act", lambda e, pb=pb, sbi=sbi: e.activation(out=sg[sbi][:, :Tn], in_=ps[:, pb, :Tn], func=AF.Silu),
                     reads=[("ps", pb)], writes=[("sg", sbi)])
                S.op("dve", lambda e, pb=pb, sbi=sbi, hc=hc: e.tensor_tensor(out=hid[:, hc, :Tn], in0=sg[sbi][:, :Tn], in1=ps[:, pb + 1, :Tn], op=ALU.mult),
                     reads=[("ps", pb + 1), ("sg", sbi)], writes=[("hid", hc)])
            yield
        for half in range(2):
            for pc in range(NPF):
                b = (half * NPF + pc) % 7
                wdb = self.wD[b] if b < 3 else self.wKV[:, 2 * (b - 3):2 * (b - 3) + 2, :]
                S.dma("pool", wdb[:, :, :], wd_d[half, pc], writes=[("wD", b)])
                for j in range(2):
                    hc = 2 * pc + j
                    for o in range(4):
                        S.op("pe", lambda e, wdb=wdb, j=j, o=o, hc=hc: e.matmul(ps[:, dbase + o, :Tn], lhsT=wdb[:, j, o * 128:(o + 1) * 128],
                                                                           rhs=hid[:, hc, :Tn], start=(hc == 0), stop=(hc == 21)),
                             reads=[("wD", b), ("hid", hc)], writes=[("ps", dbase + o)])
                yield
            for o in range(4):
                c = half * 4 + o
                S.op("dve", lambda e, o=o, c=c: e.scalar_tensor_tensor(out=xT[:, c, :Tn], in0=ps[:, dbase + o, :Tn], scalar=0.5, in1=xT[:, c, :Tn],
                                                                     op0=ALU.mult, op1=ALU.add),
                     reads=[("ps", dbase + o), ("xT", c)], writes=[("xT", c)])

    def ffn(self, *a, **k):
        for _ in self.ffn_g(*a, **k):
            pass

    def ffn_block_g(self, nm, col0, Tn, base=0, dbase=4):
        f32, b16, off = self.carve()
        off[0] = base
        hid = b16(22 * T).rearrange("p (c t) -> p c t", c=22)
        sg = [f32(T), f32(T)]
        self.rmsnorm(col0, Tn)
        yield
        yield from self.ffn_g(nm, Tn, hid, sg, dbase=dbase)
        self.S.barrier()

    def ffn_block(self, *a, **k):
        for _ in self.ffn_block_g(*a, **k):
            pass

    def linear_g(self, w_d, npieces, in_buf, in_tag, nk, Tn, consumer, banks=(0, 1, 2, 3), wtag="wA", pieces=None):
        S = self.S
        for pc in range(npieces):
            if pieces is not None and pc not in pieces:
                continue
            b = self._wrot
            self._wrot = (self._wrot + 1) % 4
            S.dma("pool", self.wA[b][:, 0:nk, :], w_d[pc], writes=[("wA", b)])
            for j in range(2):
                oc = 2 * pc + j
                bank = banks[oc % len(banks)]
                for k in range(nk):
                    S.op("pe", lambda e, k=k, b=b, j=j, bank=bank: e.matmul(self.ps[:, bank, :Tn], lhsT=self.wA[b][:, k, j * 128:(j + 1) * 128],
                                                                           rhs=in_buf[:, k, :Tn], start=(k == 0), stop=(k == nk - 1)),
                         reads=[("wA", b), (in_tag, k)], writes=[("ps", bank)])
                consumer(oc, bank)
            yield

    def linear(self, *a, **k):
        for _ in self.linear_g(*a, **k):
            pass

    _wrot = 0

    def qknorm(self, bank, Tn, gcol, out_ap, out_tag, nb=7, bufs=None):
        S = self.S
        ps = self.ps
        if bufs is None:
            sqb, rstd, tg = self.sqb, self.rstd, ""
        else:
            i = self._qkrot % len(bufs)
            self._qkrot += 1
            sqb, rstd, nb = bufs[i]
            tg = "_%d" % i
        S.op("act", lambda e: e.activation(out=sqb[:, :Tn], in_=ps[:, bank, :Tn], func=AF.Square),
             reads=[("ps", bank)], writes=["sqb" + tg])
        S.op("pe", lambda e: e.matmul(ps[:, nb, :Tn], lhsT=self.onesblk[:, :], rhs=sqb[:, :Tn], start=True, stop=True),
             reads=["sqb" + tg, "onesblk"], writes=[("ps", nb)])
        S.op("act", lambda e: e.activation(out=rstd[:, :Tn], in_=ps[:, nb, :Tn], func=AF.Sqrt, bias=self.eps[:, 0:1]),
             reads=[("ps", nb), "eps"], writes=["rstd" + tg])
        S.op("dve", lambda e: e.reciprocal(out=rstd[:, :Tn], in_=rstd[:, :Tn]), reads=["rstd" + tg], writes=["rstd" + tg])
        S.op("dve", lambda e: e.scalar_tensor_tensor(out=out_ap, in0=ps[:, bank, :Tn], scalar=self.vecs[:, gcol:gcol + 1], in1=rstd[:, :Tn],
                                                     op0=ALU.mult, op1=ALU.mult),
             reads=[("ps", bank), "rstd" + tg, "vecs"], writes=[out_tag])

    _qkrot = 0

    def tile(self, ti):
        S = self.S
        d = self.din
        own = ti >= self.n_halo
        oi = ti - self.n_halo
        last = ti == self.NT - 1
        xT = self.xT
        if ti not in self.pre_ffn1:
            S.dma("sp", xT[:, :, :], d["xT"][:, ti * T:(ti + 1) * T].rearrange("(c p) t -> p c t", p=128),
                  writes=[("xT", c) for c in range(8)])
            if self.level < 1:
                return
            self.ffn_block("1", 0, T)
        if self.level < 2:
            return
        self.dbg_dump("x1T", ti, lambda dd: S.dma("sp", dd.rearrange("(c p) t -> p c t", p=128), xT[:, :, :], reads=[("xT", c) for c in range(8)]))
        f32, b16, off = self.carve()
        self.rmsnorm(8, T)
        u_bf = b16(8 * T).rearrange("p (c t) -> p c t", c=8)
        qT = b16(6 * T).rearrange("p (c t) -> p c t", c=6)
        def cons_u(oc, bank):
            S.op("act", lambda e: e.copy(out=u_bf[:, oc, :], in_=self.ps[:, bank, :]), reads=[("ps", bank)], writes=[("u", oc)])
        self.linear(d["w_u"], 4, self.hT, "hT", 8, T, cons_u)
        need_g = [g for g in range(3) if own or (self.n_halo - ti - 1) * T + 1 <= WIN[g]]
        self.need_g = need_g
        top = self.scr.shape[1] - 3 * (T + T // 2)
        qkb = []
        for i in range(3):
            o_ = top + i * (T + T // 2)
            qkb.append((self.scr[:, o_ + T:o_ + T + T // 2].bitcast(BF16), self.scr[:, o_:o_ + T], 5 + i))

        def cons_k(oc, bank):
            g, hh = oc // 2, oc % 2
            if g not in need_g:
                return
            if g < 2:
                dst = self.kT[g][:, hh, (ti % 2) * T:(ti % 2 + 1) * T]
            else:
                dst = self.kT[2][:, hh, ti * T:(ti + 1) * T]
            self.qknorm(bank, T, 33, dst, ("kT", g), bufs=qkb)
        self.linear(d["w_k"], 3, self.hT, "hT", 8, T, cons_k, pieces=need_g)
        if own:
            def cons_q(oc, bank):
                self.qknorm(bank, T, 32, qT[:, oc, :], ("qT", oc), bufs=qkb)
            self.linear(d["w_q"], 3, self.hT, "hT", 8, T, cons_q)
        ysm = b16(8 * T).rearrange("p (c t) -> p c t", c=8)
        self.ysm = ysm
        obT = b16(2 * T).rearrange("p (c t) -> p c t", c=2)
        mark = off[0]
        if self.level < 3:
            S.barrier(); return
        self.kv_tokmajor(ti, f32, b16)
        S.barrier()
        off[0] = mark
        if self.level < 4:
            return
        self.ssm_tile(ti, u_bf, f32, b16, own)
        if self.has_dbg("ysm_%d" % ti):
            S.dma("pool", self.dout["ysm_%d" % ti].rearrange("(c p) t -> p c t", p=128), ysm, reads=[("ysm", c) for c in range(8)])
        if self.has_dbg("u_%d" % ti):
            S.dma("pool", self.dout["u_%d" % ti].rearrange("(c p) t -> p c t", p=128), u_bf, reads=[("u", c) for c in range(8)])
        if self.level < 5:
            S.barrier(); return
        if own:
            S.barrier()
            off[0] = mark
            self.attention(ti, qT, obT, f32, b16)
            if self.has_dbg("obT_%d" % ti):
                S.dma("pool", self.dout["obT_%d" % ti].rearrange("(c p) t -> p c t", p=128), obT, reads=[("obT", c) for c in range(2)])
            if self.level < 6:
                S.barrier(); return
            self.mix_out(ti, obT, f32, b16)
        S.barrier()
        if own:
            self.ffn_block("2", 16, T)
            S.dma("sp", self.dout["yT"][:, oi * T:(oi + 1) * T].rearrange("(c p) t -> p c t", p=128), xT[:, :, :],
                  reads=[("xT", c) for c in range(8)])
        if last:
            f32, b16, off = self.carve()
            hr, hi_, t0, t1 = f32(32), f32(32), f32(32), f32(32)
            self.cmul_small(hr, hi_, self.Kc[:, 0, :], self.Kc[:, 1, :], self.ainv[:, 0, :], self.ainv[:, 1, :], t0, t1, "Kc", "ainv", "pstv")
            S.dma("sp", self.dout["pst"][:, 0, :], hr, reads=["pstv"])
            S.dma("sp", self.dout["pst"][:, 1, :], hi_, reads=["pstv"])
            S.barrier()

    HALO_ABASE = 4096 + 6448 + 816

    def haloA_g(self, ti, u_bf, par):
        S = self.S
        d = self.din
        abase = self.HALO_ABASE
        S.barrier()
        S.dma("sp", self.xT[:, :, :], d["xT"][:, ti * T:(ti + 1) * T].rearrange("(c p) t -> p c t", p=128),
              writes=[("xT", c) for c in range(8)])
        yield from self.ffn_block_g("1", 0, T, base=abase, dbase=0)
        self.rmsnorm(8, T)
        yield

        def cons_u(oc, bank):
            S.op("act", lambda e: e.copy(out=u_bf[:, oc, :], in_=self.ps[:, bank, :]), reads=[("ps", bank)], writes=[("u", par, oc)])
        yield from self.linear_g(d["w_u"], 4, self.hT, "hT", 8, T, cons_u, banks=(0, 1, 2))
        need_g = [g for g in range(3) if (self.n_halo - ti - 1) * T + 1 <= WIN[g]]
        self.need_g = need_g

        def cons_k(oc, bank):
            g, hh = oc // 2, oc % 2
            if g < 2:
                dst = self.kT[g][:, hh, (ti % 2) * T:(ti % 2 + 1) * T]
            else:
                dst = self.kT[2][:, hh, ti * T:(ti + 1) * T]
            self.qknorm(bank, T, 33, dst, ("kT", g), nb=3)
        yield from self.linear_g(d["w_k"], 3, self.hT, "hT", 8, T, cons_k, banks=(0, 1, 2), pieces=need_g)
        f32, b16, off = self.carve()
        off[0] = abase
        yield from self.kv_tokmajor_g(ti, f32, b16, b0_fixed=0)

    def ownpre_g(self, ti):
        S = self.S
        S.barrier()
        S.dma("sp", self.xT[:, :, :], self.din["xT"][:, ti * T:(ti + 1) * T].rearrange("(c p) t -> p c t", p=128),
              writes=[("xT", c) for c in range(8)])
        yield from self.ffn_block_g("1", 0, T, base=self.HALO_ABASE, dbase=0)

    def ssm_halo_g(self, ti, u_bf, par):
        S = self.S
        ps = self.ps
        tab = self.tab
        TT = lambda o, a, b, op: (lambda e: e.tensor_tensor(out=o, in0=a, in1=b, op=op))
        V = lambda fn, r, w: S.op("dve", fn, reads=r, writes=w)
        AC = lambda fn, r, w: S.op("act", fn, reads=r, writes=w)
        f32, b16, off = self.carve()
        off[0] = 4096
        T2 = 2 * T
        btb = b16(4 * 2 * 128).rearrange("p (a r c) -> p a r c", a=4, r=2)
        X = [f32(T2), f32(T2)]
        ta, tb = f32(T2), f32(T2)
        G1 = [b16(T2), b16(T2)]
        G = [G1] * 8
        NZ = NCH + 1
        Z = [[f32(16 * NZ).rearrange("p (a c) -> p a c", a=16) for _ in range(2)] for _ in range(2)]
        zt = [f32(16 * NZ).rearrange("p (a c) -> p a c", a=16) for _ in range(2)]
        assert off[0] <= self.HALO_ABASE, off[0]
        v4 = lambda a: a.rearrange("p (j c l) -> p j c l", j=2, l=TC)
        for cp in range(2):
            for sub in range(4):
                ch = 4 * cp + sub
                S.dma("sp", btb, self.bt_d[ch], reads=["bt_d"], writes=["h_btb"])
                for bt in range(2):
                    b4 = 2 * sub + bt
                    P0 = 4 * ch + 2 * bt
                    for j in range(2):
                        qq = 2 * bt + j
                        for ri in range(2):
                            bank = 4 + 2 * j + ri
                            S.op("pe", lambda e, qq=qq, ri=ri, bank=bank, ch=ch: e.matmul(ps[:, bank, :], lhsT=btb[:, qq, ri, :], rhs=u_bf[:, ch, :], start=True, stop=True),
                                 reads=["h_btb", ("u", par, ch)], writes=[("ps", bank)])
                            S.op("act", lambda e, ri=ri, j=j, bank=bank: e.copy(out=X[ri][:, j * T:(j + 1) * T], in_=ps[:, bank, :]),
                                 reads=[("ps", bank)], writes=[("hX", ri)])
                    ivr = tab[:, 2, P0:P0 + 2, :].unsqueeze(2).to_broadcast([128, 2, NCH, TC])
                    ivi = tab[:, 3, P0:P0 + 2, :].unsqueeze(2).to_broadcast([128, 2, NCH, TC])
                    gr, gi = G[b4]
                    V(TT(v4(ta), v4(X[0]), ivr, ALU.mult), [("hX", 0), "tab"], ["h_ta"])
                    V(TT(v4(tb), v4(X[1]), ivi, ALU.mult), [("hX", 1), "tab"], ["h_tb"])
                    V(TT(ta, ta, tb, ALU.subtract), ["h_ta", "h_tb"], ["h_ta"])
                    V(lambda e, gr=gr: e.tensor_tensor_scan(out=gr, data0=self.scanmask[:, :], data1=ta, initial=0.0, op0=ALU.mult, op1=ALU.add),
                      ["h_ta", "scanmask"], [("hG", 0)])
                    V(TT(v4(ta), v4(X[1]), ivr, ALU.mult), [("hX", 1), "tab"], ["h_ta"])
                    V(TT(v4(tb), v4(X[0]), ivi, ALU.mult), [("hX", 0), "tab"], ["h_tb"])
                    V(TT(ta, ta, tb, ALU.add), ["h_ta", "h_tb"], ["h_ta"])
                    V(lambda e, gi=gi: e.tensor_tensor_scan(out=gi, data0=self.scanmask[:, :], data1=ta, initial=0.0, op0=ALU.mult, op1=ALU.add),
                      ["h_ta", "scanmask"], [("hG", 1)])
                    for ri in range(2):
                        ge = G[b4][ri].rearrange("p (j c l) -> p j c l", j=2, l=TC)[:, :, :, TC - 1]
                        AC(lambda e, b4=b4, ri=ri, ge=ge: e.copy(out=zt[ri][:, 2 * b4:2 * b4 + 2, 1:NZ], in_=ge), [("hG", ri)], ["h_zt"])
                    yield
            Pa = slice(16 * cp, 16 * cp + 16)
            s0 = int(math.log2(TC))
            zr, zi = Z[0]
            V(lambda e, Pa=Pa: e.tensor_copy(out=zr[:, :, 0:1], in_=self.Kc[:, 0, Pa].unsqueeze(2)), ["Kc"], ["hZ0"])
            V(lambda e, Pa=Pa: e.tensor_copy(out=zi[:, :, 0:1], in_=self.Kc[:, 1, Pa].unsqueeze(2)), ["Kc"], ["hZ0"])
            self.cmul_b(zr[:, :, 1:NZ], zi[:, :, 1:NZ], zt[0][:, :, 1:NZ], zt[1][:, :, 1:NZ], s0, Pa, NZ - 1, zt, "h_zt", "hZ0", add=None)
            yield
            cur = 0
            st = 1
            sidx = s0
            while st < NZ:
                a, b = Z[cur], Z[1 - cur]
                n = NZ - st
                V(lambda e, a=a, b=b, st=st: e.tensor_copy(out=b[0][:, :, 0:st], in_=a[0][:, :, 0:st]), ["hZ%d" % cur], ["hZ%d" % (1 - cur)])
                V(lambda e, a=a, b=b, st=st: e.tensor_copy(out=b[1][:, :, 0:st], in_=a[1][:, :, 0:st]), ["hZ%d" % cur], ["hZ%d" % (1 - cur)])
                self.cmul_b(b[0][:, :, st:NZ], b[1][:, :, st:NZ], a[0][:, :, 0:n], a[1][:, :, 0:n], sidx, Pa, n, zt, "hZ%d" % cur, "hZ%d" % (1 - cur),
                            add=(a[0][:, :, st:NZ], a[1][:, :, st:NZ]), ztg="h_zt")
                cur = 1 - cur
                st *= 2
                sidx += 1
            zf = Z[cur]
            ztag = "hZ%d" % cur
            V(lambda e, zf=zf, Pa=Pa: e.tensor_copy(out=self.Kc[:, 0, Pa].unsqueeze(2), in_=zf[0][:, :, NZ - 1:NZ]), [ztag], ["Kc"])
            V(lambda e, zf=zf, Pa=Pa: e.tensor_copy(out=self.Kc[:, 1, Pa].unsqueeze(2), in_=zf[1][:, :, NZ - 1:NZ]), [ztag], ["Kc"])
            yield

    def halo_phase(self):
        S = self.S
        f32, b16, off = self.carve()
        u_bufs = [b16(8 * T).rearrange("p (c t) -> p c t", c=8) for _ in range(2)]
        assert off[0] == 4096
        nh = self.n_halo
        rec = getattr(self, "setup_rec", [])
        ri_ = 0
        per = max(1, (len(rec) + 59) // 60)

        def replay(n):
            nonlocal ri_
            for _ in range(n):
                if ri_ >= len(rec):
                    return
                kind, a, k = rec[ri_]
                ri_ += 1
                if kind == "op":
                    S.op(*a, **k)
                elif kind == "dma":
                    S.dma(*a, **k)
                else:
                    S.barrier()
        for _ in self.haloA_g(0, u_bufs[0], 0):
            replay(per)
        replay(len(rec))
        for ti in range(nh):
            gs = self.ssm_halo_g(ti, u_bufs[ti % 2], ti % 2)
            if ti + 1 < nh:
                ga = self.haloA_g(ti + 1, u_bufs[(ti + 1) % 2], (ti + 1) % 2)
            elif self.n_own > 0:
                ga = self.ownpre_g(nh)
                self.pre_ffn1.add(nh)
            else:
                ga = None
            s_done = False
            a_done = ga is None
            while not (s_done and a_done):
                if not s_done:
                    try:
                        next(gs)
                    except StopIteration:
                        s_done = True
                for _ in range(3):
                    if not a_done:
                        try:
                            next(ga)
                        except StopIteration:
                            a_done = True
        S.barrier()

    def cmul_small(self, orr, oi, ar, ai, br, bi, t0, t1, ta, tb, tout):
        S = self.S
        TT = lambda o, a, b, op: (lambda e: e.tensor_tensor(out=o, in0=a, in1=b, op=op))
        S.op("dve", TT(t0, ar, br, ALU.mult), [ta, tb], ["cm_t0"])
        S.op("dve", TT(t1, ai, bi, ALU.mult), [ta, tb], ["cm_t1"])
        S.op("dve", TT(orr, t0, t1, ALU.subtract), ["cm_t0", "cm_t1"], [tout])
        S.op("dve", TT(t0, ar, bi, ALU.mult), [ta, tb, tout], ["cm_t0"])
        S.op("dve", TT(t1, ai, br, ALU.mult), [ta, tb, tout], ["cm_t1"])
        S.op("dve", TT(oi, t0, t1, ALU.add), ["cm_t0", "cm_t1"], [tout])

    def dbg_dump(self, name, ti, fn):
        for nm, shape in self.dbg:
            if nm == "%s_%d" % (name, ti):
                fn(self.dout[nm])

    def kv_tokmajor_g(self, ti, f32, b16, b0_fixed=None):
        S0 = self.S
        import os
        lim = int(os.environ.get("KVN", "100000"))
        cnt = [0]

        class _W:
            def op(self_, eng, fn, reads=(), writes=()):
                if eng != "pe":
                    cnt[0] += 1
                    if cnt[0] > lim:
                        return
                    if cnt[0] == lim:
                        print("LAST OP", eng, reads, writes)
                S0.op(eng, fn, reads, writes)

            def dma(self_, *a, **k):
                S0.dma(*a, **k)
        S = _W()
        d = self.din
        ps, hT = self.ps, self.hT
        own = ti >= self.n_halo
        oi = ti - self.n_halo
        kvf = [f32(4 * 512).rearrange("p (u c) -> p u c", u=4) for _ in range(2 if b0_fixed is None else 1)]
        sq = f32(4 * 256).rearrange("p (u c) -> p u c", u=4)
        ssq = f32(16)
        vst = b16(16 * 4 * 66).rearrange("p (r h c) -> p r h c", r=16, h=4)
        S.op("pool", lambda e: e.memset(vst[0:32, :, :, 64:66], 1.0), writes=["vst"])
        bi = [0]
        import os
        for g in self.need_g:
            dl = DIL[g]
            S.dma("pool", self.wKV[:, :, :], d["w_kv"][g], writes=["wKV"] + [("wD", 3), ("wD", 4), ("wD", 5), ("wD", 6)])
            if g < 2:
                M, batches = 128, [[0, 1], [2, 3]]
            else:
                M, batches = 32, [[0, 1, 2, 3], [4, 5, 6, 7], [8, 9, 10, 11], [12, 13, 14, 15]]
            for units in batches:
                nu = len(units)
                b0 = (4 if (bi[0] % 2 == 0) else 0) if b0_fixed is None else b0_fixed
                kb = kvf[bi[0] % len(kvf)]
                bi[0] += 1
                for ui, u in enumerate(units):
                    if g == 0:
                        cols = slice(u * 128, (u + 1) * 128)
                    else:
                        cols = slice(u, T, dl)
                    for k in range(8):
                        S.op("pe", lambda e, k=k, ui=ui, cols=cols, b0=b0, M=M: e.matmul(ps[0:M, b0 + ui, :], lhsT=hT[:, k, cols], rhs=self.wKV[:, k, :],
                                                                                        start=(k == 0), stop=(k == 7)),
                             reads=["wKV", ("hT", k)] + [("wD", 3), ("wD", 4), ("wD", 5), ("wD", 6)], writes=[("ps", b0 + ui)])
                rd = [("ps", b0 + ui) for ui in range(nu)]
                pk = ps[0:M, b0:b0 + nu, 0:256]
                pv = ps[0:M, b0:b0 + nu, 256:512]
                S.op("act", lambda e, pk=pk, M=M, nu=nu: e.activation(out=sq[0:M, 0:nu, :], in_=pk, func=AF.Square), reads=rd, writes=["kv_sq"])
                S.op("dve", lambda e, M=M, nu=nu: e.tensor_reduce(out=ssq[0:M, 0:nu * 4], in_=sq[0:M, 0:nu, :].rearrange("p u (h c) -> p (u h) c", h=4),
                                                                 axis=AX.X, op=ALU.add), reads=["kv_sq"], writes=["kv_ssq"])
                S.op("act", lambda e, M=M, nu=nu: e.activation(out=ssq[0:M, 0:nu * 4], in_=ssq[0:M, 0:nu * 4], func=AF.Sqrt, bias=self.eps[0:M, 0:1],
                                                               scale=1.0 / 64), reads=["kv_ssq", "eps"], writes=["kv_ssq"])
                S.op("dve", lambda e, M=M, nu=nu: e.reciprocal(out=ssq[0:M, 0:nu * 4], in_=ssq[0:M, 0:nu * 4]), reads=["kv_ssq"], writes=["kv_ssq"])
                kbk = kb[0:M, 0:nu, 0:256].rearrange("p u (h c) -> p u h c", h=4)
                S.op("dve", lambda e, pk=pk, M=M, nu=nu, kbk=kbk: e.tensor_tensor(
                    out=kbk, in0=pk.rearrange("p u (h c) -> p u h c", h=4),
                    in1=ssq[0:M, 0:nu * 4].rearrange("p (u h) -> p u h", h=4).unsqueeze(3).to_broadcast([M, nu, 4, 64]), op=ALU.mult),
                    reads=rd + ["kv_ssq"], writes=[("kvf", id(kb))])
                S.op("dve", lambda e, M=M, nu=nu, kb=kb: e.tensor_tensor(
                    out=kb[0:M, 0:nu, 0:256], in0=kb[0:M, 0:nu, 0:256], in1=self.gain_bc[0:M, 1:2, :].to_broadcast([M, nu, 256]), op=ALU.mult),
                    reads=[("kvf", id(kb)), "gain_bc"], writes=[("kvf", id(kb))])
                S.op("act", lambda e, pv=pv, M=M, nu=nu, kb=kb: e.copy(out=kb[0:M, 0:nu, 256:512], in_=pv), reads=rd, writes=[("kvf", id(kb))])
                for ui, u in enumerate(units):
                    src = ps[0:M, b0 + ui, 256:512].rearrange("p (h c) -> p h c", h=4)
                    if g == 0:
                        dst, tg = self.V[0][:, (4 * ti + u) % 8, :, 0:64], ("V", 0)
                    elif g == 1:
                        dst, tg = self.V[1][:, (ti % 2) * 4 + u, :, 0:64], ("V", 1)
                    else:
                        dst, tg = vst[0:32, u, :, 0:64], "vst"
                    S.op("act", lambda e, dst=dst, src=src: e.copy(out=dst, in_=src), reads=[("ps", b0 + ui)], writes=[tg])
                import os
                if own and not (int(os.environ.get("KVL", "0")) & 1):
                    for ui, u in enumerate(units):
                        if g == 0:
                            rows = self.dout["pkv"][0, oi * T + u * 128: oi * T + (u + 1) * 128, :]
                        else:
                            rows = self.dout["pkv"][g, oi * T + u: (oi + 1) * T: dl, :]
                        S.dma("sp", rows, kb[0:M, ui, :], reads=[("kvf", id(kb))])
                yield
            if g == 2 and not (int(os.environ.get("KVL", "0")) & 2):
                mt, q4 = ti // 4, ti % 4
                s0 = (mt % 2) * 16
                S.dma("sp", self.V[2][32 * q4:32 * q4 + 32, s0:s0 + 16, :, :], vst[0:32, :, :, :], reads=["vst"], writes=[("V", 2)])

    def kv_tokmajor(self, *a, **k):
        for _ in self.kv_tokmajor_g(*a, **k):
            pass

    def ssm_tile(self, ti, u_bf, f32, b16, own):
        S = self.S
        ps = self.ps
        tab = self.tab
        TT = lambda o, a, b, op: (lambda e: e.tensor_tensor(out=o, in0=a, in1=b, op=op))
        V = lambda fn, r, w: S.op("dve", fn, reads=r, writes=w)
        AC = lambda fn, r, w: S.op("act", fn, reads=r, writes=w)
        ysm = self.ysm
        T2 = 2 * T
        btb = b16(4 * 2 * 128).rearrange("p (a r c) -> p a r c", a=4, r=2)
        ctb = b16(2 * 4 * 128).rearrange("p (r a c) -> p r a c", r=2, a=4)
        X = [f32(T2), f32(T2)]
        ta, tb = f32(T2), f32(T2)
        G = [[b16(T2), b16(T2)] for _ in range(4)]
        hb = [[b16(T2), b16(T2)] for _ in range(2)]
        NZ = NCH + 1
        Z = [[f32(8 * NZ).rearrange("p (a c) -> p a c", a=8) for _ in range(2)] for _ in range(2)]
        zt = [f32(8 * NZ).rearrange("p (a c) -> p a c", a=8) for _ in range(2)]
        v4 = lambda a: a.rearrange("p (j c l) -> p j c l", j=2, l=TC)
        for cp in range(4):
            for sub in range(2):
                ch = 2 * cp + sub
                S.dma("sp", btb, self.bt_d[ch], reads=["bt_d"], writes=["btb"])
                for bt in range(2):
                    b4 = 2 * sub + bt
                    P0 = 4 * ch + 2 * bt
                    for j in range(2):
                        qq = 2 * bt + j
                        for ri in range(2):
                            bank = 4 + 2 * j + ri
                            S.op("pe", lambda e, qq=qq, ri=ri, bank=bank, ch=ch: e.matmul(ps[:, bank, :], lhsT=btb[:, qq, ri, :], rhs=u_bf[:, ch, :], start=True, stop=True),
                                 reads=["btb", ("u", ch)], writes=[("ps", bank)])
                            S.op("act", lambda e, ri=ri, j=j, bank=bank: e.copy(out=X[ri][:, j * T:(j + 1) * T], in_=ps[:, bank, :]),
                                 reads=[("ps", bank)], writes=[("X", ri)])
                    ivr = tab[:, 2, P0:P0 + 2, :].unsqueeze(2).to_broadcast([128, 2, NCH, TC])
                    ivi = tab[:, 3, P0:P0 + 2, :].unsqueeze(2).to_broadcast([128, 2, NCH, TC])
                    gr, gi = G[b4]
                    V(TT(v4(ta), v4(X[0]), ivr, ALU.mult), [("X", 0), "tab"], ["ta"])
                    V(TT(v4(tb), v4(X[1]), ivi, ALU.mult), [("X", 1), "tab"], ["tb"])
                    V(TT(ta, ta, tb, ALU.subtract), ["ta", "tb"], ["ta"])
                    V(lambda e, gr=gr: e.tensor_tensor_scan(out=gr, data0=self.scanmask[:, :], data1=ta, initial=0.0, op0=ALU.mult, op1=ALU.add),
                      ["ta", "scanmask"], [("G", b4, 0)])
                    V(TT(v4(ta), v4(X[1]), ivr, ALU.mult), [("X", 1), "tab"], ["ta"])
                    V(TT(v4(tb), v4(X[0]), ivi, ALU.mult), [("X", 0), "tab"], ["tb"])
                    V(TT(ta, ta, tb, ALU.add), ["ta", "tb"], ["ta"])
                    V(lambda e, gi=gi: e.tensor_tensor_scan(out=gi, data0=self.scanmask[:, :], data1=ta, initial=0.0, op0=ALU.mult, op1=ALU.add),
                      ["ta", "scanmask"], [("G", b4, 1)])
            Pa = slice(8 * cp, 8 * cp + 8)
            s0 = int(math.log2(TC))
            zr, zi = Z[0]
            V(lambda e, Pa=Pa: e.tensor_copy(out=zr[:, :, 0:1], in_=self.Kc[:, 0, Pa].unsqueeze(2)), ["Kc"], ["Z0"])
            V(lambda e, Pa=Pa: e.tensor_copy(out=zi[:, :, 0:1], in_=self.Kc[:, 1, Pa].unsqueeze(2)), ["Kc"], ["Z0"])
            for b4 in range(4):
                for ri in range(2):
                    ge = G[b4][ri].rearrange("p (j c l) -> p j c l", j=2, l=TC)[:, :, :, TC - 1]
                    AC(lambda e, b4=b4, ri=ri, ge=ge: e.copy(out=zt[ri][:, 2 * b4:2 * b4 + 2, 1:NZ], in_=ge), [("G", b4, ri)], ["zt"])
            self.cmul_b(zr[:, :, 1:NZ], zi[:, :, 1:NZ], zt[0][:, :, 1:NZ], zt[1][:, :, 1:NZ], s0, Pa, NZ - 1, zt, "zt", "Z0", add=None)
            cur = 0
            st = 1
            sidx = s0
            while st < NZ:
                a, b = Z[cur], Z[1 - cur]
                n = NZ - st
                V(lambda e, a=a, b=b, st=st: e.tensor_copy(out=b[0][:, :, 0:st], in_=a[0][:, :, 0:st]), ["Z%d" % cur], ["Z%d" % (1 - cur)])
                V(lambda e, a=a, b=b, st=st: e.tensor_copy(out=b[1][:, :, 0:st], in_=a[1][:, :, 0:st]), ["Z%d" % cur], ["Z%d" % (1 - cur)])
                self.cmul_b(b[0][:, :, st:NZ], b[1][:, :, st:NZ], a[0][:, :, 0:n], a[1][:, :, 0:n], sidx, Pa, n, zt, "Z%d" % cur, "Z%d" % (1 - cur),
                            add=(a[0][:, :, st:NZ], a[1][:, :, st:NZ]))
                cur = 1 - cur
                st *= 2
                sidx += 1
            zf = Z[cur]
            ztag = "Z%d" % cur
            V(lambda e, zf=zf, Pa=Pa: e.tensor_copy(out=self.Kc[:, 0, Pa].unsqueeze(2), in_=zf[0][:, :, NZ - 1:NZ]), [ztag], ["Kc"])
            V(lambda e, zf=zf, Pa=Pa: e.tensor_copy(out=self.Kc[:, 1, Pa].unsqueeze(2), in_=zf[1][:, :, NZ - 1:NZ]), [ztag], ["Kc"])
            if not own:
                continue
            for sub in range(2):
                ch = 2 * cp + sub
                S.dma("pool", ctb, self.din["ssm_CT"][:, :, 4 * ch:4 * ch + 4, :], writes=["ctb"])
                for bt in range(2):
                    b4 = 2 * sub + bt
                    P0 = 4 * ch + 2 * bt
                    fr = tab[:, 0, P0:P0 + 2, :].unsqueeze(2).to_broadcast([128, 2, NCH, TC])
                    fi = tab[:, 1, P0:P0 + 2, :].unsqueeze(2).to_broadcast([128, 2, NCH, TC])
                    gr, gi = G[b4]
                    kr = zf[0][:, 2 * b4:2 * b4 + 2, 0:NCH].unsqueeze(3).to_broadcast([128, 2, NCH, TC])
                    ki = zf[1][:, 2 * b4:2 * b4 + 2, 0:NCH].unsqueeze(3).to_broadcast([128, 2, NCH, TC])
                    S.op("pool", TT(v4(gr), v4(gr), kr, ALU.add), [("G", b4, 0), ztag], [("G", b4, 0)])
                    S.op("pool", TT(v4(gi), v4(gi), ki, ALU.add), [("G", b4, 1), ztag], [("G", b4, 1)])
                    V(TT(v4(X[0]), v4(gr), fr, ALU.mult), [("G", b4, 0), "tab"], [("X", 0)])
                    V(TT(v4(X[1]), v4(gi), fi, ALU.mult), [("G", b4, 1), "tab"], [("X", 1)])
                    V(TT(hb[bt][0], X[0], X[1], ALU.subtract), [("X", 0), ("X", 1)], [("hb", bt, 0)])
                    V(TT(v4(X[0]), v4(gi), fr, ALU.mult), [("G", b4, 1), "tab"], [("X", 0)])
                    V(TT(v4(X[1]), v4(gr), fi, ALU.mult), [("G", b4, 0), "tab"], [("X", 1)])
                    V(lambda e, bt=bt: e.scalar_tensor_tensor(out=hb[bt][1], in0=X[0], scalar=-1.0, in1=X[1], op0=ALU.mult, op1=ALU.subtract),
                      [("X", 0), ("X", 1)], [("hb", bt, 1)])
                yb = 2 + ch % 2
                i = 0
                for qq in range(4):
                    bt, j = qq // 2, qq % 2
                    for ri in range(2):
                        S.op("pe", lambda e, qq=qq, ri=ri, yb=yb, i=i, bt=bt, j=j: e.matmul(ps[:, yb, :], lhsT=ctb[:, ri, qq, :], rhs=hb[bt][ri][:, j * T:(j + 1) * T],
                                                                                       start=(i == 0), stop=False),
                             reads=["ctb", ("hb", bt, ri)], writes=[("ps", yb)])
                        i += 1
                S.op("pe", lambda e, yb=yb, ch=ch: e.matmul(ps[:, yb, :], lhsT=self.diagD[:, ch, :], rhs=u_bf[:, ch, :], start=False, stop=True),
                     reads=["diagD", ("u", ch)], writes=[("ps", yb)])
                self.gelu_evict(yb, ysm[:, ch, :], ("ysm", ch), ta[:, 0:T], tb[:, 0:T])

    def gelu_evict(self, bank, out_ap, out_tag, t1, t2, Tn=T):
        S = self.S
        ps = self.ps
        S.op("act", lambda e: e.activation(out=t1, in_=ps[:, bank, 0:Tn], func=AF.Square), reads=[("ps", bank)], writes=["ta"])
        S.op("dve", lambda e: e.tensor_scalar(out=t1, in0=t1, scalar1=0.044715, scalar2=1.0, op0=ALU.mult, op1=ALU.add), reads=["ta"], writes=["ta"])
        S.op("dve", lambda e: e.tensor_tensor(out=t1, in0=t1, in1=ps[:, bank, 0:Tn], op=ALU.mult), reads=["ta", ("ps", bank)], writes=["ta"])
        S.op("act", lambda e: e.activation(out=t2, in_=t1, func=AF.Sigmoid, scale=1.5957691216057308), reads=["ta"], writes=["tb"])
        S.op("dve", lambda e: e.tensor_tensor(out=out_ap, in0=t2, in1=ps[:, bank, 0:Tn], op=ALU.mult), reads=["tb", ("ps", bank)], writes=[out_tag])

    def cmul_b(self, orr, oi, ar, ai, sidx, Pa, n, zt, tin, tout, add=None, ztg="zt"):
        S = self.S
        TT = lambda o, a, b, op: (lambda e: e.tensor_tensor(out=o, in0=a, in1=b, op=op))
        mr = self.apw[:, sidx, 0, Pa].unsqueeze(2).to_broadcast([128, orr.shape[1], n])
        mi = self.apw[:, sidx, 1, Pa].unsqueeze(2).to_broadcast([128, orr.shape[1], n])
        x0, x1 = zt[0][:, :, 0:n], zt[1][:, :, 0:n]
        if add is None:
            S.op("dve", TT(orr, ar, mr, ALU.mult), [tin, "apw"], [tout])
            S.op("dve", TT(oi, ai, mi, ALU.mult), [tin, "apw"], [tout])
            S.op("dve", TT(orr, orr, oi, ALU.subtract), [tout], [tout])
            S.op("dve", TT(oi, ar, mi, ALU.mult), [tin, "apw"], [tout])
            S.op("dve", TT(ar, ai, mr, ALU.mult), [tin, "apw"], [tin])
            S.op("dve", TT(oi, oi, ar, ALU.add), [tout, tin], [tout])
            return
        S.op("dve", TT(x0, ar, mr, ALU.mult), [tin, "apw"], [ztg])
        S.op("dve", TT(x1, ai, mi, ALU.mult), [tin, "apw"], [ztg])
        S.op("dve", TT(x0, x0, x1, ALU.subtract), [ztg], [ztg])
        S.op("dve", TT(orr, x0, add[0], ALU.add), [ztg, tin], [tout])
        S.op("dve", TT(x0, ar, mi, ALU.mult), [tin, "apw"], [ztg])
        S.op("dve", TT(x1, ai, mr, ALU.mult), [tin, "apw"], [ztg])
        S.op("dve", TT(x0, x0, x1, ALU.add), [ztg], [ztg])
        S.op("dve", TT(oi, x0, add[1], ALU.add), [ztg, tin], [tout])

    def attention(self, ti, qT, obT, f32, b16):
        S = self.S
        ps = self.ps
        NT, nh = self.NT, self.n_halo
        acc = [f32(T) for _ in range(4)]
        pt = [b16(T) for _ in range(4)]
        sc = [f32(T) for _ in range(2)]
        rz = f32(T)
        mt2, q4 = ti // 4, ti % 4
        for hs in range(4):
            hh, po = hs // 2, (hs % 2) * 64
            for g in range(3):
                dl = DIL[g]
                head = 4 * g + hs
                coef = -8.0 * SLOPES[head] * dl
                qc = 2 * g + hh
                if g < 2:
                    nun, nq = 4, 128
                else:
                    nun, nq = 16, 32
                plist = []
                for which in (0, 1):
                    bank = which
                    exists = []
                    for u in range(nun):
                        if g == 0:
                            blk = 4 * ti + u - (1 - which)
                            ok = blk >= 0
                            halo = blk < 4 * nh
                            kcols = slice((blk % 8) * 128, (blk % 8) * 128 + 128) if ok else None
                            qcols = slice(u * 128, (u + 1) * 128)
                            vslot = blk % 8
                        elif g == 1:
                            tt = ti - (1 - which)
                            ok = tt >= 0
                            halo = tt < nh
                            kcols = slice((tt % 2) * T + u, (tt % 2 + 1) * T, 4) if ok else None
                            qcols = slice(u, T, 4)
                            vslot = (tt % 2) * 4 + u
                        else:
                            m = mt2 - (1 - which)
                            ok = m >= 0
                            halo = (m * 4) < nh
                            kcols = slice(m * 4 * T + u, (m * 4 + 4) * T, 16) if ok else None
                            qcols = slice(u, T, 16)
                            vslot = (m % 2) * 16 + u
                        exists.append((ok, halo, kcols, qcols, vslot))
                    if not any(x[0] for x in exists):
                        continue
                    for u, (ok, halo, kcols, qcols, vslot) in enumerate(exists):
                        S.op("pe", lambda e, u=u, kcols=kcols, qcols=qcols, bank=bank, nq=nq, po=po, g=g, hh=hh, qc=qc, ok=ok: e.matmul(
                            ps[:, bank, u * nq:(u + 1) * nq],
                            lhsT=(self.kT[g][po:po + 64, hh, kcols] if ok else self.kT[g][po:po + 64, hh, 0:128]),
                            rhs=qT[po:po + 64, qc, qcols], start=True, stop=True),
                            reads=[("kT", g), ("qT", qc)], writes=[("ps", bank)])
                    scb = sc[which]
                    sv = scb.rearrange("p (u q) -> p u q", q=nq)
                    pv_ = ps[:, bank, :].rearrange("p (u q) -> p u q", q=nq)
                    if which == 1:
                        qsl = slice(0, 128) if g < 2 else slice(32 * q4, 32 * q4 + 32)
                        S.op("dve", lambda e, sv=sv, pv_=pv_, qsl=qsl, coef=coef, nun=nun, nq=nq: e.scalar_tensor_tensor(
                            out=sv, in0=self.dm[:, 0:1, qsl].to_broadcast([128, nun, nq]), scalar=coef, in1=pv_, op0=ALU.mult, op1=ALU.add),
                            reads=["dm", ("ps", bank)], writes=[("sc", which)])
                    else:
                        groups = {}
                        for u, x in enumerate(exists):
                            var = 2 if (x[1] or not x[0]) else 1
                            if not x[0]:
                                var = 3
                            groups.setdefault(var, []).append(u)
                        for var, us in groups.items():
                            u0, u1 = us[0], us[-1] + 1
                            qsl = slice(0, 128) if g < 2 else slice(32 * q4, 32 * q4 + 32)
                            if var == 3:
                                S.op("dve", lambda e, sv=sv, u0=u0, u1=u1: e.memset(sv[:, u0:u1, :], -8.0 * MASKV), reads=[("ps", bank)], writes=[("sc", which)])
                            else:
                                S.op("dve", lambda e, sv=sv, pv_=pv_, qsl=qsl, coef=coef, u0=u0, u1=u1, nq=nq, var=var: e.scalar_tensor_tensor(
                                    out=sv[:, u0:u1, :], in0=self.dm[:, var:var + 1, qsl].to_broadcast([128, u1 - u0, nq]), scalar=coef,
                                    in1=pv_[:, u0:u1, :], op0=ALU.mult, op1=ALU.add),
                                    reads=["dm", ("ps", bank)], writes=[("sc", which)])
                    pbuf = pt[(g % 2) * 2 + which]
                    ptag = ("pt", (g % 2) * 2 + which)
                    S.op("act", lambda e, scb=scb, pbuf=pbuf: e.activation(out=pbuf, in_=scb, func=AF.Exp, scale=0.125),
                         reads=[("sc", which)], writes=[ptag])
                    plist.append((which, exists, pbuf, ptag))
                ob = 2 + g % 2
                for u in range(nun):
                    for pi_, (which, exists, pbuf, ptag) in enumerate(plist):
                        ok, halo, kcols, qcols, vslot = exists[u]
                        vs = vslot if ok else 0
                        S.op("pe", lambda e, u=u, vs=vs, ob=ob, nq=nq, pbuf=pbuf, g=g, hs=hs, st_=(pi_ == 0), sp_=(pi_ == len(plist) - 1): e.matmul(
                            ps[0:65, ob, u * nq:(u + 1) * nq], lhsT=self.V[g][:, vs, hs, 0:65], rhs=pbuf[:, u * nq:(u + 1) * nq],
                            start=st_, stop=sp_),
                            reads=[("V", g), ptag], writes=[("ps", ob)])
                ob = 2 + g % 2
                a = acc[hs]
                if g == 0:
                    S.op("act", lambda e, a=a, ob=ob: e.copy(out=a[0:65, :], in_=ps[0:65, ob, :]), reads=[("ps", ob)], writes=[("acc", hs)])
                else:
                    av = a[0:65, :].rearrange("p (m r) -> p r m", r=dl)
                    pvw = ps[0:65, ob, :].rearrange("p (r m) -> p r m", r=dl)
                    S.op("dve", lambda e, av=av, pvw=pvw: e.tensor_tensor(out=av, in0=av, in1=pvw, op=ALU.add),
                         reads=[("ps", ob), ("acc", hs)], writes=[("acc", hs)])
            a = acc[hs]
            S.op("dve", lambda e, a=a: e.reciprocal(out=rz[64:65, :], in_=a[64:65, :]), reads=[("acc", hs)], writes=["rz"])
            S.op("pe", lambda e: e.matmul(ps[0:64, 4, :], lhsT=self.ones1[64:65, 0:64], rhs=rz[64:65, :], start=True, stop=True),
                 reads=["rz", "ones1"], writes=[("ps", 4)])
            S.op("dve", lambda e, a=a, hs=hs, hh=hh, po=po: e.tensor_tensor(out=obT[po:po + 64, hh, :], in0=a[0:64, :], in1=ps[0:64, 4, :], op=ALU.mult)
                 if po == 0 else e.tensor_tensor(out=obT[po:po + 64, hh, :], in0=a[0:64, :], in1=ps[0:64, 4, :], op=ALU.mult),
                 reads=[("acc", hs), ("ps", 4)], writes=[("obT", hh)])

    def _prev_exists(self, g, ti):
        if g == 0:
            return [(4 * ti + u - 1 >= 0,) for u in range(4)]
        if g == 1:
            return [(ti - 1 >= 0,)] * 4
        return [(ti // 4 - 1 >= 0,)] * 16

    def mix_out(self, ti, obT, f32, b16, Tn=T):
        S = self.S
        d = self.din
        ps = self.ps
        ysm = self.ysm
        ya = b16(8 * T).rearrange("p (c t) -> p c t", c=8)
        mg = b16(8 * T).rearrange("p (c t) -> p c t", c=8)
        sgt = [f32(T) for _ in range(2)]
        m1 = [f32(T) for _ in range(2)]

        def cons_glu(oc, bank):
            sb_ = sgt[oc % 2]
            S.op("act", lambda e: e.activation(out=sb_[:, :Tn], in_=ps[:, bank, :Tn], func=AF.Sigmoid), reads=[("ps", bank)], writes=[("sgt", oc % 2)])
            S.op("dve", lambda e: e.tensor_tensor(out=ya[:, oc, :Tn], in0=sb_[:, :Tn], in1=ysm[:, oc, :Tn], op=ALU.mult),
                 reads=[("sgt", oc % 2), ("ysm", oc)], writes=[("ya", oc)])
        self.linear(d["w_glu"], 4, ysm, "ysm", 8, Tn, cons_glu)

        def cons_ga(oc, bank):
            S.op("act", lambda e: e.activation(out=mg[:, oc, :Tn], in_=ps[:, bank, :Tn], func=AF.Sigmoid), reads=[("ps", bank)], writes=[("mg", oc)])
        self.linear(d["w_ga"], 4, self.hT, "hT", 8, Tn, cons_ga)

        def cons_pa(oc, bank):
            S.op("dve", lambda e: e.tensor_tensor(out=mg[:, oc, :Tn], in0=mg[:, oc, :Tn], in1=ps[:, bank, :Tn], op=ALU.mult),
                 reads=[("ps", bank), ("mg", oc)], writes=[("mg", oc)])
        self.linear(d["w_pa"], 4, ya, "ya", 8, Tn, cons_pa)
        gbuf = ya

        def cons_gb(oc, bank):
            S.op("act", lambda e: e.activation(out=gbuf[:, oc, :Tn], in_=ps[:, bank, :Tn], func=AF.Sigmoid), reads=[("ps", bank)], writes=[("ya", oc)])
        self.linear(d["w_gb"], 4, self.hT, "hT", 8, Tn, cons_gb)

        def cons_pb(oc, bank):
            mm = m1[oc % 2]
            S.op("dve", lambda e: e.tensor_tensor(out=mm[:, :Tn], in0=gbuf[:, oc, :Tn], in1=ps[:, bank, :Tn], op=ALU.mult),
                 reads=[("ps", bank), ("ya", oc)], writes=[("m1", oc % 2)])
            S.op("dve", lambda e: e.tensor_tensor(out=mg[:, oc, :Tn], in0=mg[:, oc, :Tn], in1=mm[:, :Tn], op=ALU.add),
                 reads=[("m1", oc % 2), ("mg", oc)], writes=[("mg", oc)])
        self.linear(d["w_pb"], 4, obT, "obT", 2, Tn, cons_pb)

        def cons_out(oc, bank):
            S.op("dve", lambda e: e.tensor_tensor(out=self.xT[:, oc, :Tn], in0=self.xT[:, oc, :Tn], in1=ps[:, bank, :Tn], op=ALU.add),
                 reads=[("ps", bank), ("xT", oc)], writes=[("xT", oc)])
        self.linear(d["w_out"], 4, mg, "mg", 8, Tn, cons_out)

    def sample_pass(self):
        S = self.S
        d = self.din
        ps = self.ps
        xT, hT = self.xT, self.hT
        Tn = 4
        TT = lambda o, a, b, op: (lambda e: e.tensor_tensor(out=o, in0=a, in1=b, op=op))
        V = lambda fn, r, w: S.op("dve", fn, reads=r, writes=w)
        AC = lambda fn, r, w: S.op("act", fn, reads=r, writes=w)
        for g, nm in enumerate(("0", "1", "2")):
            W = WIN[g]
            for b in range(4):
                for r0 in range(0, W - 1, 256):
                    r1 = min(W - 1, r0 + 256)
                    S.dma("sp", self.dout["skv" + nm][b, r0:r1, :], d["c" + nm][b, r0 + 1:r1 + 1, :])
        S.dma("sp", xT[:, :, 0:Tn], d["xsT"].rearrange("(c p) t -> p c t", p=128), writes=[("xT", c) for c in range(8)])
        self.ffn_block("1", 0, Tn)
        f32, b16, off = self.carve()
        self.rmsnorm(8, Tn)
        u_bf = b16(8 * T).rearrange("p (c t) -> p c t", c=8)
        ysm = b16(8 * T).rearrange("p (c t) -> p c t", c=8)
        self.ysm = ysm
        obT = b16(2 * T).rearrange("p (c t) -> p c t", c=2)

        def cons_u(oc, bank):
            S.op("act", lambda e: e.copy(out=u_bf[:, oc, 0:Tn], in_=ps[:, bank, 0:Tn]), reads=[("ps", bank)], writes=[("u", oc)])
        self.linear(d["w_u"], 4, hT, "hT", 8, Tn, cons_u)
        qs = f32(768).rearrange("p (g c) -> p g c", g=3)
        knv = f32(3 * 512).rearrange("p (g c) -> p g c", g=3)
        sq = f32(256)
        ssq = f32(4)
        for g in range(3):
            S.dma("pool", self.wKV[:, :, :], d["w_kv"][g], writes=["wKV"] + [("wD", 3), ("wD", 4), ("wD", 5), ("wD", 6)])
            b = self._wrot
            self._wrot = (self._wrot + 1) % 4
            S.dma("pool", self.wA[b][:, :, :], d["w_qs"][g], writes=[("wA", b)])
            for k in range(8):
                S.op("pe", lambda e, k=k: e.matmul(ps[0:Tn, 0, :], lhsT=hT[:, k, 0:Tn], rhs=self.wKV[:, k, :], start=(k == 0), stop=(k == 7)),
                     reads=["wKV", ("hT", k)] + [("wD", 3), ("wD", 4), ("wD", 5), ("wD", 6)], writes=[("ps", 0)])
            for k in range(8):
                S.op("pe", lambda e, k=k, b=b: e.matmul(ps[0:Tn, 1, 0:256], lhsT=hT[:, k, 0:Tn], rhs=self.wA[b][:, k, :], start=(k == 0), stop=(k == 7)),
                     reads=[("wA", b), ("hT", k)], writes=[("ps", 1)])
            for (bank, gi, dst) in ((0, 1, knv[0:Tn, g, 0:256]), (1, 0, qs[0:Tn, g, :])):
                src = ps[0:Tn, bank, 0:256]
                S.op("act", lambda e, src=src: e.activation(out=sq[0:Tn, :], in_=src, func=AF.Square), reads=[("ps", bank)], writes=["s_sq"])
                V(lambda e: e.tensor_reduce(out=ssq[0:Tn, :], in_=sq[0:Tn, :].rearrange("p (h c) -> p h c", h=4), axis=AX.X, op=ALU.add), ["s_sq"], ["s_ssq"])
                S.op("act", lambda e: e.activation(out=ssq[0:Tn, :], in_=ssq[0:Tn, :], func=AF.Sqrt, bias=self.eps[0:Tn, 0:1], scale=1.0 / 64),
                     reads=["s_ssq", "eps"], writes=["s_ssq"])
                V(lambda e: e.reciprocal(out=ssq[0:Tn, :], in_=ssq[0:Tn, :]), ["s_ssq"], ["s_ssq"])
                V(lambda e, src=src, dst=dst: e.tensor_tensor(out=dst.rearrange("p (h c) -> p h c", h=4), in0=src.rearrange("p (h c) -> p h c", h=4),
                                                               in1=ssq[0:Tn, :].unsqueeze(2).to_broadcast([Tn, 4, 64]), op=ALU.mult),
                  [("ps", bank), "s_ssq"], ["qkv_s"])
                V(lambda e, dst=dst, gi=gi: e.tensor_tensor(out=dst, in0=dst, in1=self.gain_bc[0:Tn, gi, :], op=ALU.mult), ["qkv_s", "gain_bc"], ["qkv_s"])
            S.op("act", lambda e, g=g: e.copy(out=knv[0:Tn, g, 256:512], in_=ps[0:Tn, 0, 256:512]), reads=[("ps", 0)], writes=["qkv_s"])
            S.dma("sp", self.dout["skv%d" % g][:, WIN[g] - 1, :], knv[0:Tn, g, :], reads=["qkv_s"])
        mark = off[0]
        S.barrier()
        h0 = f32(2 * 32 * 4).rearrange("p (r a t) -> p r a t", r=2, a=32)
        h1 = f32(2 * 32 * 4).rearrange("p (r a t) -> p r a t", r=2, a=32)
        hb = [b16(32 * 4).rearrange("p (a t) -> p a t", a=32) for _ in range(2)]
        t1 = f32(T)
        t2 = f32(T)
        s1 = f32(128).rearrange("p (a t) -> p a t", a=32)
        s2 = f32(128).rearrange("p (a t) -> p a t", a=32)
        btb = [b16(4 * 2 * 128).rearrange("p (a r c) -> p a r c", a=4, r=2) for _ in range(2)]
        ctb = [b16(2 * 4 * 128).rearrange("p (r a c) -> p r a c", r=2, a=4) for _ in range(2)]
        S.dma("sp", h0, d["h0"], writes=["h0"])
        for ch in range(8):
            bb = btb[ch % 2]
            S.dma("sp", bb, self.bt_d[ch], reads=["bt_d"], writes=[("btb", ch % 2)])
            for qq in range(4):
                P = 4 * ch + qq
                for ri in range(2):
                    S.op("pe", lambda e, bb=bb, qq=qq, ri=ri, P=P, ch=ch: e.matmul(ps[:, ri, P * 4:(P + 1) * 4], lhsT=bb[:, qq, ri, :], rhs=u_bf[:, ch, 0:Tn],
                                                                              start=True, stop=True),
                         reads=[("btb", ch % 2), ("u", ch)], writes=[("ps", ri)])
        ar = self.apw[:, 0, 0, :].unsqueeze(2).to_broadcast([128, 32, 4])
        ai = self.apw[:, 0, 1, :].unsqueeze(2).to_broadcast([128, 32, 4])
        bur = ps[:, 0, 0:128].rearrange("p (a t) -> p a t", a=32)
        bui = ps[:, 1, 0:128].rearrange("p (a t) -> p a t", a=32)
        V(TT(s1, h0[:, 0], ar, ALU.mult), ["h0", "apw"], ["s1"])
        V(TT(s2, h0[:, 1], ai, ALU.mult), ["h0", "apw"], ["s2"])
        V(TT(s1, s1, s2, ALU.subtract), ["s1", "s2"], ["s1"])
        V(TT(h1[:, 0], s1, bur, ALU.add), ["s1", ("ps", 0)], ["h1"])
        V(TT(s1, h0[:, 0], ai, ALU.mult), ["h0", "apw"], ["s1"])
        V(TT(s2, h0[:, 1], ar, ALU.mult), ["h0", "apw"], ["s2"])
        V(TT(s1, s1, s2, ALU.add), ["s1", "s2"], ["s1"])
        V(TT(h1[:, 1], s1, bui, ALU.add), ["s1", ("ps", 1)], ["h1"])
        S.dma("sp", self.dout["sst"], h1, reads=["h1"])
        V(lambda e: e.tensor_copy(out=hb[0], in_=h1[:, 0]), ["h1"], ["hb_s"])
        V(lambda e: e.tensor_scalar(out=hb[1], in0=h1[:, 1], scalar1=-1.0, scalar2=None, op0=ALU.mult), ["h1"], ["hb_s"])
        for ch in range(8):
            cb = ctb[ch % 2]
            S.dma("pool", cb, d["ssm_CT"][:, :, 4 * ch:4 * ch + 4, :], writes=[("ctb", ch % 2)])
            yb = 4 + ch % 2
            i = 0
            for qq in range(4):
                for ri in range(2):
                    S.op("pe", lambda e, cb=cb, qq=qq, ri=ri, yb=yb, i=i, ch=ch: e.matmul(ps[:, yb, 0:Tn], lhsT=cb[:, ri, qq, :], rhs=hb[ri][:, 4 * ch + qq, :],
                                                                                     start=(i == 0), stop=False),
                         reads=[("ctb", ch % 2), "hb_s"], writes=[("ps", yb)])
                    i += 1
            S.op("pe", lambda e, yb=yb, ch=ch: e.matmul(ps[:, yb, 0:Tn], lhsT=self.diagD[:, ch, :], rhs=u_bf[:, ch, 0:Tn], start=False, stop=True),
                 reads=["diagD", ("u", ch)], writes=[("ps", yb)])
            self.gelu_evict(yb, ysm[:, ch, 0:Tn], ("ysm", ch), t1[:, 0:Tn], t2[:, 0:Tn], Tn)
        S.barrier()
        off[0] = mark
        pens = f32(12)
        sel4 = f32(4 * 128).rearrange("p (b m) -> p b m", b=4)
        selc = f32(16).rearrange("p (b m) -> p b m", b=4)
        kc = [f32(512) for _ in range(3)]
        prod = f32(256)
        sc = f32(12)
        stg = f32(780)
        OZ = f32(780)
        pn = f32(12)
        zs = f32(4)
        os_ = f32(256)
        S.dma("sp", pens, d["pens"], writes=["pens"])
        S.dma("sp", sel4[0:4], d["sel4"], writes=["sel4"])
        S.dma("sp", selc, d["selc"], writes=["selc"])
        for b in range(4):
            S.op("pe", lambda e, b=b: e.matmul(ps[:, 0, :], lhsT=sel4[0:4, b, :], rhs=qs[0:4, :, :].rearrange("p g c -> p (g c)")[:, 0:512], start=True, stop=True),
                 reads=["sel4", "qkv_s"], writes=[("ps", 0)])
            S.op("pe", lambda e, b=b: e.matmul(ps[:, 1, 0:256], lhsT=sel4[0:4, b, :], rhs=qs[0:4, :, :].rearrange("p g c -> p (g c)")[:, 512:768], start=True, stop=True),
                 reads=["sel4", "qkv_s"], writes=[("ps", 1)])
            for g in range(3):
                W, dl = WIN[g], DIL[g]
                S.dma("sp", kc[g], d["c%d" % g][b, 0:W:dl, :], writes=[("kc", g)])
                qb = ps[:, 0, g * 256:(g + 1) * 256] if g < 2 else ps[:, 1, 0:256]
                V(TT(prod, kc[g][:, 0:256], qb, ALU.mult), [("kc", g), ("ps", 0), ("ps", 1)], ["prod"])
                V(lambda e, g=g: e.tensor_reduce(out=sc[:, 4 * g:4 * g + 4], in_=prod.rearrange("p (h c) -> p h c", h=4), axis=AX.X, op=ALU.add),
                  ["prod"], ["sc_s"])
            V(TT(sc, sc, pens, ALU.add), ["sc_s", "pens"], ["sc_s"])
            S.op("act", lambda e: e.activation(out=stg[:, 768:780], in_=sc, func=AF.Exp, scale=0.125), reads=["sc_s"], writes=["stg"])
            for g in range(3):
                V(lambda e, g=g: e.tensor_tensor(out=stg[:, g * 256:(g + 1) * 256].rearrange("p (h c) -> p h c", h=4),
                                                  in0=kc[g][:, 256:512].rearrange("p (h c) -> p h c", h=4),
                                                  in1=stg[:, 768 + 4 * g:772 + 4 * g].unsqueeze(2).to_broadcast([128, 4, 64]), op=ALU.mult),
                  [("kc", g), "stg"], ["stg"])
            S.op("pe", lambda e, b=b: e.matmul(ps[0:4, 2, :], lhsT=selc[:, b, :], rhs=stg[:, 0:512], start=(b == 0), stop=(b == 3)),
                 reads=["selc", "stg"], writes=[("ps", 2)])
            S.op("pe", lambda e, b=b: e.matmul(ps[0:4, 3, 0:268], lhsT=selc[:, b, :], rhs=stg[:, 512:780], start=(b == 0), stop=(b == 3)),
                 reads=["selc", "stg"], writes=[("ps", 3)])
        S.op("act", lambda e: e.copy(out=OZ[0:4, 0:512], in_=ps[0:4, 2, :]), reads=[("ps", 2)], writes=["OZ"])
        S.op("act", lambda e: e.copy(out=OZ[0:4, 512:780], in_=ps[0:4, 3, 0:268]), reads=[("ps", 3)], writes=["OZ"])
        for g in range(3):
            V(TT(prod[0:4, :], qs[0:4, g, :], knv[0:4, g, 0:256], ALU.mult), ["qkv_s"], ["prod"])
            V(lambda e, g=g: e.tensor_reduce(out=pn[0:4, 4 * g:4 * g + 4], in_=prod[0:4, :].rearrange("p (h c) -> p h c", h=4), axis=AX.X, op=ALU.add),
              ["prod"], ["pn"])
        S.op("act", lambda e: e.activation(out=pn[0:4, :], in_=pn[0:4, :], func=AF.Exp, scale=0.125), reads=["pn"], writes=["pn"])
        V(TT(OZ[0:4, 768:780], OZ[0:4, 768:780], pn[0:4, :], ALU.add), ["OZ", "pn"], ["OZ"])
        for g in range(3):
            V(lambda e, g=g: e.tensor_tensor(out=prod[0:4, :].rearrange("p (h c) -> p h c", h=4), in0=knv[0:4, g, 256:512].rearrange("p (h c) -> p h c", h=4),
                                              in1=pn[0:4, 4 * g:4 * g + 4].unsqueeze(2).to_broadcast([4, 4, 64]), op=ALU.mult), ["qkv_s", "pn"], ["prod"])
            V(TT(OZ[0:4, g * 256:(g + 1) * 256], OZ[0:4, g * 256:(g + 1) * 256], prod[0:4, :], ALU.add), ["OZ", "prod"], ["OZ"])
        V(TT(zs[0:4, :], OZ[0:4, 768:772], OZ[0:4, 772:776], ALU.add), ["OZ"], ["zs"])
        V(TT(zs[0:4, :], zs[0:4, :], OZ[0:4, 776:780], ALU.add), ["OZ", "zs"], ["zs"])
        V(lambda e: e.reciprocal(out=zs[0:4, :], in_=zs[0:4, :]), ["zs"], ["zs"])
        V(TT(os_[0:4, :], OZ[0:4, 0:256], OZ[0:4, 256:512], ALU.add), ["OZ"], ["os"])
        V(TT(os_[0:4, :], os_[0:4, :], OZ[0:4, 512:768], ALU.add), ["OZ", "os"], ["os"])
        V(lambda e: e.tensor_tensor(out=os_[0:4, :].rearrange("p (h c) -> p h c", h=4), in0=os_[0:4, :].rearrange("p (h c) -> p h c", h=4),
                                    in1=zs[0:4, :].unsqueeze(2).to_broadcast([4, 4, 64]), op=ALU.mult), ["os", "zs"], ["os"])
        for hh in range(2):
            S.op("pe", lambda e, hh=hh: e.matmul(ps[:, 4 + hh, 0:4], lhsT=os_[0:4, hh * 128:(hh + 1) * 128], rhs=self.identf[0:4, 0:4], start=True, stop=True),
                 reads=["os", "identf"], writes=[("ps", 4 + hh)])
            S.op("act", lambda e, hh=hh: e.copy(out=obT[:, hh, 0:Tn], in_=ps[:, 4 + hh, 0:4]), reads=[("ps", 4 + hh)], writes=[("obT", hh)])
        S.barrier()
        off[0] = mark
        self.mix_out(0, obT, f32, b16, Tn)
        S.barrier()
        self.ffn_block("2", 16, Tn)
        S.dma("sp", self.dout["ysT"].rearrange("(c p) t -> p c t", p=128), xT[:, :, 0:Tn], reads=[("xT", c) for c in range(8)])


_NC_CACHE = {}


def _get_nc():
    if "nc" not in _NC_CACHE:
        kk = K(4, 4, sample=True)
        _NC_CACHE["nc"] = kk.build()
    return _NC_CACHE["nc"]


def kernel(**inp):
    inp = {k: np.asarray(v) for k, v in inp.items()}
    nc = _get_nc()
    shared = prep_shared(inp)
    consts = [host_consts(True), host_consts(False)]
    xp = inp["x_prompt"]
    in_maps = []
    for c in range(8):
        b, half = c // 2, c % 2
        m = dict(shared)
        m.update(consts[half])
        own = xp[b, half * 2048:(half + 1) * 2048]
        halo = xp[b, 0:2048] if half == 1 else np.zeros_like(own)
        m["xT"] = np.ascontiguousarray(np.concatenate([halo, own], axis=0).T)
        sl = slice(4 * c, 4 * c + 4)
        m["xsT"] = np.ascontiguousarray(inp["x_sample"][sl, 0, :].T)
        h0 = np.stack([np.stack([lay_gp(inp["state_ssm_re"][0, 4 * c + t]) for t in range(4)], axis=-1),
                       np.stack([lay_gp(inp["state_ssm_im"][0, 4 * c + t]) for t in range(4)], axis=-1)], axis=1)
        m["h0"] = np.ascontiguousarray(h0)
        m["c0"] = np.ascontiguousarray(inp["cache_kv_w128"][0, sl].reshape(4, 128, 512))
        m["c1"] = np.ascontiguousarray(inp["cache_kv_w512"][0, sl].reshape(4, 512, 512))
        m["c2"] = np.ascontiguousarray(inp["cache_kv_w2048"][0, sl].reshape(4, 2048, 512))
        in_maps.append({k: np.ascontiguousarray(v, dtype=np.float32) for k, v in m.items()})
    res = run_bass_kernel_spmd(nc, in_maps, core_ids=list(range(8)))
    R = res.results
    unlay = lambda a: a.reshape(2, 64, 32).transpose(2, 0, 1).reshape(64, 64)
    y_prompt = np.zeros((4, 4096, 1024), np.float32)
    y_sample = np.zeros((32, 1, 1024), np.float32)
    p_re = np.zeros((1, 4, 64, 64), np.float32)
    p_im = np.zeros((1, 4, 64, 64), np.float32)
    s_re = np.zeros((1, 32, 64, 64), np.float32)
    s_im = np.zeros((1, 32, 64, 64), np.float32)
    pkv = [np.zeros((1, 4, w, 2, 4, 64), np.float32) for w in WIN]
    skv = [np.zeros((1, 32, w, 2, 4, 64), np.float32) for w in WIN]
    for c in range(8):
        b, half = c // 2, c % 2
        r = R[c]
        y_prompt[b, half * 2048:(half + 1) * 2048] = r["yT"].T
        y_sample[4 * c:4 * c + 4, 0, :] = r["ysT"].T
        if half == 1:
            p_re[0, b] = unlay(r["pst"][:, 0, :])
            p_im[0, b] = unlay(r["pst"][:, 1, :])
            for g, w in enumerate(WIN):
                pkv[g][0, b] = r["pkv"][g][2048 - w:].reshape(w, 2, 4, 64)
        for t in range(4):
            s_re[0, 4 * c + t] = unlay(r["sst"][:, 0, :, t])
            s_im[0, 4 * c + t] = unlay(r["sst"][:, 1, :, t])
        for g, w in enumerate(WIN):
            skv[g][0, 4 * c:4 * c + 4] = r["skv%d" % g].reshape(4, w, 2, 4, 64)
    return (y_prompt, y_sample, p_re, p_im, pkv[0], pkv[1], pkv[2], s_re, s_im, skv[0], skv[1], skv[2])
```

```python
import contextlib
import math
import numpy as np
import concourse.bass as bass
import concourse.mybir as mybir
from concourse.bass_utils import run_bass_kernel_spmd

F32 = mybir.dt.float32
BF16 = mybir.dt.bfloat16
ALU = mybir.AluOpType
AF = mybir.ActivationFunctionType
AX = mybir.AxisListType

ENGS = ["pe", "act", "dve", "pool", "sp"]
T = 512
NPF = 11
TC = 32
NCH = T // TC
DIL = (1, 4, 16)
WIN = (128, 512, 2048)
SLOPES = [2.0 ** (-8.0 * i / 12.0) for i in range(1, 13)]
MASKV = 1.0e5


class Sched:
    def __init__(self, nc, stack, n_dma_slots=40):
        self.nc = nc
        self.ops = {e: [] for e in ENGS}
        self.sem = {e: stack.enter_context(nc.semaphore("s_" + e)) for e in ENGS if e != "sp"}
        self.cnt = {e: 0 for e in ENGS}
        self.dsem = [stack.enter_context(nc.semaphore("d%d" % i)) for i in range(n_dma_slots)]
        self.dcnt = [0] * n_dma_slots
        self.dnext = 0
        self.dnext_sw = 0
        self.waited = {e: {} for e in ENGS}
        self.lastw = {}
        self.readers = {}

    def _semof(self, pk):
        return self.sem[pk[1]] if pk[0] == "e" else self.dsem[pk[1]]

    def _need(self, eng, waits, pk, v):
        if pk == ("e", eng) and eng == "pe":
            return
        if self.waited[eng].get(pk, 0) >= v:
            return
        self.waited[eng][pk] = v
        waits[pk] = v

    def _deps(self, eng, reads, writes):
        waits = {}
        for r in reads:
            lw = self.lastw.get(r)
            if lw is not None:
                self._need(eng, waits, lw[0], lw[1])
        for w in writes:
            lw = self.lastw.get(w)
            if lw is not None:
                self._need(eng, waits, lw[0], lw[1])
            for pk, v in self.readers.get(w, {}).items():
                self._need(eng, waits, pk, v)
        return waits

    def _record(self, pk, val, reads, writes):
        for r in reads:
            self.readers.setdefault(r, {})[pk] = val
        for w in writes:
            self.lastw[w] = (pk, val)
            self.readers[w] = {}

    def op(self, eng, fn, reads=(), writes=()):
        waits = self._deps(eng, reads, writes)
        self.cnt[eng] += 1
        self.ops[eng].append((list(waits.items()), fn, ("e", eng), 1))
        self._record(("e", eng), self.cnt[eng], reads, writes)

    def dma(self, q, out, in_, reads=(), writes=()):
        half = len(self.dsem) // 2
        if q == "pool":
            slot = half + self.dnext_sw
            self.dnext_sw = (self.dnext_sw + 1) % (len(self.dsem) - half)
        else:
            slot = self.dnext
            self.dnext = (self.dnext + 1) % half
        waits = self._deps(q, reads, writes)
        k = self.dcnt[slot] + 1
        self.dcnt[slot] = k
        if k > 1:
            self._need(q, waits, ("d", slot), 16 * (k - 1))
        fn = lambda e, out=out, in_=in_: e.dma_start(out=out, in_=in_)
        self.ops[q].append((list(waits.items()), fn, ("d", slot), 16))
        self._record(("d", slot), 16 * k, reads, writes)

    def barrier(self):
        for e in ENGS:
            waits = {}
            for slot in range(len(self.dsem)):
                if self.dcnt[slot]:
                    self._need(e, waits, ("d", slot), 16 * self.dcnt[slot])
            for p in ENGS:
                if p != "sp" and p != e and self.cnt[p]:
                    self._need(e, waits, ("e", p), self.cnt[p])
            if waits:
                self.ops[e].append((list(waits.items()), None, None, 0))

    def finish(self):
        waits = {}
        for slot in range(len(self.dsem)):
            if self.dcnt[slot]:
                self._need("sp", waits, ("d", slot), 16 * self.dcnt[slot])
        for e in ENGS:
            if e != "sp" and self.cnt[e]:
                self._need("sp", waits, ("e", e), self.cnt[e])
        self.ops["sp"].append((list(waits.items()), None, None, 0))

    def emit(self):
        nc = self.nc
        with nc.Block() as block:
            def run(e, name):
                for waits, fn, pk, inc in self.ops[name]:
                    for wpk, v in waits:
                        e.wait_ge(self._semof(wpk), v)
                    if fn is not None:
                        fn(e).then_inc(self._semof(pk), inc)

            @block.tensor
            def _(e):
                run(e, "pe")

            @block.scalar
            def _(e):
                run(e, "act")

            @block.vector
            def _(e):
                run(e, "dve")

            @block.gpsimd
            def _(e):
                run(e, "pool")

            @block.sync
            def _(e):
                run(e, "sp")


def lay_w8(w, piece=256):
    n = w.shape[1] // piece
    return np.ascontiguousarray(w.reshape(8, 128, n, piece).transpose(2, 1, 0, 3))


def lay_wk(w, nk, piece=256):
    n = w.shape[1] // piece
    return np.ascontiguousarray(w.reshape(nk, 128, n, piece).transpose(2, 1, 0, 3))


def lay_wd(w):
    return np.ascontiguousarray(w.reshape(NPF, 2, 128, 2, 512).transpose(3, 0, 2, 1, 4))


def lay_vec(v):
    return np.ascontiguousarray(v.reshape(8, 128).T)


def lay_gp(a):
    return np.ascontiguousarray(a.reshape(32, 2, 64).transpose(1, 2, 0).reshape(128, 32))


def host_consts(half_is_first):
    c = {}
    c["ident"] = np.eye(128, dtype=np.float32)
    ob = np.zeros((128, 128), np.float32)
    ob[:64, :64] = 1.0 / 64
    ob[64:, 64:] = 1.0 / 64
    c["onesblk"] = ob
    k = np.arange(128)[:, None].astype(np.float32)
    q = np.arange(128)[None, :].astype(np.float32)
    cur = np.where(k <= q, q - k, MASKV).astype(np.float32)
    prev = np.where(k >= q, q + 128 - k, MASKV).astype(np.float32)
    prevh = np.full((128, 128), MASKV, np.float32) if half_is_first else prev
    c["dm"] = np.ascontiguousarray(np.stack([cur, prev, prevh], axis=1))
    m = np.ones((128, 2 * T), np.float32)
    m[:, ::TC] = 0.0
    c["scanmask"] = m
    pen = np.zeros((128, 12), np.float32)
    for g in range(3):
        for h in range(4):
            pen[:, 4 * g + h] = -8.0 * SLOPES[4 * g + h] * (WIN[g] - DIL[g] * np.arange(128))
    c["pens"] = pen
    sel = np.zeros((4, 4, 128), np.float32)
    for b in range(4):
        sel[b, b, :] = 1.0
    c["sel4"] = sel
    selc = np.zeros((128, 4, 4), np.float32)
    for b in range(4):
        selc[:, b, b] = 1.0
    c["selc"] = selc
    return c


def prep_shared(inp):
    L = 0
    d = {}
    for nm, key in (("1", "ffn1"), ("2", "ffn2")):
        d["wg" + nm] = lay_w8(inp[key + "_w_gate"][L])
        d["wu" + nm] = lay_w8(inp[key + "_w_up"][L])
        d["wd" + nm] = lay_wd(inp[key + "_w_down"][L])
    w_in = inp["w_in"][L]
    d["w_u"] = lay_w8(w_in[:, 0:1024])
    d["w_q"] = lay_w8(w_in[:, 1024:1792])
    d["w_k"] = lay_w8(w_in[:, 1792:2560])
    wk = w_in[:, 1792:2560].reshape(1024, 3, 256)
    wv = w_in[:, 2560:3328].reshape(1024, 3, 256)
    wkv = np.concatenate([wk, wv], axis=2).reshape(1024, 3 * 512)
    d["w_kv"] = lay_w8(wkv, piece=512)
    wq = w_in[:, 1024:1792].reshape(1024, 3, 256)
    d["w_qs"] = lay_w8(np.ascontiguousarray(wq.reshape(1024, 768)), piece=256)
    d["w_ga"] = lay_w8(w_in[:, 3328:4352])
    d["w_gb"] = lay_w8(w_in[:, 4352:5376])
    d["w_glu"] = lay_w8(inp["w_glu"][L])
    d["w_pa"] = lay_w8(inp["w_proj_a"][L])
    d["w_out"] = lay_w8(inp["w_out"][L])
    d["w_pb"] = lay_wk(inp["w_proj_b"][L], 2)
    vecs = np.zeros((128, 40), np.float32)
    vecs[:, 0:8] = lay_vec(inp["ffn1_norm"][L])
    vecs[:, 8:16] = lay_vec(inp["mix_norm"][L])
    vecs[:, 16:24] = lay_vec(inp["ffn2_norm"][L])
    vecs[:, 24:32] = lay_vec(inp["ssm_d"][L])
    vecs[:, 32] = np.tile(inp["q_gain"][L], 2)
    vecs[:, 33] = np.tile(inp["k_gain"][L], 2)
    d["vecs"] = vecs
    d["gain_bc"] = np.ascontiguousarray(np.stack([np.tile(inp["q_gain"][L][None, :], (128, 4)),
                                                  np.tile(inp["k_gain"][L][None, :], (128, 4))], axis=1))
    sc = np.stack([lay_gp(inp["ssm_lambda_re"][L]), lay_gp(inp["ssm_lambda_im"][L]),
                   lay_gp(np.tile(inp["ssm_log_dt"][L][:, None], (1, 64)))], axis=1)
    d["ssm_sc"] = np.ascontiguousarray(sc)
    B = np.stack([inp["ssm_b_re"][L], inp["ssm_b_im"][L]], axis=0)
    B = B.reshape(2, 32, 2, 64, 16).transpose(2, 3, 0, 1, 4).reshape(128, 2, 32, 16)
    d["ssm_B"] = np.ascontiguousarray(B)
    C = np.stack([inp["ssm_c_re"][L], inp["ssm_c_im"][L]], axis=0)
    CT = np.zeros((128, 2, 32, 128), np.float32)
    for P in range(32):
        qq = P % 4
        for g2 in range(2):
            g = 2 * P + g2
            for ri in range(2):
                CT[g2 * 64:(g2 + 1) * 64, ri, P, 32 * qq + 16 * g2: 32 * qq + 16 * g2 + 16] = C[ri, g].T
    d["ssm_CT"] = CT
    return d


class K:
    def __init__(self, n_halo, n_own, sample, dbg=None, level=9):
        self.level = level
        self.pipe_halo = True
        self.pre_ffn1 = set()
        self.n_halo, self.n_own, self.sample = n_halo, n_own, sample
        self.NT = n_halo + n_own
        self.dbg = dbg or []
        self.nc = bass.Bass("TRN2", target_bir_lowering=False)
        self.din = {}
        self.dout = {}

    def di(self, name, shape):
        self.din[name] = self.nc.dram_tensor(name, list(shape), F32, kind="ExternalInput").ap()
        return self.din[name]

    def do(self, name, shape):
        self.dout[name] = self.nc.dram_tensor(name, list(shape), F32, kind="ExternalOutput").ap()
        return self.dout[name]

    def sb(self, name, shape, dt):
        return self.st.enter_context(self.nc.sbuf_tensor(name, list(shape), dt))

    def build(self):
        nc = self.nc
        NT = self.NT
        di, do = self.di, self.do
        di("xT", [1024, NT * T])
        for nm in ("1", "2"):
            di("wg" + nm, [NPF, 128, 8, 256]); di("wu" + nm, [NPF, 128, 8, 256]); di("wd" + nm, [2, NPF, 128, 2, 512])
        for nm in ("w_u", "w_ga", "w_gb", "w_glu", "w_pa", "w_out"):
            di(nm, [4, 128, 8, 256])
        di("w_q", [3, 128, 8, 256]); di("w_k", [3, 128, 8, 256]); di("w_qs", [3, 128, 8, 256])
        di("w_kv", [3, 128, 8, 512])
        di("w_pb", [4, 128, 2, 256])
        di("vecs", [128, 40]); di("gain_bc", [128, 2, 256])
        di("ssm_sc", [128, 3, 32]); di("ssm_B", [128, 2, 32, 16]); di("ssm_CT", [128, 2, 32, 128])
        di("ident", [128, 128]); di("onesblk", [128, 128]); di("dm", [128, 3, 128]); di("scanmask", [128, 2 * T])
        do("yT", [1024, self.n_own * T])
        do("pst", [128, 2, 32])
        do("pkv", [3, self.n_own * T, 512])
        if self.sample:
            di("xsT", [1024, 4]); di("h0", [128, 2, 32, 4])
            di("c0", [4, 128, 512]); di("c1", [4, 512, 512]); di("c2", [4, 2048, 512])
            di("pens", [128, 12]); di("sel4", [4, 4, 128]); di("selc", [128, 4, 4])
            do("ysT", [1024, 4]); do("sst", [128, 2, 32, 4])
            do("skv0", [4, 128, 512]); do("skv1", [4, 512, 512]); do("skv2", [4, 2048, 512])
        for name, shape in self.dbg:
            do(name, shape)
        self.bt_d = nc.dram_tensor("bt_scr", [8, 128, 4, 2, 128], BF16, kind="Internal").ap()

        with contextlib.ExitStack() as st:
            self.st = st
            self.S = Sched(nc, st)
            self.alloc()
            self.setup()
            if self.pipe_halo and self.n_halo > 0:
                self.halo_phase()
                for ti in range(self.n_halo, NT):
                    self.tile(ti)
            else:
                for ti in range(NT):
                    self.tile(ti)
            if self.sample:
                self.sample_pass()
            self.S.finish()
            self.S.emit()
        return nc

    def alloc(self):
        sb = self.sb
        nc = self.nc
        NT = self.NT
        self.ps = self.st.enter_context(nc.psum_tensor("ps", [128, 8, 512], F32))
        self.xT = sb("xT_s", [128, 8, T], F32)
        self.hT = sb("hT_s", [128, 8, T], BF16)
        self.wA = [sb("wA%d" % i, [128, 8, 256], BF16) for i in range(4)]
        self.wD = [sb("wD%d" % i, [128, 2, 512], BF16) for i in range(3)]
        self.wKV = sb("wKV", [128, 8, 512], BF16)
        self.kT = [sb("kT0", [128, 2, 2 * T], BF16), sb("kT1", [128, 2, 2 * T], BF16), sb("kT2", [128, 2, ((NT + 3) // 4) * 4 * T], BF16)]
        self.V = [sb("V0", [128, 8, 4, 66], BF16), sb("V1", [128, 8, 4, 66], BF16), sb("V2", [128, 32, 4, 66], BF16)]
        self.tab = sb("tab_s", [128, 4, 32, TC], F32)
        self.apw = sb("apw_s", [128, 10, 2, 32], F32)
        self.ainv = sb("ainv", [128, 2, 32], F32)
        self.Kc = sb("Kcarry", [128, 2, 32], F32)
        self.vecs = sb("vecs_s", [128, 40], F32)
        self.gain_bc = sb("gain_bc_s", [128, 2, 256], F32)
        self.ident = sb("ident_b", [128, 128], BF16)
        self.identf = sb("ident_f", [128, 128], F32)
        self.onesblk = sb("onesblk_b", [128, 128], BF16)
        self.ones = sb("ones_b", [128, 128], BF16)
        self.ones1 = sb("ones1_f", [128, 64], F32)
        self.dm = sb("dm_s", [128, 3, 128], F32)
        self.scanmask = sb("scanmask_s", [128, 2 * T], F32)
        self.diagD = sb("diagD", [128, 8, 128], BF16)
        self.eps = sb("eps_s", [128, 1], F32)
        self.one_c = sb("one_c_s", [128, 1], F32)
        self.negpi = sb("negpi_s", [128, 1], F32)
        self.rstd = sb("rstd_s", [128, T], F32)
        self.sqb = sb("sqb_s", [128, T], BF16)
        SCR = 18304
        self.scr = sb("scr", [128, SCR], F32)

    def carve(self):
        scr = self.scr
        off = [0]

        def f32(n):
            v = scr[:, off[0]:off[0] + n]
            off[0] += n
            return v

        def b16(n):
            w = (n + 1) // 2
            v = scr[:, off[0]:off[0] + w].bitcast(BF16)
            off[0] += w
            return v
        return f32, b16, off

    def setup(self):
        S, nc = self.S, self.nc
        d = self.din
        S.dma("sp", self.vecs[:, :], d["vecs"], writes=["vecs"])
        S.dma("sp", self.gain_bc[:, :, :], d["gain_bc"], writes=["gain_bc"])
        S.dma("sp", self.identf[:, :], d["ident"], writes=["identf"])
        S.dma("sp", self.dm[:, :, :], d["dm"], writes=["dm"])
        S.dma("sp", self.scanmask[:, :], d["scanmask"], writes=["scanmask"])
        S.dma("pool", self.ident[:, :], d["ident"], writes=["ident"])
        S.dma("pool", self.onesblk[:, :], d["onesblk"], writes=["onesblk"])
        S.op("pool", lambda e: e.memset(self.ones[:, :], 1.0 / 1024.0), writes=["ones"])
        S.op("pool", lambda e: e.memset(self.ones1[:, :], 1.0), writes=["ones1"])
        S.op("pool", lambda e: e.memset(self.eps[:, :], 1e-6), writes=["eps"])
        S.op("pool", lambda e: e.memset(self.one_c[:, :], 1.0), writes=["one_c"])
        S.op("pool", lambda e: e.memset(self.negpi[:, :], -math.pi), writes=["negpi"])
        S.op("pool", lambda e: e.memset(self.Kc[:, :, :], 0.0), writes=["Kc"])
        for g in range(3):
            S.op("pool", lambda e, g=g: e.memset(self.kT[g][:, :, :], 0.0), writes=[("kT", g)])
            S.op("pool", lambda e, g=g: e.memset(self.V[g][:, :, :, :], 0.0), writes=[("V", g)])
            S.op("pool", lambda e, g=g: e.memset(self.V[g][:, :, :, 64:65], 1.0), writes=[("V", g)])
        for c in range(8):
            S.op("dve", lambda e, c=c: e.tensor_scalar(out=self.diagD[:, c, :], in0=self.identf[:, :], scalar1=self.vecs[:, 24 + c:25 + c],
                                                       scalar2=None, op0=ALU.mult),
                 reads=["identf", "vecs"], writes=["diagD"])
        if self.pipe_halo and self.n_halo > 0:
            rec = []
            real = self.S

            class _Rec:
                def op(self_, *a, **k):
                    rec.append(("op", a, k))

                def dma(self_, *a, **k):
                    rec.append(("dma", a, k))

                def barrier(self_):
                    rec.append(("barrier", (), {}))
            self.S = _Rec()
            self.ssm_setup()
            self.S = real
            self.setup_rec = rec
        else:
            self.ssm_setup()

    def ssm_setup(self):
        S = self.S
        d = self.din
        f32, b16, off = self.carve()
        off[0] = 4096 if (self.pipe_halo and self.n_halo > 0) else 0
        sc = f32(96).rearrange("p (a b) -> p a b", a=3)
        Bm = f32(2 * 32 * 16).rearrange("p (r a c) -> p r a c", r=2, a=32)
        tmp = [f32(32) for _ in range(12)]
        Bb = f32(2 * 32 * 16).rearrange("p (r a c) -> p r a c", r=2, a=32)
        big = [f32(32 * 16) for _ in range(2)]
        xpad = [f32(2 * 128).rearrange("p (r c) -> p r c", r=2) for _ in range(4)]
        btb = [b16(4 * 2 * 128).rearrange("p (a r c) -> p a r c", a=4, r=2) for _ in range(2)]
        S.dma("sp", sc, d["ssm_sc"], writes=["sc"])
        S.dma("sp", Bm, d["ssm_B"], writes=["Bm"])
        lr, li, dt, mag, ang, cs, sn, a_re, a_im, t0, t1, t2 = tmp
        V = lambda fn, r, w: S.op("dve", fn, reads=r, writes=w)
        A = lambda fn, r, w: S.op("act", fn, reads=r, writes=w)
        TT = lambda o, a, b, op: (lambda e: e.tensor_tensor(out=o, in0=a, in1=b, op=op))
        V(lambda e: e.tensor_scalar_min(out=lr, in0=sc[:, 0, :], scalar1=-1e-4), ["sc"], ["lr"])
        V(lambda e: e.tensor_copy(out=li, in_=sc[:, 1, :]), ["sc"], ["li"])
        A(lambda e: e.activation(out=dt, in_=sc[:, 2, :], func=AF.Exp), ["sc"], ["dt"])
        V(TT(mag, lr, dt, ALU.mult), ["lr", "dt"], ["mag"])
        A(lambda e: e.activation(out=mag, in_=mag, func=AF.Exp), ["mag"], ["mag"])
        V(TT(ang, li, dt, ALU.mult), ["li", "dt"], ["ang"])
        ki = f32(32).bitcast(mybir.dt.int32)
        kf = f32(32)
        for dst, shift, tg in ((sn, 0.0, "sn"), (cs, 0.5 * math.pi, "cs")):
            V(lambda e, dst=dst, shift=shift: e.tensor_scalar(out=dst, in0=ang, scalar1=shift, scalar2=None, op0=ALU.add), ["ang"], [tg])
            V(lambda e, dst=dst: e.tensor_scalar(out=kf, in0=dst, scalar1=1.0 / (2.0 * math.pi), scalar2=None, op0=ALU.mult), [tg], ["kf"])
            V(lambda e: e.tensor_copy(out=ki, in_=kf), ["kf"], ["ki"])
            V(lambda e: e.tensor_copy(out=kf, in_=ki), ["ki"], ["kf"])
            V(lambda e, dst=dst: e.scalar_tensor_tensor(out=dst, in0=kf, scalar=-2.0 * math.pi, in1=dst, op0=ALU.mult, op1=ALU.add), ["kf", tg], [tg])
            V(lambda e, dst=dst: e.tensor_scalar(out=kf, in0=dst, scalar1=math.pi, scalar2=-2.0 * math.pi, op0=ALU.is_gt, op1=ALU.mult), [tg], ["kf"])
            V(TT(dst, dst, kf, ALU.add), [tg, "kf"], [tg])
            V(lambda e, dst=dst: e.tensor_scalar(out=kf, in0=dst, scalar1=-math.pi, scalar2=2.0 * math.pi, op0=ALU.is_lt, op1=ALU.mult), [tg], ["kf"])
            V(TT(dst, dst, kf, ALU.add), [tg, "kf"], [tg])
            A(lambda e, dst=dst: e.activation(out=dst, in_=dst, func=AF.Sin), [tg], [tg])
        V(TT(a_re, mag, cs, ALU.mult), ["mag", "cs"], ["a_re"])
        V(TT(a_im, mag, sn, ALU.mult), ["mag", "sn"], ["a_im"])
        nr, den, fr, fi = cs, sn, mag, ang
        V(lambda e: e.tensor_scalar_add(out=nr, in0=a_re, scalar1=-1.0), ["a_re", "cs"], ["nr"])
        V(TT(t0, lr, lr, ALU.mult), ["lr"], ["t0"])
        V(TT(t1, li, li, ALU.mult), ["li"], ["t1"])
        V(TT(den, t0, t1, ALU.add), ["t0", "t1", "sn"], ["den"])
        V(lambda e: e.reciprocal(out=den, in_=den), ["den"], ["den"])
        V(TT(t0, nr, lr, ALU.mult), ["nr", "lr"], ["t0"])
        V(TT(t1, a_im, li, ALU.mult), ["a_im", "li"], ["t1"])
        V(TT(t0, t0, t1, ALU.add), ["t0", "t1"], ["t0"])
        V(TT(fr, t0, den, ALU.mult), ["t0", "den", "mag"], ["fr"])
        V(TT(t0, a_im, lr, ALU.mult), ["a_im", "lr"], ["t0"])
        V(TT(t1, nr, li, ALU.mult), ["nr", "li"], ["t1"])
        V(TT(t0, t0, t1, ALU.subtract), ["t0", "t1"], ["t0"])
        V(TT(fi, t0, den, ALU.mult), ["t0", "den", "ang"], ["fi"])
        frb = fr.unsqueeze(2).to_broadcast([128, 32, 16])
        fib = fi.unsqueeze(2).to_broadcast([128, 32, 16])
        b0 = big[0].rearrange("p (a c) -> p a c", a=32)
        b1 = big[1].rearrange("p (a c) -> p a c", a=32)
        V(TT(b0, Bm[:, 0, :, :], frb, ALU.mult), ["Bm", "fr"], ["b0"])
        V(TT(b1, Bm[:, 1, :, :], fib, ALU.mult), ["Bm", "fi"], ["b1"])
        V(TT(Bb[:, 0, :, :], b0, b1, ALU.subtract), ["b0", "b1"], ["Bb"])
        V(TT(b0, Bm[:, 1, :, :], frb, ALU.mult), ["Bm", "fr"], ["b0"])
        V(TT(b1, Bm[:, 0, :, :], fib, ALU.mult), ["Bm", "fi"], ["b1"])
        V(TT(Bb[:, 1, :, :], b0, b1, ALU.add), ["b0", "b1"], ["Bb"])
        for qq in range(4):
            S.op("pool", lambda e, qq=qq: e.memset(xpad[qq], 0.0), writes=[("xpad", qq)])
        for P in range(32):
            ch, qq = P // 4, P % 4
            xp = xpad[qq]
            for ri in range(2):
                for g2 in range(2):
                    V(lambda e, xp=xp, ri=ri, g2=g2, P=P, qq=qq: e.tensor_copy(
                        out=xp[g2 * 64:(g2 + 1) * 64, ri, 32 * qq + 16 * g2: 32 * qq + 16 * g2 + 16],
                        in_=Bb[g2 * 64:(g2 + 1) * 64, ri, P, :]), ["Bb"], [("xpad", qq)])
            bb = btb[ch % 2]
            for ri in range(2):
                pb = 6 + (P * 2 + ri) % 2
                S.op("pe", lambda e, xp=xp, ri=ri, pb=pb: e.matmul(self.ps[:, pb, 0:128], lhsT=xp[:, ri, :], rhs=self.identf[:, :],
                                                                     start=True, stop=True),
                     reads=[("xpad", qq), "identf"], writes=[("ps", pb)])
                A(lambda e, bb=bb, qq=qq, ri=ri, pb=pb: e.copy(out=bb[:, qq, ri, :], in_=self.ps[:, pb, 0:128]),
                  [("ps", pb)], [("btb", ch % 2)])
            if qq == 3:
                S.dma("sp", self.bt_d[ch], bb, reads=[("btb", ch % 2)], writes=["bt_d"])
        apw = self.apw
        V(lambda e: e.tensor_copy(out=apw[:, 0, 0, :], in_=a_re), ["a_re"], ["apw"])
        V(lambda e: e.tensor_copy(out=apw[:, 0, 1, :], in_=a_im), ["a_im"], ["apw"])
        for s in range(1, 10):
            pr, pi_, nr_, ni_ = apw[:, s - 1, 0, :], apw[:, s - 1, 1, :], apw[:, s, 0, :], apw[:, s, 1, :]
            V(TT(t0, pr, pr, ALU.mult), ["apw"], ["t0"])
            V(TT(t1, pi_, pi_, ALU.mult), ["apw"], ["t1"])
            V(TT(nr_, t0, t1, ALU.subtract), ["t0", "t1"], ["apw"])
            V(TT(t0, pr, pi_, ALU.mult), ["apw"], ["t0"])
            V(lambda e, ni_=ni_: e.tensor_scalar(out=ni_, in0=t0, scalar1=2.0, scalar2=None, op0=ALU.mult), ["t0"], ["apw"])
        V(TT(t0, a_re, a_re, ALU.mult), ["a_re"], ["t0"])
        V(TT(t1, a_im, a_im, ALU.mult), ["a_im"], ["t1"])
        V(TT(t0, t0, t1, ALU.add), ["t0", "t1"], ["t0"])
        V(lambda e: e.reciprocal(out=t0, in_=t0), ["t0"], ["t0"])
        V(TT(self.ainv[:, 0, :], a_re, t0, ALU.mult), ["a_re", "t0"], ["ainv"])
        V(lambda e: e.scalar_tensor_tensor(out=self.ainv[:, 1, :], in0=a_im, scalar=-1.0, in1=t0, op0=ALU.mult, op1=ALU.mult),
          ["a_im", "t0"], ["ainv"])
        tab = self.tab
        ivr, ivi = lr, li
        V(lambda e: e.tensor_copy(out=ivr, in_=self.ainv[:, 0, :]), ["ainv", "lr"], ["ivr"])
        V(lambda e: e.tensor_copy(out=ivi, in_=self.ainv[:, 1, :]), ["ainv", "li"], ["ivi"])
        for base, getp in ((0, lambda s: (apw[:, s, 0, :], apw[:, s, 1, :])), (2, None)):
            S.op("pool", lambda e, base=base: e.memset(tab[:, base, :, 0:1], 1.0), writes=["tab"])
            S.op("pool", lambda e, base=base: e.memset(tab[:, base + 1, :, 0:1], 0.0), writes=["tab"])
            s = 0
            n = 1
            while n < TC:
                if getp is not None:
                    mr, mi = getp(s)
                else:
                    mr, mi = ivr, ivi
                mrb = mr.unsqueeze(2).to_broadcast([128, 32, n])
                mib = mi.unsqueeze(2).to_broadcast([128, 32, n])
                sr, si = tab[:, base, :, 0:n], tab[:, base + 1, :, 0:n]
                orr, oi = tab[:, base, :, n:2 * n], tab[:, base + 1, :, n:2 * n]
                bb0 = big[0].rearrange("p (a c) -> p a c", a=32)[:, :, 0:n]
                bb1 = big[1].rearrange("p (a c) -> p a c", a=32)[:, :, 0:n]
                V(TT(bb0, sr, mrb, ALU.mult), ["tab", "apw", "ivr"], ["b0"])
                V(TT(bb1, si, mib, ALU.mult), ["tab", "apw", "ivi"], ["b1"])
                V(TT(orr, bb0, bb1, ALU.subtract), ["b0", "b1"], ["tab"])
                V(TT(bb0, sr, mib, ALU.mult), ["tab", "apw", "ivi"], ["b0"])
                V(TT(bb1, si, mrb, ALU.mult), ["tab", "apw", "ivr"], ["b1"])
                V(TT(oi, bb0, bb1, ALU.add), ["b0", "b1"], ["tab"])
                if getp is None:
                    V(TT(t0, ivr, ivr, ALU.mult), ["ivr"], ["t0"])
                    V(TT(t1, ivi, ivi, ALU.mult), ["ivi"], ["t1"])
                    V(TT(t2, ivr, ivi, ALU.mult), ["ivr", "ivi"], ["t2"])
                    V(TT(ivr, t0, t1, ALU.subtract), ["t0", "t1"], ["ivr"])
                    V(lambda e: e.tensor_scalar(out=ivi, in0=t2, scalar1=2.0, scalar2=None, op0=ALU.mult), ["t2"], ["ivi"])
                n *= 2
                s += 1
        if self.has_dbg("apw0"):
            S.dma("sp", self.dout["apw0"], self.apw[:, 0, :, :], reads=["apw"])
        if self.has_dbg("tab"):
            S.dma("sp", self.dout["tab"], self.tab[:, :, :, :], reads=["tab"])
        if self.has_dbg("btd"):
            S.dma("pool", self.dout["btd"], self.bt_d, reads=["bt_d"])
        S.barrier()

    def has_dbg(self, name):
        return any(nm == name for nm, _ in self.dbg)

    def rmsnorm(self, col0, Tn):
        S = self.S
        xT, hT = self.xT, self.hT
        for c in range(8):
            S.op("act", lambda e, c=c: e.activation(out=hT[:, c, :Tn], in_=xT[:, c, :Tn], func=AF.Square),
                 reads=[("xT", c)], writes=[("hT", c)])
        for c in range(8):
            S.op("pe", lambda e, c=c: e.matmul(self.ps[:, 0, :Tn], lhsT=self.ones[:, :], rhs=hT[:, c, :Tn], start=(c == 0), stop=(c == 7)),
                 reads=[("hT", c), "ones"], writes=[("ps", 0)])
        S.op("act", lambda e: e.activation(out=self.rstd[:, :Tn], in_=self.ps[:, 0, :Tn], func=AF.Sqrt, bias=self.eps[:, 0:1]),
             reads=[("ps", 0), "eps"], writes=["rstd"])
        S.op("dve", lambda e: e.reciprocal(out=self.rstd[:, :Tn], in_=self.rstd[:, :Tn]), reads=["rstd"], writes=["rstd"])
        for c in range(8):
            S.op("dve", lambda e, c=c: e.scalar_tensor_tensor(out=hT[:, c, :Tn], in0=xT[:, c, :Tn], scalar=self.vecs[:, col0 + c:col0 + c + 1],
                                                           in1=self.rstd[:, :Tn], op0=ALU.mult, op1=ALU.mult),
                 reads=[("xT", c), "rstd", "vecs"], writes=[("hT", c)])

    def ffn_g(self, nm, Tn, hid, sg, dbase=4):
        S = self.S
        d = self.din
        xT, hT, ps = self.xT, self.hT, self.ps
        wg_d, wu_d, wd_d = d["wg" + nm], d["wu" + nm], d["wd" + nm]
        for pc in range(NPF):
            bg, bu = (0, 1) if pc % 2 == 0 else (2, 0)
            bg = (2 * pc) % 4
            bu = (2 * pc + 1) % 4
            S.dma("pool", self.wA[bg][:, :, :], wg_d[pc], writes=[("wA", bg)])
            S.dma("pool", self.wA[bu][:, :, :], wu_d[pc], writes=[("wA", bu)])
            for j in range(2):
                hc = 2 * pc + j
                pb = (hc % 2) * 2
                for k in range(8):
                    S.op("pe", lambda e, k=k, bg=bg, j=j, pb=pb: e.matmul(ps[:, pb, :Tn], lhsT=self.wA[bg][:, k, j * 128:(j + 1) * 128],
                                                                         rhs=hT[:, k, :Tn], start=(k == 0), stop=(k == 7)),
                         reads=[("wA", bg), ("hT", k)], writes=[("ps", pb)])
                for k in range(8):
                    S.op("pe", lambda e, k=k, bu=bu, j=j, pb=pb: e.matmul(ps[:, pb + 1, :Tn], lhsT=self.wA[bu][:, k, j * 128:(j + 1) * 128],
                                                                         rhs=hT[:, k, :Tn], start=(k == 0), stop=(k == 7)),
                         reads=[("wA", bu), ("hT", k)], writes=[("ps", pb + 1)])
                sbi = hc % 2
                S.op("act", lambda e, pb=pb, sbi=sbi: e.activation(out=sg[sbi][:, :Tn], in_=ps[:, pb, :Tn], func=AF.Silu),
                     reads=[("ps", pb)], writes=[("sg", sbi)])
                S.op("dve", lambda e, pb=pb, sbi=sbi, hc=hc: e.tensor_tensor(out=hid[:, hc, :Tn], in0=sg[sbi][:, :Tn], in1=ps[:, pb + 1, :Tn], op=ALU.mult),
                     reads=[("ps", pb + 1), ("sg", sbi)], writes=[("hid", hc)])
            yield
        for half in range(2):
            for pc in range(NPF):
                b = (half * NPF + pc) % 7
                wdb = self.wD[b] if b < 3 else self.wKV[:, 2 * (b - 3):2 * (b - 3) + 2, :]
                S.dma("pool", wdb[:, :, :], wd_d[half, pc], writes=[("wD", b)])
                for j in range(2):
                    hc = 2 * pc + j
                    for o in range(4):
                        S.op("pe", lambda e, wdb=wdb, j=j, o=o, hc=hc: e.matmul(ps[:, dbase + o, :Tn], lhsT=wdb[:, j, o * 128:(o + 1) * 128],
                                                                           rhs=hid[:, hc, :Tn], start=(hc == 0), stop=(hc == 21)),
                             reads=[("wD", b), ("hid", hc)], writes=[("ps", dbase + o)])
                yield
            for o in range(4):
                c = half * 4 + o
                S.op("dve", lambda e, o=o, c=c: e.scalar_tensor_tensor(out=xT[:, c, :Tn], in0=ps[:, dbase + o, :Tn], scalar=0.5, in1=xT[:, c, :Tn],
                                                                     op0=ALU.mult, op1=ALU.add),
                     reads=[("ps", dbase + o), ("xT", c)], writes=[("xT", c)])

    def ffn(self, *a, **k):
        for _ in self.ffn_g(*a, **k):
            pass

    def ffn_block_g(self, nm, col0, Tn, base=0, dbase=4):
        f32, b16, off = self.carve()
        off[0] = base
        hid = b16(22 * T).rearrange("p (c t) -> p c t", c=22)
        sg = [f32(T), f32(T)]
        self.rmsnorm(col0, Tn)
        yield
        yield from self.ffn_g(nm, Tn, hid, sg, dbase=dbase)
        self.S.barrier()

    def ffn_block(self, *a, **k):
        for _ in self.ffn_block_g(*a, **k):
            pass

    def linear_g(self, w_d, npieces, in_buf, in_tag, nk, Tn, consumer, banks=(0, 1, 2, 3), wtag="wA", pieces=None):
        S = self.S
        for pc in range(npieces):
            if pieces is not None and pc not in pieces:
                continue
            b = self._wrot
            self._wrot = (self._wrot + 1) % 4
            S.dma("pool", self.wA[b][:, 0:nk, :], w_d[pc], writes=[("wA", b)])
            for j in range(2):
                oc = 2 * pc + j
                bank = banks[oc % len(banks)]
                for k in range(nk):
                    S.op("pe", lambda e, k=k, b=b, j=j, bank=bank: e.matmul(self.ps[:, bank, :Tn], lhsT=self.wA[b][:, k, j * 128:(j + 1) * 128],
                                                                           rhs=in_buf[:, k, :Tn], start=(k == 0), stop=(k == nk - 1)),
                         reads=[("wA", b), (in_tag, k)], writes=[("ps", bank)])
                consumer(oc, bank)
            yield

    def linear(self, *a, **k):
        for _ in self.linear_g(*a, **k):
            pass

    _wrot = 0

    def qknorm(self, bank, Tn, gcol, out_ap, out_tag, nb=7, bufs=None):
        S = self.S
        ps = self.ps
        if bufs is None:
            sqb, rstd, tg = self.sqb, self.rstd, ""
        else:
            i = self._qkrot % len(bufs)
            self._qkrot += 1
            sqb, rstd, nb = bufs[i]
            tg = "_%d" % i
        S.op("act", lambda e: e.activation(out=sqb[:, :Tn], in_=ps[:, bank, :Tn], func=AF.Square),
             reads=[("ps", bank)], writes=["sqb" + tg])
        S.op("pe", lambda e: e.matmul(ps[:, nb, :Tn], lhsT=self.onesblk[:, :], rhs=sqb[:, :Tn], start=True, stop=True),
             reads=["sqb" + tg, "onesblk"], writes=[("ps", nb)])
        S.op("act", lambda e: e.activation(out=rstd[:, :Tn], in_=ps[:, nb, :Tn], func=AF.Sqrt, bias=self.eps[:, 0:1]),
             reads=[("ps", nb), "eps"], writes=["rstd" + tg])
        S.op("dve", lambda e: e.reciprocal(out=rstd[:, :Tn], in_=rstd[:, :Tn]), reads=["rstd" + tg], writes=["rstd" + tg])
        S.op("dve", lambda e: e.scalar_tensor_tensor(out=out_ap, in0=ps[:, bank, :Tn], scalar=self.vecs[:, gcol:gcol + 1], in1=rstd[:, :Tn],
                                                     op0=ALU.mult, op1=ALU.mult),
             reads=[("ps", bank), "rstd" + tg, "vecs"], writes=[out_tag])

    _qkrot = 0

    def tile(self, ti):
        S = self.S
        d = self.din
        own = ti >= self.n_halo
        oi = ti - self.n_halo
        last = ti == self.NT - 1
        xT = self.xT
        if ti not in self.pre_ffn1:
            S.dma("sp", xT[:, :, :], d["xT"][:, ti * T:(ti + 1) * T].rearrange("(c p) t -> p c t", p=128),
                  writes=[("xT", c) for c in range(8)])
            if self.level < 1:
                return
            self.ffn_block("1", 0, T)
        if self.level < 2:
            return
        self.dbg_dump("x1T", ti, lambda dd: S.dma("sp", dd.rearrange("(c p) t -> p c t", p=128), xT[:, :, :], reads=[("xT", c) for c in range(8)]))
        f32, b16, off = self.carve()
        self.rmsnorm(8, T)
        u_bf = b16(8 * T).rearrange("p (c t) -> p c t", c=8)
        qT = b16(6 * T).rearrange("p (c t) -> p c t", c=6)
        def cons_u(oc, bank):
            S.op("act", lambda e: e.copy(out=u_bf[:, oc, :], in_=self.ps[:, bank, :]), reads=[("ps", bank)], writes=[("u", oc)])
        self.linear(d["w_u"], 4, self.hT, "hT", 8, T, cons_u)
        need_g = [g for g in range(3) if own or (self.n_halo - ti - 1) * T + 1 <= WIN[g]]
        self.need_g = need_g
        top = self.scr.shape[1] - 3 * (T + T // 2)
        qkb = []
        for i in range(3):
            o_ = top + i * (T + T // 2)
            qkb.append((self.scr[:, o_ + T:o_ + T + T // 2].bitcast(BF16), self.scr[:, o_:o_ + T], 5 + i))

        def cons_k(oc, bank):
            g, hh = oc // 2, oc % 2
            if g not in need_g:
                return
            if g < 2:
                dst = self.kT[g][:, hh, (ti % 2) * T:(ti % 2 + 1) * T]
            else:
                dst = self.kT[2][:, hh, ti * T:(ti + 1) * T]
            self.qknorm(bank, T, 33, dst, ("kT", g), bufs=qkb)
        self.linear(d["w_k"], 3, self.hT, "hT", 8, T, cons_k, pieces=need_g)
        if own:
            def cons_q(oc, bank):
                self.qknorm(bank, T, 32, qT[:, oc, :], ("qT", oc), bufs=qkb)
            self.linear(d["w_q"], 3, self.hT, "hT", 8, T, cons_q)
        ysm = b16(8 * T).rearrange("p (c t) -> p c t", c=8)
        self.ysm = ysm
        obT = b16(2 * T).rearrange("p (c t) -> p c t", c=2)
        mark = off[0]
        if self.level < 3:
            S.barrier(); return
        self.kv_tokmajor(ti, f32, b16)
        S.barrier()
        off[0] = mark
        if self.level < 4:
            return
        self.ssm_tile(ti, u_bf, f32, b16, own)
        if self.has_dbg("ysm_%d" % ti):
            S.dma("pool", self.dout["ysm_%d" % ti].rearrange("(c p) t -> p c t", p=128), ysm, reads=[("ysm", c) for c in range(8)])
        if self.has_dbg("u_%d" % ti):
            S.dma("pool", self.dout["u_%d" % ti].rearrange("(c p) t -> p c t", p=128), u_bf, reads=[("u", c) for c in range(8)])
        if self.level < 5:
            S.barrier(); return
        if own:
            S.barrier()
            off[0] = mark
            self.attention(ti, qT, obT, f32, b16)
            if self.has_dbg("obT_%d" % ti):
                S.dma("pool", self.dout["obT_%d" % ti].rearrange("(c p) t -> p c t", p=128), obT, reads=[("obT", c) for c in range(2)])
            if self.level < 6:
                S.barrier(); return
            self.mix_out(ti, obT, f32, b16)
        S.barrier()
        if own:
            self.ffn_block("2", 16, T)
            S.dma("sp", self.dout["yT"][:, oi * T:(oi + 1) * T].rearrange("(c p) t -> p c t", p=128), xT[:, :, :],
                  reads=[("xT", c) for c in range(8)])
        if last:
            f32, b16, off = self.carve()
            hr, hi_, t0, t1 = f32(32), f32(32), f32(32), f32(32)
            self.cmul_small(hr, hi_, self.Kc[:, 0, :], self.Kc[:, 1, :], self.ainv[:, 0, :], self.ainv[:, 1, :], t0, t1, "Kc", "ainv", "pstv")
            S.dma("sp", self.dout["pst"][:, 0, :], hr, reads=["pstv"])
            S.dma("sp", self.dout["pst"][:, 1, :], hi_, reads=["pstv"])
            S.barrier()

    HALO_ABASE = 4096 + 6448 + 816

    def haloA_g(self, ti, u_bf, par):
        S = self.S
        d = self.din
        abase = self.HALO_ABASE
        S.barrier()
        S.dma("sp", self.xT[:, :, :], d["xT"][:, ti * T:(ti + 1) * T].rearrange("(c p) t -> p c t", p=128),
              writes=[("xT", c) for c in range(8)])
        yield from self.ffn_block_g("1", 0, T, base=abase, dbase=0)
        self.rmsnorm(8, T)
        yield

        def cons_u(oc, bank):
            S.op("act", lambda e: e.copy(out=u_bf[:, oc, :], in_=self.ps[:, bank, :]), reads=[("ps", bank)], writes=[("u", par, oc)])
        yield from self.linear_g(d["w_u"], 4, self.hT, "hT", 8, T, cons_u, banks=(0, 1, 2))
        need_g = [g for g in range(3) if (self.n_halo - ti - 1) * T + 1 <= WIN[g]]
        self.need_g = need_g

        def cons_k(oc, bank):
            g, hh = oc // 2, oc % 2
            if g < 2:
                dst = self.kT[g][:, hh, (ti % 2) * T:(ti % 2 + 1) * T]
            else:
                dst = self.kT[2][:, hh, ti * T:(ti + 1) * T]
            self.qknorm(bank, T, 33, dst, ("kT", g), nb=3)
        yield from self.linear_g(d["w_k"], 3, self.hT, "hT", 8, T, cons_k, banks=(0, 1, 2), pieces=need_g)
        f32, b16, off = self.carve()
        off[0] = abase
        yield from self.kv_tokmajor_g(ti, f32, b16, b0_fixed=0)

    def ownpre_g(self, ti):
        S = self.S
        S.barrier()
        S.dma("sp", self.xT[:, :, :], self.din["xT"][:, ti * T:(ti + 1) * T].rearrange("(c p) t -> p c t", p=128),
              writes=[("xT", c) for c in range(8)])
        yield from self.ffn_block_g("1", 0, T, base=self.HALO_ABASE, dbase=0)

    def ssm_halo_g(self, ti, u_bf, par):
        S = self.S
        ps = self.ps
        tab = self.tab
        TT = lambda o, a, b, op: (lambda e: e.tensor_tensor(out=o, in0=a, in1=b, op=op))
        V = lambda fn, r, w: S.op("dve", fn, reads=r, writes=w)
        f32, b16, off = self.carve()
        off[0] = 4096
        T2 = 2 * T
        btb = b16(4 * 2 * 128).rearrange("p (a r c) -> p a r c", a=4, r=2)
        X = [f32(T2), f32(T2)]
        ta, tb = f32(T2), f32(T2)
        G1 = [b16(T2), b16(T2)]
        G = [G1] * 8
        NZ = NCH + 1
        Z = [[f32(16 * NZ).rearrange("p (a c) -> p a c", a=16) for _ in range(2)] for _ in range(2)]
        zt = [f32(16 * NZ).rearrange("p (a c) -> p a c", a=16) for _ in range(2)]
        assert off[0] <= self.HALO_ABASE, off[0]
        v4 = lambda a: a.rearrange("p (j c l) -> p j c l", j=2, l=TC)
        for cp in range(2):
            for sub in range(4):
                ch = 4 * cp + sub
                S.dma("sp", btb, self.bt_d[ch], reads=["bt_d"], writes=["h_btb"])
                for bt in range(2):
                    b4 = 2 * sub + bt
                    P0 = 4 * ch + 2 * bt
                    for j in range(2):
                        qq = 2 * bt + j
                        for ri in range(2):
                            bank = 4 + 2 * j + ri
                            S.op("pe", lambda e, qq=qq, ri=ri, bank=bank, ch=ch: e.matmul(ps[:, bank, :], lhsT=btb[:, qq, ri, :], rhs=u_bf[:, ch, :], start=True, stop=True),
                                 reads=["h_btb", ("u", par, ch)], writes=[("ps", bank)])
                            S.op("act", lambda e, ri=ri, j=j, bank=bank: e.copy(out=X[ri][:, j * T:(j + 1) * T], in_=ps[:, bank, :]),
                                 reads=[("ps", bank)], writes=[("hX", ri)])
                    ivr = tab[:, 2, P0:P0 + 2, :].unsqueeze(2).to_broadcast([128, 2, NCH, TC])
                    ivi = tab[:, 3, P0:P0 + 2, :].unsqueeze(2).to_broadcast([128, 2, NCH, TC])
                    gr, gi = G[b4]
                    V(TT(v4(ta), v4(X[0]), ivr, ALU.mult), [("hX", 0), "tab"], ["h_ta"])
                    V(TT(v4(tb), v4(X[1]), ivi, ALU.mult), [("hX", 1), "tab"], ["h_tb"])
                    V(TT(ta, ta, tb, ALU.subtract), ["h_ta", "h_tb"], ["h_ta"])
                    V(lambda e, gr=gr: e.tensor_tensor_scan(out=gr, data0=self.scanmask[:, :], data1=ta, initial=0.0, op0=ALU.mult, op1=ALU.add),
                      ["h_ta", "scanmask"], [("hG", 0)])
                    V(TT(v4(ta), v4(X[1]), ivr, ALU.mult), [("hX", 1), "tab"], ["h_ta"])
                    V(TT(v4(tb), v4(X[0]), ivi, ALU.mult), [("hX", 0), "tab"], ["h_tb"])
                    V(TT(ta, ta, tb, ALU.add), ["h_ta", "h_tb"], ["h_ta"])
                    V(lambda e, gi=gi: e.tensor_tensor_scan(out=gi, data0=self.scanmask[:, :], data1=ta, initial=0.0, op0=ALU.mult, op1=ALU.add),
                      ["h_ta", "scanmask"], [("hG", 1)])
                    for ri in range(2):
                        ge = G[b4][ri].rearrange("p (j c l) -> p j c l", j=2, l=TC)[:, :, :, TC - 1]
                        V(lambda e, b4=b4, ri=ri, ge=ge: e.tensor_copy(out=zt[ri][:, 2 * b4:2 * b4 + 2, 1:NZ], in_=ge), [("hG", ri)], ["h_zt"])
                    yield
            Pa = slice(16 * cp, 16 * cp + 16)
            s0 = int(math.log2(TC))
            zr, zi = Z[0]
            V(lambda e, Pa=Pa: e.tensor_copy(out=zr[:, :, 0:1], in_=self.Kc[:, 0, Pa].unsqueeze(2)), ["Kc"], ["hZ0"])
            V(lambda e, Pa=Pa: e.tensor_copy(out=zi[:, :, 0:1], in_=self.Kc[:, 1, Pa].unsqueeze(2)), ["Kc"], ["hZ0"])
            self.cmul_b(zr[:, :, 1:NZ], zi[:, :, 1:NZ], zt[0][:, :, 1:NZ], zt[1][:, :, 1:NZ], s0, Pa, NZ - 1, zt, "h_zt", "hZ0", add=None)
            yield
            cur = 0
            st = 1
            sidx = s0
            while st < NZ:
                a, b = Z[cur], Z[1 - cur]
                n = NZ - st
                V(lambda e, a=a, b=b, st=st: e.tensor_copy(out=b[0][:, :, 0:st], in_=a[0][:, :, 0:st]), ["hZ%d" % cur], ["hZ%d" % (1 - cur)])
                V(lambda e, a=a, b=b, st=st: e.tensor_copy(out=b[1][:, :, 0:st], in_=a[1][:, :, 0:st]), ["hZ%d" % cur], ["hZ%d" % (1 - cur)])
                self.cmul_b(b[0][:, :, st:NZ], b[1][:, :, st:NZ], a[0][:, :, 0:n], a[1][:, :, 0:n], sidx, Pa, n, zt, "hZ%d" % cur, "hZ%d" % (1 - cur),
                            add=(a[0][:, :, st:NZ], a[1][:, :, st:NZ]), ztg="h_zt")
                cur = 1 - cur
                st *= 2
                sidx += 1
            zf = Z[cur]
            ztag = "hZ%d" % cur
            V(lambda e, zf=zf, Pa=Pa: e.tensor_copy(out=self.Kc[:, 0, Pa].unsqueeze(2), in_=zf[0][:, :, NZ - 1:NZ]), [ztag], ["Kc"])
            V(lambda e, zf=zf, Pa=Pa: e.tensor_copy(out=self.Kc[:, 1, Pa].unsqueeze(2), in_=zf[1][:, :, NZ - 1:NZ]), [ztag], ["Kc"])
            yield

    def halo_phase(self):
        S = self.S
        f32, b16, off = self.carve()
        u_bufs = [b16(8 * T).rearrange("p (c t) -> p c t", c=8) for _ in range(2)]
        assert off[0] == 4096
        nh = self.n_halo
        rec = getattr(self, "setup_rec", [])
        ri_ = 0
        per = max(1, (len(rec) + 59) // 60)

        def replay(n):
            nonlocal ri_
            for _ in range(n):
                if ri_ >= len(rec):
                    return
                kind, a, k = rec[ri_]
                ri_ += 1
                if kind == "op":
                    S.op(*a, **k)
                elif kind == "dma":
                    S.dma(*a, **k)
                else:
                    S.barrier()
        for _ in self.haloA_g(0, u_bufs[0], 0):
            replay(per)
        replay(len(rec))
        for ti in range(nh):
            gs = self.ssm_halo_g(ti, u_bufs[ti % 2], ti % 2)
            if ti + 1 < nh:
                ga = self.haloA_g(ti + 1, u_bufs[(ti + 1) % 2], (ti + 1) % 2)
            elif self.n_own > 0:
                ga = self.ownpre_g(nh)
                self.pre_ffn1.add(nh)
            else:
                ga = None
            s_done = False
            a_done = ga is None
            while not (s_done and a_done):
                if not s_done:
                    try:
                        next(gs)
                    except StopIteration:
                        s_done = True
                for _ in range(3):
                    if not a_done:
                        try:
                            next(ga)
                        except StopIteration:
                            a_done = True
        S.barrier()

    def cmul_small(self, orr, oi, ar, ai, br, bi, t0, t1, ta, tb, tout):
        S = self.S
        TT = lambda o, a, b, op: (lambda e: e.tensor_tensor(out=o, in0=a, in1=b, op=op))
        S.op("dve", TT(t0, ar, br, ALU.mult), [ta, tb], ["cm_t0"])
        S.op("dve", TT(t1, ai, bi, ALU.mult), [ta, tb], ["cm_t1"])
        S.op("dve", TT(orr, t0, t1, ALU.subtract), ["cm_t0", "cm_t1"], [tout])
        S.op("dve", TT(t0, ar, bi, ALU.mult), [ta, tb, tout], ["cm_t0"])
        S.op("dve", TT(t1, ai, br, ALU.mult), [ta, tb, tout], ["cm_t1"])
        S.op("dve", TT(oi, t0, t1, ALU.add), ["cm_t0", "cm_t1"], [tout])

    def dbg_dump(self, name, ti, fn):
        for nm, shape in self.dbg:
            if nm == "%s_%d" % (name, ti):
                fn(self.dout[nm])

    def kv_tokmajor_g(self, ti, f32, b16, b0_fixed=None):
        S0 = self.S
        import os
        lim = int(os.environ.get("KVN", "100000"))
        cnt = [0]

        class _W:
            def op(self_, eng, fn, reads=(), writes=()):
                if eng != "pe":
                    cnt[0] += 1
                    if cnt[0] > lim:
                        return
                    if cnt[0] == lim:
                        print("LAST OP", eng, reads, writes)
                S0.op(eng, fn, reads, writes)

            def dma(self_, *a, **k):
                S0.dma(*a, **k)
        S = _W()
        d = self.din
        ps, hT = self.ps, self.hT
        own = ti >= self.n_halo
        oi = ti - self.n_halo
        kvf = [f32(4 * 512).rearrange("p (u c) -> p u c", u=4) for _ in range(2 if b0_fixed is None else 1)]
        sq = f32(4 * 256).rearrange("p (u c) -> p u c", u=4)
        ssq = f32(16)
        vst = b16(16 * 4 * 66).rearrange("p (r h c) -> p r h c", r=16, h=4)
        S.op("pool", lambda e: e.memset(vst[0:32, :, :, 64:66], 1.0), writes=["vst"])
        bi = [0]
        import os
        for g in self.need_g:
            dl = DIL[g]
            S.dma("pool", self.wKV[:, :, :], d["w_kv"][g], writes=["wKV"] + [("wD", 3), ("wD", 4), ("wD", 5), ("wD", 6)])
            if g < 2:
                M, batches = 128, [[0, 1], [2, 3]]
            else:
                M, batches = 32, [[0, 1, 2, 3], [4, 5, 6, 7], [8, 9, 10, 11], [12, 13, 14, 15]]
            for units in batches:
                nu = len(units)
                b0 = (4 if (bi[0] % 2 == 0) else 0) if b0_fixed is None else b0_fixed
                kb = kvf[bi[0] % len(kvf)]
                bi[0] += 1
                for ui, u in enumerate(units):
                    if g == 0:
                        cols = slice(u * 128, (u + 1) * 128)
                    else:
                        cols = slice(u, T, dl)
                    for k in range(8):
                        S.op("pe", lambda e, k=k, ui=ui, cols=cols, b0=b0, M=M: e.matmul(ps[0:M, b0 + ui, :], lhsT=hT[:, k, cols], rhs=self.wKV[:, k, :],
                                                                                        start=(k == 0), stop=(k == 7)),
                             reads=["wKV", ("hT", k)] + [("wD", 3), ("wD", 4), ("wD", 5), ("wD", 6)], writes=[("ps", b0 + ui)])
                rd = [("ps", b0 + ui) for ui in range(nu)]
                pk = ps[0:M, b0:b0 + nu, 0:256]
                pv = ps[0:M, b0:b0 + nu, 256:512]
                S.op("act", lambda e, pk=pk, M=M, nu=nu: e.activation(out=sq[0:M, 0:nu, :], in_=pk, func=AF.Square), reads=rd, writes=["kv_sq"])
                S.op("dve", lambda e, M=M, nu=nu: e.tensor_reduce(out=ssq[0:M, 0:nu * 4], in_=sq[0:M, 0:nu, :].rearrange("p u (h c) -> p (u h) c", h=4),
                                                                 axis=AX.X, op=ALU.add), reads=["kv_sq"], writes=["kv_ssq"])
                S.op("act", lambda e, M=M, nu=nu: e.activation(out=ssq[0:M, 0:nu * 4], in_=ssq[0:M, 0:nu * 4], func=AF.Sqrt, bias=self.eps[0:M, 0:1],
                                                               scale=1.0 / 64), reads=["kv_ssq", "eps"], writes=["kv_ssq"])
                S.op("dve", lambda e, M=M, nu=nu: e.reciprocal(out=ssq[0:M, 0:nu * 4], in_=ssq[0:M, 0:nu * 4]), reads=["kv_ssq"], writes=["kv_ssq"])
                kbk = kb[0:M, 0:nu, 0:256].rearrange("p u (h c) -> p u h c", h=4)
                S.op("dve", lambda e, pk=pk, M=M, nu=nu, kbk=kbk: e.tensor_tensor(
                    out=kbk, in0=pk.rearrange("p u (h c) -> p u h c", h=4),
                    in1=ssq[0:M, 0:nu * 4].rearrange("p (u h) -> p u h", h=4).unsqueeze(3).to_broadcast([M, nu, 4, 64]), op=ALU.mult),
                    reads=rd + ["kv_ssq"], writes=[("kvf", id(kb))])
                S.op("dve", lambda e, M=M, nu=nu, kb=kb: e.tensor_tensor(
                    out=kb[0:M, 0:nu, 0:256], in0=kb[0:M, 0:nu, 0:256], in1=self.gain_bc[0:M, 1:2, :].to_broadcast([M, nu, 256]), op=ALU.mult),
                    reads=[("kvf", id(kb)), "gain_bc"], writes=[("kvf", id(kb))])
                S.op("act", lambda e, pv=pv, M=M, nu=nu, kb=kb: e.copy(out=kb[0:M, 0:nu, 256:512], in_=pv), reads=rd, writes=[("kvf", id(kb))])
                for ui, u in enumerate(units):
                    src = ps[0:M, b0 + ui, 256:512].rearrange("p (h c) -> p h c", h=4)
                    if g == 0:
                        dst, tg = self.V[0][:, (4 * ti + u) % 8, :, 0:64], ("V", 0)
                    elif g == 1:
                        dst, tg = self.V[1][:, (ti % 2) * 4 + u, :, 0:64], ("V", 1)
                    else:
                        dst, tg = vst[0:32, u, :, 0:64], "vst"
                    S.op("act", lambda e, dst=dst, src=src: e.copy(out=dst, in_=src), reads=[("ps", b0 + ui)], writes=[tg])
                import os
                if own and not (int(os.environ.get("KVL", "0")) & 1):
                    for ui, u in enumerate(units):
                        if g == 0:
                            rows = self.dout["pkv"][0, oi * T + u * 128: oi * T + (u + 1) * 128, :]
                        else:
                            rows = self.dout["pkv"][g, oi * T + u: (oi + 1) * T: dl, :]
                        S.dma("sp", rows, kb[0:M, ui, :], reads=[("kvf", id(kb))])
                yield
            if g == 2 and not (int(os.environ.get("KVL", "0")) & 2):
                mt, q4 = ti // 4, ti % 4
                s0 = (mt % 2) * 16
                S.dma("sp", self.V[2][32 * q4:32 * q4 + 32, s0:s0 + 16, :, :], vst[0:32, :, :, :], reads=["vst"], writes=[("V", 2)])

    def kv_tokmajor(self, *a, **k):
        for _ in self.kv_tokmajor_g(*a, **k):
            pass

    def ssm_tile(self, ti, u_bf, f32, b16, own):
        S = self.S
        ps = self.ps
        tab = self.tab
        TT = lambda o, a, b, op: (lambda e: e.tensor_tensor(out=o, in0=a, in1=b, op=op))
        V = lambda fn, r, w: S.op("dve", fn, reads=r, writes=w)
        ysm = self.ysm
        T2 = 2 * T
        btb = b16(4 * 2 * 128).rearrange("p (a r c) -> p a r c", a=4, r=2)
        ctb = b16(2 * 4 * 128).rearrange("p (r a c) -> p r a c", r=2, a=4)
        X = [f32(T2), f32(T2)]
        ta, tb = f32(T2), f32(T2)
        G = [[b16(T2), b16(T2)] for _ in range(4)]
        hb = [[b16(T2), b16(T2)] for _ in range(2)]
        NZ = NCH + 1
        Z = [[f32(8 * NZ).rearrange("p (a c) -> p a c", a=8) for _ in range(2)] for _ in range(2)]
        zt = [f32(8 * NZ).rearrange("p (a c) -> p a c", a=8) for _ in range(2)]
        v4 = lambda a: a.rearrange("p (j c l) -> p j c l", j=2, l=TC)
        for cp in range(4):
            for sub in range(2):
                ch = 2 * cp + sub
                S.dma("sp", btb, self.bt_d[ch], reads=["bt_d"], writes=["btb"])
                for bt in range(2):
                    b4 = 2 * sub + bt
                    P0 = 4 * ch + 2 * bt
                    for j in range(2):
                        qq = 2 * bt + j
                        for ri in range(2):
                            bank = 4 + 2 * j + ri
                            S.op("pe", lambda e, qq=qq, ri=ri, bank=bank, ch=ch: e.matmul(ps[:, bank, :], lhsT=btb[:, qq, ri, :], rhs=u_bf[:, ch, :], start=True, stop=True),
                                 reads=["btb", ("u", ch)], writes=[("ps", bank)])
                            S.op("act", lambda e, ri=ri, j=j, bank=bank: e.copy(out=X[ri][:, j * T:(j + 1) * T], in_=ps[:, bank, :]),
                                 reads=[("ps", bank)], writes=[("X", ri)])
                    ivr = tab[:, 2, P0:P0 + 2, :].unsqueeze(2).to_broadcast([128, 2, NCH, TC])
                    ivi = tab[:, 3, P0:P0 + 2, :].unsqueeze(2).to_broadcast([128, 2, NCH, TC])
                    gr, gi = G[b4]
                    V(TT(v4(ta), v4(X[0]), ivr, ALU.mult), [("X", 0), "tab"], ["ta"])
                    V(TT(v4(tb), v4(X[1]), ivi, ALU.mult), [("X", 1), "tab"], ["tb"])
                    V(TT(ta, ta, tb, ALU.subtract), ["ta", "tb"], ["ta"])
                    V(lambda e, gr=gr: e.tensor_tensor_scan(out=gr, data0=self.scanmask[:, :], data1=ta, initial=0.0, op0=ALU.mult, op1=ALU.add),
                      ["ta", "scanmask"], [("G", b4, 0)])
                    V(TT(v4(ta), v4(X[1]), ivr, ALU.mult), [("X", 1), "tab"], ["ta"])
                    V(TT(v4(tb), v4(X[0]), ivi, ALU.mult), [("X", 0), "tab"], ["tb"])
                    V(TT(ta, ta, tb, ALU.add), ["ta", "tb"], ["ta"])
                    V(lambda e, gi=gi: e.tensor_tensor_scan(out=gi, data0=self.scanmask[:, :], data1=ta, initial=0.0, op0=ALU.mult, op1=ALU.add),
                      ["ta", "scanmask"], [("G", b4, 1)])
            Pa = slice(8 * cp, 8 * cp + 8)
            s0 = int(math.log2(TC))
            zr, zi = Z[0]
            V(lambda e, Pa=Pa: e.tensor_copy(out=zr[:, :, 0:1], in_=self.Kc[:, 0, Pa].unsqueeze(2)), ["Kc"], ["Z0"])
            V(lambda e, Pa=Pa: e.tensor_copy(out=zi[:, :, 0:1], in_=self.Kc[:, 1, Pa].unsqueeze(2)), ["Kc"], ["Z0"])
            for b4 in range(4):
                for ri in range(2):
                    ge = G[b4][ri].rearrange("p (j c l) -> p j c l", j=2, l=TC)[:, :, :, TC - 1]
                    V(lambda e, b4=b4, ri=ri, ge=ge: e.tensor_copy(out=zt[ri][:, 2 * b4:2 * b4 + 2, 1:NZ], in_=ge), [("G", b4, ri)], ["zt"])
            self.cmul_b(zr[:, :, 1:NZ], zi[:, :, 1:NZ], zt[0][:, :, 1:NZ], zt[1][:, :, 1:NZ], s0, Pa, NZ - 1, zt, "zt", "Z0", add=None)
            cur = 0
            st = 1
            sidx = s0
            while st < NZ:
                a, b = Z[cur], Z[1 - cur]
                n = NZ - st
                V(lambda e, a=a, b=b, st=st: e.tensor_copy(out=b[0][:, :, 0:st], in_=a[0][:, :, 0:st]), ["Z%d" % cur], ["Z%d" % (1 - cur)])
                V(lambda e, a=a, b=b, st=st: e.tensor_copy(out=b[1][:, :, 0:st], in_=a[1][:, :, 0:st]), ["Z%d" % cur], ["Z%d" % (1 - cur)])
                self.cmul_b(b[0][:, :, st:NZ], b[1][:, :, st:NZ], a[0][:, :, 0:n], a[1][:, :, 0:n], sidx, Pa, n, zt, "Z%d" % cur, "Z%d" % (1 - cur),
                            add=(a[0][:, :, st:NZ], a[1][:, :, st:NZ]))
                cur = 1 - cur
                st *= 2
                sidx += 1
            zf = Z[cur]
            ztag = "Z%d" % cur
            V(lambda e, zf=zf, Pa=Pa: e.tensor_copy(out=self.Kc[:, 0, Pa].unsqueeze(2), in_=zf[0][:, :, NZ - 1:NZ]), [ztag], ["Kc"])
            V(lambda e, zf=zf, Pa=Pa: e.tensor_copy(out=self.Kc[:, 1, Pa].unsqueeze(2), in_=zf[1][:, :, NZ - 1:NZ]), [ztag], ["Kc"])
            if not own:
                continue
            for sub in range(2):
                ch = 2 * cp + sub
                S.dma("pool", ctb, self.din["ssm_CT"][:, :, 4 * ch:4 * ch + 4, :], writes=["ctb"])
                for bt in range(2):
                    b4 = 2 * sub + bt
                    P0 = 4 * ch + 2 * bt
                    fr = tab[:, 0, P0:P0 + 2, :].unsqueeze(2).to_broadcast([128, 2, NCH, TC])
                    fi = tab[:, 1, P0:P0 + 2, :].unsqueeze(2).to_broadcast([128, 2, NCH, TC])
                    gr, gi = G[b4]
                    kr = zf[0][:, 2 * b4:2 * b4 + 2, 0:NCH].unsqueeze(3).to_broadcast([128, 2, NCH, TC])
                    ki = zf[1][:, 2 * b4:2 * b4 + 2, 0:NCH].unsqueeze(3).to_broadcast([128, 2, NCH, TC])
                    S.op("pool", TT(v4(gr), v4(gr), kr, ALU.add), [("G", b4, 0), ztag], [("G", b4, 0)])
                    S.op("pool", TT(v4(gi), v4(gi), ki, ALU.add), [("G", b4, 1), ztag], [("G", b4, 1)])
                    V(TT(v4(X[0]), v4(gr), fr, ALU.mult), [("G", b4, 0), "tab"], [("X", 0)])
                    V(TT(v4(X[1]), v4(gi), fi, ALU.mult), [("G", b4, 1), "tab"], [("X", 1)])
                    V(TT(hb[bt][0], X[0], X[1], ALU.subtract), [("X", 0), ("X", 1)], [("hb", bt, 0)])
                    V(TT(v4(X[0]), v4(gi), fr, ALU.mult), [("G", b4, 1), "tab"], [("X", 0)])
                    V(TT(v4(X[1]), v4(gr), fi, ALU.mult), [("G", b4, 0), "tab"], [("X", 1)])
                    V(lambda e, bt=bt: e.scalar_tensor_tensor(out=hb[bt][1], in0=X[0], scalar=-1.0, in1=X[1], op0=ALU.mult, op1=ALU.subtract),
                      [("X", 0), ("X", 1)], [("hb", bt, 1)])
                yb = 2 + ch % 2
                i = 0
                for qq in range(4):
                    bt, j = qq // 2, qq % 2
                    for ri in range(2):
                        S.op("pe", lambda e, qq=qq, ri=ri, yb=yb, i=i, bt=bt, j=j: e.matmul(ps[:, yb, :], lhsT=ctb[:, ri, qq, :], rhs=hb[bt][ri][:, j * T:(j + 1) * T],
                                                                                       start=(i == 0), stop=False),
                             reads=["ctb", ("hb", bt, ri)], writes=[("ps", yb)])
                        i += 1
                S.op("pe", lambda e, yb=yb, ch=ch: e.matmul(ps[:, yb, :], lhsT=self.diagD[:, ch, :], rhs=u_bf[:, ch, :], start=False, stop=True),
                     reads=["diagD", ("u", ch)], writes=[("ps", yb)])
                self.gelu_evict(yb, ysm[:, ch, :], ("ysm", ch), ta[:, 0:T], tb[:, 0:T])

    def gelu_evict(self, bank, out_ap, out_tag, t1, t2, Tn=T):
        S = self.S
        ps = self.ps
        S.op("act", lambda e: e.activation(out=t1, in_=ps[:, bank, 0:Tn], func=AF.Square), reads=[("ps", bank)], writes=["ta"])
        S.op("act", lambda e: e.activation(out=t1, in_=t1, func=AF.Identity, bias=self.one_c[:, 0:1], scale=0.044715), reads=["ta", "one_c"], writes=["ta"])
        S.op("dve", lambda e: e.tensor_tensor(out=t1, in0=t1, in1=ps[:, bank, 0:Tn], op=ALU.mult), reads=["ta", ("ps", bank)], writes=["ta"])
        S.op("act", lambda e: e.activation(out=t2, in_=t1, func=AF.Sigmoid, scale=1.5957691216057308), reads=["ta"], writes=["tb"])
        S.op("dve", lambda e: e.tensor_tensor(out=out_ap, in0=t2, in1=ps[:, bank, 0:Tn], op=ALU.mult), reads=["tb", ("ps", bank)], writes=[out_tag])

    def cmul_b(self, orr, oi, ar, ai, sidx, Pa, n, zt, tin, tout, add=None, ztg="zt"):
        S = self.S
        TT = lambda o, a, b, op: (lambda e: e.tensor_tensor(out=o, in0=a, in1=b, op=op))
        mr = self.apw[:, sidx, 0, Pa].unsqueeze(2).to_broadcast([128, orr.shape[1], n])
        mi = self.apw[:, sidx, 1, Pa].unsqueeze(2).to_broadcast([128, orr.shape[1], n])
        x0, x1 = zt[0][:, :, 0:n], zt[1][:, :, 0:n]
        if add is None:
            S.op("dve", TT(orr, ar, mr, ALU.mult), [tin, "apw"], [tout])
            S.op("dve", TT(oi, ai, mi, ALU.mult), [tin, "apw"], [tout])
            S.op("dve", TT(orr, orr, oi, ALU.subtract), [tout], [tout])
            S.op("dve", TT(oi, ar, mi, ALU.mult), [tin, "apw"], [tout])
            S.op("dve", TT(ar, ai, mr, ALU.mult), [tin, "apw"], [tin])
            S.op("dve", TT(oi, oi, ar, ALU.add), [tout, tin], [tout])
            return
        S.op("dve", TT(x0, ar, mr, ALU.mult), [tin, "apw"], [ztg])
        S.op("dve", TT(x1, ai, mi, ALU.mult), [tin, "apw"], [ztg])
        S.op("dve", TT(x0, x0, x1, ALU.subtract), [ztg], [ztg])
        S.op("dve", TT(orr, x0, add[0], ALU.add), [ztg, tin], [tout])
        S.op("dve", TT(x0, ar, mi, ALU.mult), [tin, "apw"], [ztg])
        S.op("dve", TT(x1, ai, mr, ALU.mult), [tin, "apw"], [ztg])
        S.op("dve", TT(x0, x0, x1, ALU.add), [ztg], [ztg])
        S.op("dve", TT(oi, x0, add[1], ALU.add), [ztg, tin], [tout])

    def attention(self, ti, qT, obT, f32, b16):
        S = self.S
        ps = self.ps
        NT, nh = self.NT, self.n_halo
        acc = [f32(T) for _ in range(4)]
        pt = [b16(T) for _ in range(4)]
        sc = [f32(T) for _ in range(2)]
        rz = f32(T)
        mt2, q4 = ti // 4, ti % 4
        for hs in range(4):
            hh, po = hs // 2, (hs % 2) * 64
            for g in range(3):
                dl = DIL[g]
                head = 4 * g + hs
                coef = -8.0 * SLOPES[head] * dl
                qc = 2 * g + hh
                if g < 2:
                    nun, nq = 4, 128
                else:
                    nun, nq = 16, 32
                plist = []
                for which in (0, 1):
                    bank = which
                    exists = []
                    for u in range(nun):
                        if g == 0:
                            blk = 4 * ti + u - (1 - which)
                            ok = blk >= 0
                            halo = blk < 4 * nh
                            kcols = slice((blk % 8) * 128, (blk % 8) * 128 + 128) if ok else None
                            qcols = slice(u * 128, (u + 1) * 128)
                            vslot = blk % 8
                        elif g == 1:
                            tt = ti - (1 - which)
                            ok = tt >= 0
                            halo = tt < nh
                            kcols = slice((tt % 2) * T + u, (tt % 2 + 1) * T, 4) if ok else None
                            qcols = slice(u, T, 4)
                            vslot = (tt % 2) * 4 + u
                        else:
                            m = mt2 - (1 - which)
                            ok = m >= 0
                            halo = (m * 4) < nh
                            kcols = slice(m * 4 * T + u, (m * 4 + 4) * T, 16) if ok else None
                            qcols = slice(u, T, 16)
                            vslot = (m % 2) * 16 + u
                        exists.append((ok, halo, kcols, qcols, vslot))
                    if not any(x[0] for x in exists):
                        continue
                    for u, (ok, halo, kcols, qcols, vslot) in enumerate(exists):
                        S.op("pe", lambda e, u=u, kcols=kcols, qcols=qcols, bank=bank, nq=nq, po=po, g=g, hh=hh, qc=qc, ok=ok: e.matmul(
                            ps[:, bank, u * nq:(u + 1) * nq],
                            lhsT=(self.kT[g][po:po + 64, hh, kcols] if ok else self.kT[g][po:po + 64, hh, 0:128]),
                            rhs=qT[po:po + 64, qc, qcols], start=True, stop=True),
                            reads=[("kT", g), ("qT", qc)], writes=[("ps", bank)])
                    scb = sc[which]
                    sv = scb.rearrange("p (u q) -> p u q", q=nq)
                    pv_ = ps[:, bank, :].rearrange("p (u q) -> p u q", q=nq)
                    if which == 1:
                        qsl = slice(0, 128) if g < 2 else slice(32 * q4, 32 * q4 + 32)
                        S.op("dve", lambda e, sv=sv, pv_=pv_, qsl=qsl, coef=coef, nun=nun, nq=nq: e.scalar_tensor_tensor(
                            out=sv, in0=self.dm[:, 0:1, qsl].to_broadcast([128, nun, nq]), scalar=coef, in1=pv_, op0=ALU.mult, op1=ALU.add),
                            reads=["dm", ("ps", bank)], writes=[("sc", which)])
                    else:
                        groups = {}
                        for u, x in enumerate(exists):
                            var = 2 if (x[1] or not x[0]) else 1
                            if not x[0]:
                                var = 3
                            groups.setdefault(var, []).append(u)
                        for var, us in groups.items():
                            u0, u1 = us[0], us[-1] + 1
                            qsl = slice(0, 128) if g < 2 else slice(32 * q4, 32 * q4 + 32)
                            if var == 3:
                                S.op("dve", lambda e, sv=sv, u0=u0, u1=u1: e.memset(sv[:, u0:u1, :], -8.0 * MASKV), reads=[("ps", bank)], writes=[("sc", which)])
                            else:
                                S.op("dve", lambda e, sv=sv, pv_=pv_, qsl=qsl, coef=coef, u0=u0, u1=u1, nq=nq, var=var: e.scalar_tensor_tensor(
                                    out=sv[:, u0:u1, :], in0=self.dm[:, var:var + 1, qsl].to_broadcast([128, u1 - u0, nq]), scalar=coef,
                                    in1=pv_[:, u0:u1, :], op0=ALU.mult, op1=ALU.add),
                                    reads=["dm", ("ps", bank)], writes=[("sc", which)])
                    pbuf = pt[(g % 2) * 2 + which]
                    ptag = ("pt", (g % 2) * 2 + which)
                    S.op("act", lambda e, scb=scb, pbuf=pbuf: e.activation(out=pbuf, in_=scb, func=AF.Exp, scale=0.125),
                         reads=[("sc", which)], writes=[ptag])
                    plist.append((which, exists, pbuf, ptag))
                ob = 2 + g % 2
                for u in range(nun):
                    for pi_, (which, exists, pbuf, ptag) in enumerate(plist):
                        ok, halo, kcols, qcols, vslot = exists[u]
                        vs = vslot if ok else 0
                        S.op("pe", lambda e, u=u, vs=vs, ob=ob, nq=nq, pbuf=pbuf, g=g, hs=hs, st_=(pi_ == 0), sp_=(pi_ == len(plist) - 1): e.matmul(
                            ps[0:65, ob, u * nq:(u + 1) * nq], lhsT=self.V[g][:, vs, hs, 0:65], rhs=pbuf[:, u * nq:(u + 1) * nq],
                            start=st_, stop=sp_),
                            reads=[("V", g), ptag], writes=[("ps", ob)])
                ob = 2 + g % 2
                a = acc[hs]
                if g == 0:
                    S.op("act", lambda e, a=a, ob=ob: e.copy(out=a[0:65, :], in_=ps[0:65, ob, :]), reads=[("ps", ob)], writes=[("acc", hs)])
                else:
                    av = a[0:65, :].rearrange("p (m r) -> p r m", r=dl)
                    pvw = ps[0:65, ob, :].rearrange("p (r m) -> p r m", r=dl)
                    S.op("dve", lambda e, av=av, pvw=pvw: e.tensor_tensor(out=av, in0=av, in1=pvw, op=ALU.add),
                         reads=[("ps", ob), ("acc", hs)], writes=[("acc", hs)])
            a = acc[hs]
            S.op("dve", lambda e, a=a: e.reciprocal(out=rz[64:65, :], in_=a[64:65, :]), reads=[("acc", hs)], writes=["rz"])
            S.op("pe", lambda e: e.matmul(ps[0:64, 4, :], lhsT=self.ones1[64:65, 0:64], rhs=rz[64:65, :], start=True, stop=True),
                 reads=["rz", "ones1"], writes=[("ps", 4)])
            S.op("dve", lambda e, a=a, hs=hs, hh=hh, po=po: e.tensor_tensor(out=obT[po:po + 64, hh, :], in0=a[0:64, :], in1=ps[0:64, 4, :], op=ALU.mult)
                 if po == 0 else e.tensor_tensor(out=obT[po:po + 64, hh, :], in0=a[0:64, :], in1=ps[0:64, 4, :], op=ALU.mult),
                 reads=[("acc", hs), ("ps", 4)], writes=[("obT", hh)])

    def _prev_exists(self, g, ti):
        if g == 0:
            return [(4 * ti + u - 1 >= 0,) for u in range(4)]
        if g == 1:
            return [(ti - 1 >= 0,)] * 4
        return [(ti // 4 - 1 >= 0,)] * 16

    def mix_out(self, ti, obT, f32, b16, Tn=T):
        S = self.S
        d = self.din
        ps = self.ps
        ysm = self.ysm
        ya = b16(8 * T).rearrange("p (c t) -> p c t", c=8)
        mg = b16(8 * T).rearrange("p (c t) -> p c t", c=8)
        sgt = [f32(T) for _ in range(2)]
        m1 = [f32(T) for _ in range(2)]

        def cons_glu(oc, bank):
            sb_ = sgt[oc % 2]
            S.op("act", lambda e: e.activation(out=sb_[:, :Tn], in_=ps[:, bank, :Tn], func=AF.Sigmoid), reads=[("ps", bank)], writes=[("sgt", oc % 2)])
            S.op("dve", lambda e: e.tensor_tensor(out=ya[:, oc, :Tn], in0=sb_[:, :Tn], in1=ysm[:, oc, :Tn], op=ALU.mult),
                 reads=[("sgt", oc % 2), ("ysm", oc)], writes=[("ya", oc)])
        self.linear(d["w_glu"], 4, ysm, "ysm", 8, Tn, cons_glu)

        def cons_ga(oc, bank):
            S.op("act", lambda e: e.activation(out=mg[:, oc, :Tn], in_=ps[:, bank, :Tn], func=AF.Sigmoid), reads=[("ps", bank)], writes=[("mg", oc)])
        self.linear(d["w_ga"], 4, self.hT, "hT", 8, Tn, cons_ga)

        def cons_pa(oc, bank):
            S.op("dve", lambda e: e.tensor_tensor(out=mg[:, oc, :Tn], in0=mg[:, oc, :Tn], in1=ps[:, bank, :Tn], op=ALU.mult),
                 reads=[("ps", bank), ("mg", oc)], writes=[("mg", oc)])
        self.linear(d["w_pa"], 4, ya, "ya", 8, Tn, cons_pa)
        gbuf = ya

        def cons_gb(oc, bank):
            S.op("act", lambda e: e.activation(out=gbuf[:, oc, :Tn], in_=ps[:, bank, :Tn], func=AF.Sigmoid), reads=[("ps", bank)], writes=[("ya", oc)])
        self.linear(d["w_gb"], 4, self.hT, "hT", 8, Tn, cons_gb)

        def cons_pb(oc, bank):
            mm = m1[oc % 2]
            S.op("dve", lambda e: e.tensor_tensor(out=mm[:, :Tn], in0=gbuf[:, oc, :Tn], in1=ps[:, bank, :Tn], op=ALU.mult),
                 reads=[("ps", bank), ("ya", oc)], writes=[("m1", oc % 2)])
            S.op("dve", lambda e: e.tensor_tensor(out=mg[:, oc, :Tn], in0=mg[:, oc, :Tn], in1=mm[:, :Tn], op=ALU.add),
                 reads=[("m1", oc % 2), ("mg", oc)], writes=[("mg", oc)])
        self.linear(d["w_pb"], 4, obT, "obT", 2, Tn, cons_pb)

        def cons_out(oc, bank):
            S.op("dve", lambda e: e.tensor_tensor(out=self.xT[:, oc, :Tn], in0=self.xT[:, oc, :Tn], in1=ps[:, bank, :Tn], op=ALU.add),
                 reads=[("ps", bank), ("xT", oc)], writes=[("xT", oc)])
        self.linear(d["w_out"], 4, mg, "mg", 8, Tn, cons_out)

    def sample_pass(self):
        S = self.S
        d = self.din
        ps = self.ps
        xT, hT = self.xT, self.hT
        Tn = 4
        TT = lambda o, a, b, op: (lambda e: e.tensor_tensor(out=o, in0=a, in1=b, op=op))
        V = lambda fn, r, w: S.op("dve", fn, reads=r, writes=w)
        for g, nm in enumerate(("0", "1", "2")):
            W = WIN[g]
            for b in range(4):
                for r0 in range(0, W - 1, 256):
                    r1 = min(W - 1, r0 + 256)
                    S.dma("sp", self.dout["skv" + nm][b, r0:r1, :], d["c" + nm][b, r0 + 1:r1 + 1, :])
        S.dma("sp", xT[:, :, 0:Tn], d["xsT"].rearrange("(c p) t -> p c t", p=128), writes=[("xT", c) for c in range(8)])
        self.ffn_block("1", 0, Tn)
        f32, b16, off = self.carve()
        self.rmsnorm(8, Tn)
        u_bf = b16(8 * T).rearrange("p (c t) -> p c t", c=8)
        ysm = b16(8 * T).rearrange("p (c t) -> p c t", c=8)
        self.ysm = ysm
        obT = b16(2 * T).rearrange("p (c t) -> p c t", c=2)

        def cons_u(oc, bank):
            S.op("act", lambda e: e.copy(out=u_bf[:, oc, 0:Tn], in_=ps[:, bank, 0:Tn]), reads=[("ps", bank)], writes=[("u", oc)])
        self.linear(d["w_u"], 4, hT, "hT", 8, Tn, cons_u)
        qs = f32(768).rearrange("p (g c) -> p g c", g=3)
        knv = f32(3 * 512).rearrange("p (g c) -> p g c", g=3)
        sq = f32(256)
        ssq = f32(4)
        for g in range(3):
            S.dma("pool", self.wKV[:, :, :], d["w_kv"][g], writes=["wKV"] + [("wD", 3), ("wD", 4), ("wD", 5), ("wD", 6)])
            b = self._wrot
            self._wrot = (self._wrot + 1) % 4
            S.dma("pool", self.wA[b][:, :, :], d["w_qs"][g], writes=[("wA", b)])
            for k in range(8):
                S.op("pe", lambda e, k=k: e.matmul(ps[0:Tn, 0, :], lhsT=hT[:, k, 0:Tn], rhs=self.wKV[:, k, :], start=(k == 0), stop=(k == 7)),
                     reads=["wKV", ("hT", k)] + [("wD", 3), ("wD", 4), ("wD", 5), ("wD", 6)], writes=[("ps", 0)])
            for k in range(8):
                S.op("pe", lambda e, k=k, b=b: e.matmul(ps[0:Tn, 1, 0:256], lhsT=hT[:, k, 0:Tn], rhs=self.wA[b][:, k, :], start=(k == 0), stop=(k == 7)),
                     reads=[("wA", b), ("hT", k)], writes=[("ps", 1)])
            for (bank, gi, dst) in ((0, 1, knv[0:Tn, g, 0:256]), (1, 0, qs[0:Tn, g, :])):
                src = ps[0:Tn, bank, 0:256]
                S.op("act", lambda e, src=src: e.activation(out=sq[0:Tn, :], in_=src, func=AF.Square), reads=[("ps", bank)], writes=["s_sq"])
                V(lambda e: e.tensor_reduce(out=ssq[0:Tn, :], in_=sq[0:Tn, :].rearrange("p (h c) -> p h c", h=4), axis=AX.X, op=ALU.add), ["s_sq"], ["s_ssq"])
                S.op("act", lambda e: e.activation(out=ssq[0:Tn, :], in_=ssq[0:Tn, :], func=AF.Sqrt, bias=self.eps[0:Tn, 0:1], scale=1.0 / 64),
                     reads=["s_ssq", "eps"], writes=["s_ssq"])
                V(lambda e: e.reciprocal(out=ssq[0:Tn, :], in_=ssq[0:Tn, :]), ["s_ssq"], ["s_ssq"])
                V(lambda e, src=src, dst=dst: e.tensor_tensor(out=dst.rearrange("p (h c) -> p h c", h=4), in0=src.rearrange("p (h c) -> p h c", h=4),
                                                               in1=ssq[0:Tn, :].unsqueeze(2).to_broadcast([Tn, 4, 64]), op=ALU.mult),
                  [("ps", bank), "s_ssq"], ["qkv_s"])
                V(lambda e, dst=dst, gi=gi: e.tensor_tensor(out=dst, in0=dst, in1=self.gain_bc[0:Tn, gi, :], op=ALU.mult), ["qkv_s", "gain_bc"], ["qkv_s"])
            S.op("act", lambda e, g=g: e.copy(out=knv[0:Tn, g, 256:512], in_=ps[0:Tn, 0, 256:512]), reads=[("ps", 0)], writes=["qkv_s"])
            S.dma("sp", self.dout["skv%d" % g][:, WIN[g] - 1, :], knv[0:Tn, g, :], reads=["qkv_s"])
        mark = off[0]
        S.barrier()
        h0 = f32(2 * 32 * 4).rearrange("p (r a t) -> p r a t", r=2, a=32)
        h1 = f32(2 * 32 * 4).rearrange("p (r a t) -> p r a t", r=2, a=32)
        hb = [b16(32 * 4).rearrange("p (a t) -> p a t", a=32) for _ in range(2)]
        t1 = f32(T)
        t2 = f32(T)
        s1 = f32(128).rearrange("p (a t) -> p a t", a=32)
        s2 = f32(128).rearrange("p (a t) -> p a t", a=32)
        btb = [b16(4 * 2 * 128).rearrange("p (a r c) -> p a r c", a=4, r=2) for _ in range(2)]
        ctb = [b16(2 * 4 * 128).rearrange("p (r a c) -> p r a c", r=2, a=4) for _ in range(2)]
        S.dma("sp", h0, d["h0"], writes=["h0"])
        for ch in range(8):
            bb = btb[ch % 2]
            S.dma("sp", bb, self.bt_d[ch], reads=["bt_d"], writes=[("btb", ch % 2)])
            for qq in range(4):
                P = 4 * ch + qq
                for ri in range(2):
                    S.op("pe", lambda e, bb=bb, qq=qq, ri=ri, P=P, ch=ch: e.matmul(ps[:, ri, P * 4:(P + 1) * 4], lhsT=bb[:, qq, ri, :], rhs=u_bf[:, ch, 0:Tn],
                                                                              start=True, stop=True),
                         reads=[("btb", ch % 2), ("u", ch)], writes=[("ps", ri)])
        ar = self.apw[:, 0, 0, :].unsqueeze(2).to_broadcast([128, 32, 4])
        ai = self.apw[:, 0, 1, :].unsqueeze(2).to_broadcast([128, 32, 4])
        bur = ps[:, 0, 0:128].rearrange("p (a t) -> p a t", a=32)
        bui = ps[:, 1, 0:128].rearrange("p (a t) -> p a t", a=32)
        V(TT(s1, h0[:, 0], ar, ALU.mult), ["h0", "apw"], ["s1"])
        V(TT(s2, h0[:, 1], ai, ALU.mult), ["h0", "apw"], ["s2"])
        V(TT(s1, s1, s2, ALU.subtract), ["s1", "s2"], ["s1"])
        V(TT(h1[:, 0], s1, bur, ALU.add), ["s1", ("ps", 0)], ["h1"])
        V(TT(s1, h0[:, 0], ai, ALU.mult), ["h0", "apw"], ["s1"])
        V(TT(s2, h0[:, 1], ar, ALU.mult), ["h0", "apw"], ["s2"])
        V(TT(s1, s1, s2, ALU.add), ["s1", "s2"], ["s1"])
        V(TT(h1[:, 1], s1, bui, ALU.add), ["s1", ("ps", 1)], ["h1"])
        S.dma("sp", self.dout["sst"], h1, reads=["h1"])
        V(lambda e: e.tensor_copy(out=hb[0], in_=h1[:, 0]), ["h1"], ["hb_s"])
        V(lambda e: e.tensor_scalar(out=hb[1], in0=h1[:, 1], scalar1=-1.0, scalar2=None, op0=ALU.mult), ["h1"], ["hb_s"])
        for ch in range(8):
            cb = ctb[ch % 2]
            S.dma("pool", cb, d["ssm_CT"][:, :, 4 * ch:4 * ch + 4, :], writes=[("ctb", ch % 2)])
            yb = 4 + ch % 2
            i = 0
            for qq in range(4):
                for ri in range(2):
                    S.op("pe", lambda e, cb=cb, qq=qq, ri=ri, yb=yb, i=i, ch=ch: e.matmul(ps[:, yb, 0:Tn], lhsT=cb[:, ri, qq, :], rhs=hb[ri][:, 4 * ch + qq, :],
                                                                                     start=(i == 0), stop=False),
                         reads=[("ctb", ch % 2), "hb_s"], writes=[("ps", yb)])
                    i += 1
            S.op("pe", lambda e, yb=yb, ch=ch: e.matmul(ps[:, yb, 0:Tn], lhsT=self.diagD[:, ch, :], rhs=u_bf[:, ch, 0:Tn], start=False, stop=True),
                 reads=["diagD", ("u", ch)], writes=[("ps", yb)])
            self.gelu_evict(yb, ysm[:, ch, 0:Tn], ("ysm", ch), t1[:, 0:Tn], t2[:, 0:Tn], Tn)
        S.barrier()
        off[0] = mark
        pens = f32(12)
        sel4 = f32(4 * 128).rearrange("p (b m) -> p b m", b=4)
        selc = f32(16).rearrange("p (b m) -> p b m", b=4)
        kc = [f32(512) for _ in range(3)]
        prod = f32(256)
        sc = f32(12)
        stg = f32(780)
        OZ = f32(780)
        pn = f32(12)
        zs = f32(4)
        os_ = f32(256)
        S.dma("sp", pens, d["pens"], writes=["pens"])
        S.dma("sp", sel4[0:4], d["sel4"], writes=["sel4"])
        S.dma("sp", selc, d["selc"], writes=["selc"])
        for b in range(4):
            S.op("pe", lambda e, b=b: e.matmul(ps[:, 0, :], lhsT=sel4[0:4, b, :], rhs=qs[0:4, :, :].rearrange("p g c -> p (g c)")[:, 0:512], start=True, stop=True),
                 reads=["sel4", "qkv_s"], writes=[("ps", 0)])
            S.op("pe", lambda e, b=b: e.matmul(ps[:, 1, 0:256], lhsT=sel4[0:4, b, :], rhs=qs[0:4, :, :].rearrange("p g c -> p (g c)")[:, 512:768], start=True, stop=True),
                 reads=["sel4", "qkv_s"], writes=[("ps", 1)])
            for g in range(3):
                W, dl = WIN[g], DIL[g]
                S.dma("sp", kc[g], d["c%d" % g][b, 0:W:dl, :], writes=[("kc", g)])
                qb = ps[:, 0, g * 256:(g + 1) * 256] if g < 2 else ps[:, 1, 0:256]
                V(TT(prod, kc[g][:, 0:256], qb, ALU.mult), [("kc", g), ("ps", 0), ("ps", 1)], ["prod"])
                V(lambda e, g=g: e.tensor_reduce(out=sc[:, 4 * g:4 * g + 4], in_=prod.rearrange("p (h c) -> p h c", h=4), axis=AX.X, op=ALU.add),
                  ["prod"], ["sc_s"])
            V(TT(sc, sc, pens, ALU.add), ["sc_s", "pens"], ["sc_s"])
            S.op("act", lambda e: e.activation(out=stg[:, 768:780], in_=sc, func=AF.Exp, scale=0.125), reads=["sc_s"], writes=["stg"])
            for g in range(3):
                V(lambda e, g=g: e.tensor_tensor(out=stg[:, g * 256:(g + 1) * 256].rearrange("p (h c) -> p h c", h=4),
                                                  in0=kc[g][:, 256:512].rearrange("p (h c) -> p h c", h=4),
                                                  in1=stg[:, 768 + 4 * g:772 + 4 * g].unsqueeze(2).to_broadcast([128, 4, 64]), op=ALU.mult),
                  [("kc", g), "stg"], ["stg"])
            S.op("pe", lambda e, b=b: e.matmul(ps[0:4, 2, :], lhsT=selc[:, b, :], rhs=stg[:, 0:512], start=(b == 0), stop=(b == 3)),
                 reads=["selc", "stg"], writes=[("ps", 2)])
            S.op("pe", lambda e, b=b: e.matmul(ps[0:4, 3, 0:268], lhsT=selc[:, b, :], rhs=stg[:, 512:780], start=(b == 0), stop=(b == 3)),
                 reads=["selc", "stg"], writes=[("ps", 3)])
        S.op("act", lambda e: e.copy(out=OZ[0:4, 0:512], in_=ps[0:4, 2, :]), reads=[("ps", 2)], writes=["OZ"])
        S.op("act", lambda e: e.copy(out=OZ[0:4, 512:780], in_=ps[0:4, 3, 0:268]), reads=[("ps", 3)], writes=["OZ"])
        for g in range(3):
            V(TT(prod[0:4, :], qs[0:4, g, :], knv[0:4, g, 0:256], ALU.mult), ["qkv_s"], ["prod"])
            V(lambda e, g=g: e.tensor_reduce(out=pn[0:4, 4 * g:4 * g + 4], in_=prod[0:4, :].rearrange("p (h c) -> p h c", h=4), axis=AX.X, op=ALU.add),
              ["prod"], ["pn"])
        S.op("act", lambda e: e.activation(out=pn[0:4, :], in_=pn[0:4, :], func=AF.Exp, scale=0.125), reads=["pn"], writes=["pn"])
        V(TT(OZ[0:4, 768:780], OZ[0:4, 768:780], pn[0:4, :], ALU.add), ["OZ", "pn"], ["OZ"])
        for g in range(3):
            V(lambda e, g=g: e.tensor_tensor(out=prod[0:4, :].rearrange("p (h c) -> p h c", h=4), in0=knv[0:4, g, 256:512].rearrange("p (h c) -> p h c", h=4),
                                              in1=pn[0:4, 4 * g:4 * g + 4].unsqueeze(2).to_broadcast([4, 4, 64]), op=ALU.mult), ["qkv_s", "pn"], ["prod"])
            V(TT(OZ[0:4, g * 256:(g + 1) * 256], OZ[0:4, g * 256:(g + 1) * 256], prod[0:4, :], ALU.add), ["OZ", "prod"], ["OZ"])
        V(TT(zs[0:4, :], OZ[0:4, 768:772], OZ[0:4, 772:776], ALU.add), ["OZ"], ["zs"])
        V(TT(zs[0:4, :], zs[0:4, :], OZ[0:4, 776:780], ALU.add), ["OZ", "zs"], ["zs"])
        V(lambda e: e.reciprocal(out=zs[0:4, :], in_=zs[0:4, :]), ["zs"], ["zs"])
        V(TT(os_[0:4, :], OZ[0:4, 0:256], OZ[0:4, 256:512], ALU.add), ["OZ"], ["os"])
        V(TT(os_[0:4, :], os_[0:4, :], OZ[0:4, 512:768], ALU.add), ["OZ", "os"], ["os"])
        V(lambda e: e.tensor_tensor(out=os_[0:4, :].rearrange("p (h c) -> p h c", h=4), in0=os_[0:4, :].rearrange("p (h c) -> p h c", h=4),
                                    in1=zs[0:4, :].unsqueeze(2).to_broadcast([4, 4, 64]), op=ALU.mult), ["os", "zs"], ["os"])
        for hh in range(2):
            S.op("pe", lambda e, hh=hh: e.matmul(ps[:, 4 + hh, 0:4], lhsT=os_[0:4, hh * 128:(hh + 1) * 128], rhs=self.identf[0:4, 0:4], start=True, stop=True),
                 reads=["os", "identf"], writes=[("ps", 4 + hh)])
            S.op("act", lambda e, hh=hh: e.copy(out=obT[:, hh, 0:Tn], in_=ps[:, 4 + hh, 0:4]), reads=[("ps", 4 + hh)], writes=[("obT", hh)])
        S.barrier()
        off[0] = mark
        self.mix_out(0, obT, f32, b16, Tn)
        S.barrier()
        self.ffn_block("2", 16, Tn)
        S.dma("sp", self.dout["ysT"].rearrange("(c p) t -> p c t", p=128), xT[:, :, 0:Tn], reads=[("xT", c) for c in range(8)])


_NC_CACHE = {}


def _get_nc():
    if "nc" not in _NC_CACHE:
        kk = K(4, 4, sample=True)
        _NC_CACHE["nc"] = kk.build()
    return _NC_CACHE["nc"]


def kernel(**inp):
    inp = {k: np.asarray(v) for k, v in inp.items()}
    nc = _get_nc()
    shared = prep_shared(inp)
    consts = [host_consts(True), host_consts(False)]
    xp = inp["x_prompt"]
    in_maps = []
    for c in range(8):
        b, half = c // 2, c % 2
        m = dict(shared)
        m.update(consts[half])
        own = xp[b, half * 2048:(half + 1) * 2048]
        halo = xp[b, 0:2048] if half == 1 else np.zeros_like(own)
        m["xT"] = np.ascontiguousarray(np.concatenate([halo, own], axis=0).T)
        sl = slice(4 * c, 4 * c + 4)
        m["xsT"] = np.ascontiguousarray(inp["x_sample"][sl, 0, :].T)
        h0 = np.stack([np.stack([lay_gp(inp["state_ssm_re"][0, 4 * c + t]) for t in range(4)], axis=-1),
                       np.stack([lay_gp(inp["state_ssm_im"][0, 4 * c + t]) for t in range(4)], axis=-1)], axis=1)
        m["h0"] = np.ascontiguousarray(h0)
        m["c0"] = np.ascontiguousarray(inp["cache_kv_w128"][0, sl].reshape(4, 128, 512))
        m["c1"] = np.ascontiguousarray(inp["cache_kv_w512"][0, sl].reshape(4, 512, 512))
        m["c2"] = np.ascontiguousarray(inp["cache_kv_w2048"][0, sl].reshape(4, 2048, 512))
        in_maps.append({k: np.ascontiguousarray(v, dtype=np.float32) for k, v in m.items()})
    res = run_bass_kernel_spmd(nc, in_maps, core_ids=list(range(8)))
    R = res.results
    unlay = lambda a: a.reshape(2, 64, 32).transpose(2, 0, 1).reshape(64, 64)
    y_prompt = np.zeros((4, 4096, 1024), np.float32)
    y_sample = np.zeros((32, 1, 1024), np.float32)
    p_re = np.zeros((1, 4, 64, 64), np.float32)
    p_im = np.zeros((1, 4, 64, 64), np.float32)
    s_re = np.zeros((1, 32, 64, 64), np.float32)
    s_im = np.zeros((1, 32, 64, 64), np.float32)
    pkv = [np.zeros((1, 4, w, 2, 4, 64), np.float32) for w in WIN]
    skv = [np.zeros((1, 32, w, 2, 4, 64), np.float32) for w in WIN]
    for c in range(8):
        b, half = c // 2, c % 2
        r = R[c]
        y_prompt[b, half * 2048:(half + 1) * 2048] = r["yT"].T
        y_sample[4 * c:4 * c + 4, 0, :] = r["ysT"].T
        if half == 1:
            p_re[0, b] = unlay(r["pst"][:, 0, :])
            p_im[0, b] = unlay(r["pst"][:, 1, :])
            for g, w in enumerate(WIN):
                pkv[g][0, b] = r["pkv"][g][2048 - w:].reshape(w, 2, 4, 64)
        for t in range(4):
            s_re[0, 4 * c + t] = unlay(r["sst"][:, 0, :, t])
            s_im[0, 4 * c + t] = unlay(r["sst"][:, 1, :, t])
        for g, w in enumerate(WIN):
            skv[g][0, 4 * c:4 * c + 4] = r["skv%d" % g].reshape(4, w, 2, 4, 64)
    return (y_prompt, y_sample, p_re, p_im, pkv[0], pkv[1], pkv[2], s_re, s_im, skv[0], skv[1], skv[2])
```

```python
import contextlib
import math
import numpy as np
import concourse.bass as bass
import concourse.mybir as mybir
from concourse.bass_utils import run_bass_kernel_spmd

F32 = mybir.dt.float32
BF16 = mybir.dt.bfloat16
ALU = mybir.AluOpType
AF = mybir.ActivationFunctionType
AX = mybir.AxisListType

ENGS = ["pe", "act", "dve", "pool", "sp"]
T = 512
NPF = 11
TC = 32
NCH = T // TC
DIL = (1, 4, 16)
WIN = (128, 512, 2048)
SLOPES = [2.0 ** (-8.0 * i / 12.0) for i in range(1, 13)]
MASKV = 1.0e5


class Sched:
    def __init__(self, nc, stack, n_dma_slots=40):
        self.nc = nc
        self.ops = {e: [] for e in ENGS}
        self.sem = {e: stack.enter_context(nc.semaphore("s_" + e)) for e in ENGS if e != "sp"}
        self.cnt = {e: 0 for e in ENGS}
        self.dsem = [stack.enter_context(nc.semaphore("d%d" % i)) for i in range(n_dma_slots)]
        self.dcnt = [0] * n_dma_slots
        self.dnext = 0
        self.dnext_sw = 0
        self.pool_pending = None
        self.waited = {e: {} for e in ENGS}
        self.lastw = {}
        self.readers = {}

    def _semof(self, pk):
        return self.sem[pk[1]] if pk[0] == "e" else self.dsem[pk[1]]

    def _need(self, eng, waits, pk, v):
        if pk == ("e", eng) and eng == "pe":
            return
        if self.waited[eng].get(pk, 0) >= v:
            return
        self.waited[eng][pk] = v
        waits[pk] = v

    def _deps(self, eng, reads, writes):
        waits = {}
        for r in reads:
            lw = self.lastw.get(r)
            if lw is not None:
                self._need(eng, waits, lw[0], lw[1])
        for w in writes:
            lw = self.lastw.get(w)
            if lw is not None:
                self._need(eng, waits, lw[0], lw[1])
            for pk, v in self.readers.get(w, {}).items():
                self._need(eng, waits, pk, v)
        return waits

    def _record(self, pk, val, reads, writes):
        for r in reads:
            self.readers.setdefault(r, {})[pk] = val
        for w in writes:
            self.lastw[w] = (pk, val)
            self.readers[w] = {}

    def _flush_pool(self):
        if self.pool_pending is None:
            return
        dsn, csn = self.pool_pending
        self.pool_pending = None
        waits = {}
        for slot, v in enumerate(dsn):
            if v:
                self._need("pool", waits, ("d", slot), 16 * v)
        for p, v in csn.items():
            if p not in ("sp", "pool") and v:
                self._need("pool", waits, ("e", p), v)
        if waits:
            self.ops["pool"].append((list(waits.items()), None, None, 0))

    def op(self, eng, fn, reads=(), writes=()):
        if eng == "pool":
            self._flush_pool()
        waits = self._deps(eng, reads, writes)
        self.cnt[eng] += 1
        self.ops[eng].append((list(waits.items()), fn, ("e", eng), 1))
        self._record(("e", eng), self.cnt[eng], reads, writes)

    def dma(self, q, out, in_, reads=(), writes=(), free=False):
        if q == "pool" and not free:
            self._flush_pool()
        half = len(self.dsem) // 2
        if q == "pool":
            slot = half + self.dnext_sw
            self.dnext_sw = (self.dnext_sw + 1) % (len(self.dsem) - half)
        else:
            slot = self.dnext
            self.dnext = (self.dnext + 1) % half
        waits = self._deps(q, reads, writes)
        k = self.dcnt[slot] + 1
        self.dcnt[slot] = k
        if k > 1:
            self._need(q, waits, ("d", slot), 16 * (k - 1))
        fn = lambda e, out=out, in_=in_: e.dma_start(out=out, in_=in_)
        self.ops[q].append((list(waits.items()), fn, ("d", slot), 16))
        self._record(("d", slot), 16 * k, reads, writes)

    def barrier(self):
        for e in ENGS:
            if e == "pool":
                self.pool_pending = (list(self.dcnt), dict(self.cnt))
                continue
            waits = {}
            for slot in range(len(self.dsem)):
                if self.dcnt[slot]:
                    self._need(e, waits, ("d", slot), 16 * self.dcnt[slot])
            for p in ENGS:
                if p != "sp" and p != e and self.cnt[p]:
                    self._need(e, waits, ("e", p), self.cnt[p])
            if waits:
                self.ops[e].append((list(waits.items()), None, None, 0))

    def finish(self):
        waits = {}
        for slot in range(len(self.dsem)):
            if self.dcnt[slot]:
                self._need("sp", waits, ("d", slot), 16 * self.dcnt[slot])
        for e in ENGS:
            if e != "sp" and self.cnt[e]:
                self._need("sp", waits, ("e", e), self.cnt[e])
        self.ops["sp"].append((list(waits.items()), None, None, 0))

    def emit(self):
        nc = self.nc
        with nc.Block() as block:
            def run(e, name):
                for waits, fn, pk, inc in self.ops[name]:
                    for wpk, v in waits:
                        e.wait_ge(self._semof(wpk), v)
                    if fn is not None:
                        fn(e).then_inc(self._semof(pk), inc)

            @block.tensor
            def _(e):
                run(e, "pe")

            @block.scalar
            def _(e):
                run(e, "act")

            @block.vector
            def _(e):
                run(e, "dve")

            @block.gpsimd
            def _(e):
                run(e, "pool")

            @block.sync
            def _(e):
                run(e, "sp")


def lay_w8(w, piece=256):
    n = w.shape[1] // piece
    return np.ascontiguousarray(w.reshape(8, 128, n, piece).transpose(2, 1, 0, 3))


def lay_wk(w, nk, piece=256):
    n = w.shape[1] // piece
    return np.ascontiguousarray(w.reshape(nk, 128, n, piece).transpose(2, 1, 0, 3))


def lay_wd(w):
    return np.ascontiguousarray(w.reshape(NPF, 2, 128, 2, 512).transpose(3, 0, 2, 1, 4))


def lay_vec(v):
    return np.ascontiguousarray(v.reshape(8, 128).T)


def lay_gp(a):
    return np.ascontiguousarray(a.reshape(32, 2, 64).transpose(1, 2, 0).reshape(128, 32))


def host_consts(half_is_first):
    c = {}
    c["ident"] = np.eye(128, dtype=np.float32)
    ob = np.zeros((128, 128), np.float32)
    ob[:64, :64] = 1.0 / 64
    ob[64:, 64:] = 1.0 / 64
    c["onesblk"] = ob
    k = np.arange(128)[:, None].astype(np.float32)
    q = np.arange(128)[None, :].astype(np.float32)
    cur = np.where(k <= q, q - k, MASKV).astype(np.float32)
    prev = np.where(k >= q, q + 128 - k, MASKV).astype(np.float32)
    prevh = np.full((128, 128), MASKV, np.float32) if half_is_first else prev
    c["dm"] = np.ascontiguousarray(np.stack([cur, prev, prevh], axis=1))
    m = np.ones((128, 2 * T), np.float32)
    m[:, ::TC] = 0.0
    c["scanmask"] = m
    pen = np.zeros((128, 12), np.float32)
    for g in range(3):
        for h in range(4):
            pen[:, 4 * g + h] = -8.0 * SLOPES[4 * g + h] * (WIN[g] - DIL[g] * np.arange(128))
    c["pens"] = pen
    sel = np.zeros((4, 4, 128), np.float32)
    for b in range(4):
        sel[b, b, :] = 1.0
    c["sel4"] = sel
    selc = np.zeros((128, 4, 4), np.float32)
    for b in range(4):
        selc[:, b, b] = 1.0
    c["selc"] = selc
    return c


def prep_shared(inp):
    L = 0
    d = {}
    for nm, key in (("1", "ffn1"), ("2", "ffn2")):
        d["wg" + nm] = lay_w8(inp[key + "_w_gate"][L])
        d["wu" + nm] = lay_w8(inp[key + "_w_up"][L])
        d["wd" + nm] = lay_wd(inp[key + "_w_down"][L])
    w_in = inp["w_in"][L]
    d["w_u"] = lay_w8(w_in[:, 0:1024])
    d["w_q"] = lay_w8(w_in[:, 1024:1792])
    d["w_k"] = lay_w8(w_in[:, 1792:2560])
    wk = w_in[:, 1792:2560].reshape(1024, 3, 256)
    wv = w_in[:, 2560:3328].reshape(1024, 3, 256)
    wkv = np.concatenate([wk, wv], axis=2).reshape(1024, 3 * 512)
    d["w_kv"] = lay_w8(wkv, piece=512)
    wq = w_in[:, 1024:1792].reshape(1024, 3, 256)
    d["w_qs"] = lay_w8(np.ascontiguousarray(wq.reshape(1024, 768)), piece=256)
    d["w_ga"] = lay_w8(w_in[:, 3328:4352])
    d["w_gb"] = lay_w8(w_in[:, 4352:5376])
    d["w_glu"] = lay_w8(inp["w_glu"][L])
    d["w_pa"] = lay_w8(inp["w_proj_a"][L])
    d["w_out"] = lay_w8(inp["w_out"][L])
    d["w_pb"] = lay_wk(inp["w_proj_b"][L], 2)
    vecs = np.zeros((128, 40), np.float32)
    vecs[:, 0:8] = lay_vec(inp["ffn1_norm"][L])
    vecs[:, 8:16] = lay_vec(inp["mix_norm"][L])
    vecs[:, 16:24] = lay_vec(inp["ffn2_norm"][L])
    vecs[:, 24:32] = lay_vec(inp["ssm_d"][L])
    vecs[:, 32] = np.tile(inp["q_gain"][L], 2)
    vecs[:, 33] = np.tile(inp["k_gain"][L], 2)
    d["vecs"] = vecs
    d["gain_bc"] = np.ascontiguousarray(np.stack([np.tile(inp["q_gain"][L][None, :], (128, 4)),
                                                  np.tile(inp["k_gain"][L][None, :], (128, 4))], axis=1))
    sc = np.stack([lay_gp(inp["ssm_lambda_re"][L]), lay_gp(inp["ssm_lambda_im"][L]),
                   lay_gp(np.tile(inp["ssm_log_dt"][L][:, None], (1, 64)))], axis=1)
    d["ssm_sc"] = np.ascontiguousarray(sc)
    B = np.stack([inp["ssm_b_re"][L], inp["ssm_b_im"][L]], axis=0)
    B = B.reshape(2, 32, 2, 64, 16).transpose(2, 3, 0, 1, 4).reshape(128, 2, 32, 16)
    d["ssm_B"] = np.ascontiguousarray(B)
    C = np.stack([inp["ssm_c_re"][L], inp["ssm_c_im"][L]], axis=0)
    CT = np.zeros((128, 2, 32, 128), np.float32)
    for P in range(32):
        qq = P % 4
        for g2 in range(2):
            g = 2 * P + g2
            for ri in range(2):
                CT[g2 * 64:(g2 + 1) * 64, ri, P, 32 * qq + 16 * g2: 32 * qq + 16 * g2 + 16] = C[ri, g].T
    d["ssm_CT"] = CT
    return d


class K:
    def __init__(self, n_halo, n_own, sample, dbg=None, level=9):
        self.level = level
        self.pipe_halo = True
        self.pre_ffn1 = set()
        self.n_halo, self.n_own, self.sample = n_halo, n_own, sample
        self.NT = n_halo + n_own
        self.dbg = dbg or []
        self.nc = bass.Bass("TRN2", target_bir_lowering=False)
        self.din = {}
        self.dout = {}

    def di(self, name, shape):
        self.din[name] = self.nc.dram_tensor(name, list(shape), F32, kind="ExternalInput").ap()
        return self.din[name]

    def do(self, name, shape):
        self.dout[name] = self.nc.dram_tensor(name, list(shape), F32, kind="ExternalOutput").ap()
        return self.dout[name]

    def sb(self, name, shape, dt):
        return self.st.enter_context(self.nc.sbuf_tensor(name, list(shape), dt))

    def build(self):
        nc = self.nc
        NT = self.NT
        di, do = self.di, self.do
        di("xT", [1024, NT * T])
        for nm in ("1", "2"):
            di("wg" + nm, [NPF, 128, 8, 256]); di("wu" + nm, [NPF, 128, 8, 256]); di("wd" + nm, [2, NPF, 128, 2, 512])
        for nm in ("w_u", "w_ga", "w_gb", "w_glu", "w_pa", "w_out"):
            di(nm, [4, 128, 8, 256])
        di("w_q", [3, 128, 8, 256]); di("w_k", [3, 128, 8, 256]); di("w_qs", [3, 128, 8, 256])
        di("w_kv", [3, 128, 8, 512])
        di("w_pb", [4, 128, 2, 256])
        di("vecs", [128, 40]); di("gain_bc", [128, 2, 256])
        di("ssm_sc", [128, 3, 32]); di("ssm_B", [128, 2, 32, 16]); di("ssm_CT", [128, 2, 32, 128])
        di("ident", [128, 128]); di("onesblk", [128, 128]); di("dm", [128, 3, 128]); di("scanmask", [128, 2 * T])
        do("yT", [1024, self.n_own * T])
        do("pst", [128, 2, 32])
        do("pkv", [3, self.n_own * T, 512])
        if self.sample:
            di("xsT", [1024, 4]); di("h0", [128, 2, 32, 4])
            di("c0", [4, 128, 512]); di("c1", [4, 512, 512]); di("c2", [4, 2048, 512])
            di("pens", [128, 12]); di("sel4", [4, 4, 128]); di("selc", [128, 4, 4])
            do("ysT", [1024, 4]); do("sst", [128, 2, 32, 4])
            do("skv0", [4, 128, 512]); do("skv1", [4, 512, 512]); do("skv2", [4, 2048, 512])
        for name, shape in self.dbg:
            do(name, shape)
        self.bt_d = nc.dram_tensor("bt_scr", [8, 128, 4, 2, 128], BF16, kind="Internal").ap()

        with contextlib.ExitStack() as st:
            self.st = st
            self.S = Sched(nc, st)
            self.alloc()
            self.setup()
            if self.pipe_halo and self.n_halo > 0:
                self.halo_phase()
                for ti in range(self.n_halo, NT):
                    self.tile(ti)
            else:
                for ti in range(NT):
                    self.tile(ti)
            if self.sample:
                self.sample_pass()
            self.S.finish()
            self.S.emit()
        return nc

    def alloc(self):
        sb = self.sb
        nc = self.nc
        NT = self.NT
        self.ps = self.st.enter_context(nc.psum_tensor("ps", [128, 8, 512], F32))
        self.xT = sb("xT_s", [128, 8, T], F32)
        self.hT = sb("hT_s", [128, 8, T], BF16)
        self.wA = [sb("wA%d" % i, [128, 8, 256], BF16) for i in range(4)]
        self.wD = [sb("wD%d" % i, [128, 2, 512], BF16) for i in range(3)]
        self.wKV = sb("wKV", [128, 8, 512], BF16)
        self.kT = [sb("kT0", [128, 2, 2 * T], BF16), sb("kT1", [128, 2, 2 * T], BF16), sb("kT2", [128, 2, ((NT + 3) // 4) * 4 * T], BF16)]
        self.V = [sb("V0", [128, 8, 4, 66], BF16), sb("V1", [128, 8, 4, 66], BF16), sb("V2", [128, 32, 4, 66], BF16)]
        self.tab = sb("tab_s", [128, 4, 32, TC], F32)
        self.apw = sb("apw_s", [128, 10, 2, 32], F32)
        self.ainv = sb("ainv", [128, 2, 32], F32)
        self.Kc = sb("Kcarry", [128, 2, 32], F32)
        self.vecs = sb("vecs_s", [128, 40], F32)
        self.gain_bc = sb("gain_bc_s", [128, 2, 256], F32)
        self.ident = sb("ident_b", [128, 128], BF16)
        self.identf = sb("ident_f", [128, 128], F32)
        self.onesblk = sb("onesblk_b", [128, 128], BF16)
        self.ones = sb("ones_b", [128, 128], BF16)
        self.ones1 = sb("ones1_f", [128, 64], F32)
        self.dm = sb("dm_s", [128, 3, 128], F32)
        self.scanmask = sb("scanmask_s", [128, 2 * T], F32)
        self.diagD = sb("diagD", [128, 8, 128], BF16)
        self.eps = sb("eps_s", [128, 1], F32)
        self.negpi = sb("negpi_s", [128, 1], F32)
        self.rstd = sb("rstd_s", [128, T], F32)
        self.sqb = sb("sqb_s", [128, T], BF16)
        SCR = 18304
        self.scr = sb("scr", [128, SCR], F32)

    def carve(self):
        scr = self.scr
        off = [0]

        def f32(n):
            v = scr[:, off[0]:off[0] + n]
            off[0] += n
            return v

        def b16(n):
            w = (n + 1) // 2
            v = scr[:, off[0]:off[0] + w].bitcast(BF16)
            off[0] += w
            return v
        return f32, b16, off

    def setup(self):
        S, nc = self.S, self.nc
        d = self.din
        S.dma("sp", self.vecs[:, :], d["vecs"], writes=["vecs"])
        S.dma("sp", self.gain_bc[:, :, :], d["gain_bc"], writes=["gain_bc"])
        S.dma("sp", self.identf[:, :], d["ident"], writes=["identf"])
        S.dma("sp", self.dm[:, :, :], d["dm"], writes=["dm"])
        S.dma("sp", self.scanmask[:, :], d["scanmask"], writes=["scanmask"])
        S.dma("pool", self.ident[:, :], d["ident"], writes=["ident"])
        S.dma("pool", self.onesblk[:, :], d["onesblk"], writes=["onesblk"])
        S.op("pool", lambda e: e.memset(self.ones[:, :], 1.0 / 1024.0), writes=["ones"])
        S.op("pool", lambda e: e.memset(self.ones1[:, :], 1.0), writes=["ones1"])
        S.op("pool", lambda e: e.memset(self.eps[:, :], 1e-6), writes=["eps"])
        S.op("pool", lambda e: e.memset(self.negpi[:, :], -math.pi), writes=["negpi"])
        S.op("pool", lambda e: e.memset(self.Kc[:, :, :], 0.0), writes=["Kc"])
        for g in range(3):
            S.op("pool", lambda e, g=g: e.memset(self.kT[g][:, :, :], 0.0), writes=[("kT", g)])
            S.op("pool", lambda e, g=g: e.memset(self.V[g][:, :, :, :], 0.0), writes=[("V", g)])
            S.op("pool", lambda e, g=g: e.memset(self.V[g][:, :, :, 64:65], 1.0), writes=[("V", g)])
        for c in range(8):
            S.op("dve", lambda e, c=c: e.tensor_scalar(out=self.diagD[:, c, :], in0=self.identf[:, :], scalar1=self.vecs[:, 24 + c:25 + c],
                                                       scalar2=None, op0=ALU.mult),
                 reads=["identf", "vecs"], writes=["diagD"])
        if self.pipe_halo and self.n_halo > 0:
            rec = []
            real = self.S

            class _Rec:
                def op(self_, *a, **k):
                    rec.append(("op", a, k))

                def dma(self_, *a, **k):
                    rec.append(("dma", a, k))

                def barrier(self_):
                    rec.append(("barrier", (), {}))
            self.S = _Rec()
            self.ssm_setup()
            self.S = real
            self.setup_rec = rec
        else:
            self.ssm_setup()

    def ssm_setup(self):
        S = self.S
        d = self.din
        f32, b16, off = self.carve()
        off[0] = 4096 if (self.pipe_halo and self.n_halo > 0) else 0
        sc = f32(96).rearrange("p (a b) -> p a b", a=3)
        Bm = f32(2 * 32 * 16).rearrange("p (r a c) -> p r a c", r=2, a=32)
        tmp = [f32(32) for _ in range(12)]
        Bb = f32(2 * 32 * 16).rearrange("p (r a c) -> p r a c", r=2, a=32)
        big = [f32(32 * 16) for _ in range(2)]
        xpad = [f32(2 * 128).rearrange("p (r c) -> p r c", r=2) for _ in range(4)]
        btb = [b16(4 * 2 * 128).rearrange("p (a r c) -> p a r c", a=4, r=2) for _ in range(2)]
        S.dma("sp", sc, d["ssm_sc"], writes=["sc"])
        S.dma("sp", Bm, d["ssm_B"], writes=["Bm"])
        lr, li, dt, mag, ang, cs, sn, a_re, a_im, t0, t1, t2 = tmp
        V = lambda fn, r, w: S.op("dve", fn, reads=r, writes=w)
        A = lambda fn, r, w: S.op("act", fn, reads=r, writes=w)
        TT = lambda o, a, b, op: (lambda e: e.tensor_tensor(out=o, in0=a, in1=b, op=op))
        V(lambda e: e.tensor_scalar_min(out=lr, in0=sc[:, 0, :], scalar1=-1e-4), ["sc"], ["lr"])
        V(lambda e: e.tensor_copy(out=li, in_=sc[:, 1, :]), ["sc"], ["li"])
        A(lambda e: e.activation(out=dt, in_=sc[:, 2, :], func=AF.Exp), ["sc"], ["dt"])
        V(TT(mag, lr, dt, ALU.mult), ["lr", "dt"], ["mag"])
        A(lambda e: e.activation(out=mag, in_=mag, func=AF.Exp), ["mag"], ["mag"])
        V(TT(ang, li, dt, ALU.mult), ["li", "dt"], ["ang"])
        ki = f32(32).bitcast(mybir.dt.int32)
        kf = f32(32)
        for dst, shift, tg in ((sn, 0.0, "sn"), (cs, 0.5 * math.pi, "cs")):
            V(lambda e, dst=dst, shift=shift: e.tensor_scalar(out=dst, in0=ang, scalar1=shift, scalar2=None, op0=ALU.add), ["ang"], [tg])
            V(lambda e, dst=dst: e.tensor_scalar(out=kf, in0=dst, scalar1=1.0 / (2.0 * math.pi), scalar2=None, op0=ALU.mult), [tg], ["kf"])
            V(lambda e: e.tensor_copy(out=ki, in_=kf), ["kf"], ["ki"])
            V(lambda e: e.tensor_copy(out=kf, in_=ki), ["ki"], ["kf"])
            V(lambda e, dst=dst: e.scalar_tensor_tensor(out=dst, in0=kf, scalar=-2.0 * math.pi, in1=dst, op0=ALU.mult, op1=ALU.add), ["kf", tg], [tg])
            V(lambda e, dst=dst: e.tensor_scalar(out=kf, in0=dst, scalar1=math.pi, scalar2=-2.0 * math.pi, op0=ALU.is_gt, op1=ALU.mult), [tg], ["kf"])
            V(TT(dst, dst, kf, ALU.add), [tg, "kf"], [tg])
            V(lambda e, dst=dst: e.tensor_scalar(out=kf, in0=dst, scalar1=-math.pi, scalar2=2.0 * math.pi, op0=ALU.is_lt, op1=ALU.mult), [tg], ["kf"])
            V(TT(dst, dst, kf, ALU.add), [tg, "kf"], [tg])
            A(lambda e, dst=dst: e.activation(out=dst, in_=dst, func=AF.Sin), [tg], [tg])
        V(TT(a_re, mag, cs, ALU.mult), ["mag", "cs"], ["a_re"])
        V(TT(a_im, mag, sn, ALU.mult), ["mag", "sn"], ["a_im"])
        nr, den, fr, fi = cs, sn, mag, ang
        V(lambda e: e.tensor_scalar_add(out=nr, in0=a_re, scalar1=-1.0), ["a_re", "cs"], ["nr"])
        V(TT(t0, lr, lr, ALU.mult), ["lr"], ["t0"])
        V(TT(t1, li, li, ALU.mult), ["li"], ["t1"])
        V(TT(den, t0, t1, ALU.add), ["t0", "t1", "sn"], ["den"])
        V(lambda e: e.reciprocal(out=den, in_=den), ["den"], ["den"])
        V(TT(t0, nr, lr, ALU.mult), ["nr", "lr"], ["t0"])
        V(TT(t1, a_im, li, ALU.mult), ["a_im", "li"], ["t1"])
        V(TT(t0, t0, t1, ALU.add), ["t0", "t1"], ["t0"])
        V(TT(fr, t0, den, ALU.mult), ["t0", "den", "mag"], ["fr"])
        V(TT(t0, a_im, lr, ALU.mult), ["a_im", "lr"], ["t0"])
        V(TT(t1, nr, li, ALU.mult), ["nr", "li"], ["t1"])
        V(TT(t0, t0, t1, ALU.subtract), ["t0", "t1"], ["t0"])
        V(TT(fi, t0, den, ALU.mult), ["t0", "den", "ang"], ["fi"])
        frb = fr.unsqueeze(2).to_broadcast([128, 32, 16])
        fib = fi.unsqueeze(2).to_broadcast([128, 32, 16])
        b0 = big[0].rearrange("p (a c) -> p a c", a=32)
        b1 = big[1].rearrange("p (a c) -> p a c", a=32)
        V(TT(b0, Bm[:, 0, :, :], frb, ALU.mult), ["Bm", "fr"], ["b0"])
        V(TT(b1, Bm[:, 1, :, :], fib, ALU.mult), ["Bm", "fi"], ["b1"])
        V(TT(Bb[:, 0, :, :], b0, b1, ALU.subtract), ["b0", "b1"], ["Bb"])
        V(TT(b0, Bm[:, 1, :, :], frb, ALU.mult), ["Bm", "fr"], ["b0"])
        V(TT(b1, Bm[:, 0, :, :], fib, ALU.mult), ["Bm", "fi"], ["b1"])
        V(TT(Bb[:, 1, :, :], b0, b1, ALU.add), ["b0", "b1"], ["Bb"])
        for qq in range(4):
            S.op("pool", lambda e, qq=qq: e.memset(xpad[qq], 0.0), writes=[("xpad", qq)])
        for P in range(32):
            ch, qq = P // 4, P % 4
            xp = xpad[qq]
            for ri in range(2):
                for g2 in range(2):
                    V(lambda e, xp=xp, ri=ri, g2=g2, P=P, qq=qq: e.tensor_copy(
                        out=xp[g2 * 64:(g2 + 1) * 64, ri, 32 * qq + 16 * g2: 32 * qq + 16 * g2 + 16],
                        in_=Bb[g2 * 64:(g2 + 1) * 64, ri, P, :]), ["Bb"], [("xpad", qq)])
            bb = btb[ch % 2]
            for ri in range(2):
                pb = 6 + (P * 2 + ri) % 2
                S.op("pe", lambda e, xp=xp, ri=ri, pb=pb: e.matmul(self.ps[:, pb, 0:128], lhsT=xp[:, ri, :], rhs=self.identf[:, :],
                                                                     start=True, stop=True),
                     reads=[("xpad", qq), "identf"], writes=[("ps", pb)])
                A(lambda e, bb=bb, qq=qq, ri=ri, pb=pb: e.copy(out=bb[:, qq, ri, :], in_=self.ps[:, pb, 0:128]),
                  [("ps", pb)], [("btb", ch % 2)])
            if qq == 3:
                S.dma("sp", self.bt_d[ch], bb, reads=[("btb", ch % 2)], writes=["bt_d"])
        apw = self.apw
        V(lambda e: e.tensor_copy(out=apw[:, 0, 0, :], in_=a_re), ["a_re"], ["apw"])
        V(lambda e: e.tensor_copy(out=apw[:, 0, 1, :], in_=a_im), ["a_im"], ["apw"])
        for s in range(1, 10):
            pr, pi_, nr_, ni_ = apw[:, s - 1, 0, :], apw[:, s - 1, 1, :], apw[:, s, 0, :], apw[:, s, 1, :]
            V(TT(t0, pr, pr, ALU.mult), ["apw"], ["t0"])
            V(TT(t1, pi_, pi_, ALU.mult), ["apw"], ["t1"])
            V(TT(nr_, t0, t1, ALU.subtract), ["t0", "t1"], ["apw"])
            V(TT(t0, pr, pi_, ALU.mult), ["apw"], ["t0"])
            V(lambda e, ni_=ni_: e.tensor_scalar(out=ni_, in0=t0, scalar1=2.0, scalar2=None, op0=ALU.mult), ["t0"], ["apw"])
        V(TT(t0, a_re, a_re, ALU.mult), ["a_re"], ["t0"])
        V(TT(t1, a_im, a_im, ALU.mult), ["a_im"], ["t1"])
        V(TT(t0, t0, t1, ALU.add), ["t0", "t1"], ["t0"])
        V(lambda e: e.reciprocal(out=t0, in_=t0), ["t0"], ["t0"])
        V(TT(self.ainv[:, 0, :], a_re, t0, ALU.mult), ["a_re", "t0"], ["ainv"])
        V(lambda e: e.scalar_tensor_tensor(out=self.ainv[:, 1, :], in0=a_im, scalar=-1.0, in1=t0, op0=ALU.mult, op1=ALU.mult),
          ["a_im", "t0"], ["ainv"])
        tab = self.tab
        ivr, ivi = lr, li
        V(lambda e: e.tensor_copy(out=ivr, in_=self.ainv[:, 0, :]), ["ainv", "lr"], ["ivr"])
        V(lambda e: e.tensor_copy(out=ivi, in_=self.ainv[:, 1, :]), ["ainv", "li"], ["ivi"])
        for base, getp in ((0, lambda s: (apw[:, s, 0, :], apw[:, s, 1, :])), (2, None)):
            S.op("pool", lambda e, base=base: e.memset(tab[:, base, :, 0:1], 1.0), writes=["tab"])
            S.op("pool", lambda e, base=base: e.memset(tab[:, base + 1, :, 0:1], 0.0), writes=["tab"])
            s = 0
            n = 1
            while n < TC:
                if getp is not None:
                    mr, mi = getp(s)
                else:
                    mr, mi = ivr, ivi
                mrb = mr.unsqueeze(2).to_broadcast([128, 32, n])
                mib = mi.unsqueeze(2).to_broadcast([128, 32, n])
                sr, si = tab[:, base, :, 0:n], tab[:, base + 1, :, 0:n]
                orr, oi = tab[:, base, :, n:2 * n], tab[:, base + 1, :, n:2 * n]
                bb0 = big[0].rearrange("p (a c) -> p a c", a=32)[:, :, 0:n]
                bb1 = big[1].rearrange("p (a c) -> p a c", a=32)[:, :, 0:n]
                V(TT(bb0, sr, mrb, ALU.mult), ["tab", "apw", "ivr"], ["b0"])
                V(TT(bb1, si, mib, ALU.mult), ["tab", "apw", "ivi"], ["b1"])
                V(TT(orr, bb0, bb1, ALU.subtract), ["b0", "b1"], ["tab"])
                V(TT(bb0, sr, mib, ALU.mult), ["tab", "apw", "ivi"], ["b0"])
                V(TT(bb1, si, mrb, ALU.mult), ["tab", "apw", "ivr"], ["b1"])
                V(TT(oi, bb0, bb1, ALU.add), ["b0", "b1"], ["tab"])
                if getp is None:
                    V(TT(t0, ivr, ivr, ALU.mult), ["ivr"], ["t0"])
                    V(TT(t1, ivi, ivi, ALU.mult), ["ivi"], ["t1"])
                    V(TT(t2, ivr, ivi, ALU.mult), ["ivr", "ivi"], ["t2"])
                    V(TT(ivr, t0, t1, ALU.subtract), ["t0", "t1"], ["ivr"])
                    V(lambda e: e.tensor_scalar(out=ivi, in0=t2, scalar1=2.0, scalar2=None, op0=ALU.mult), ["t2"], ["ivi"])
                n *= 2
                s += 1
        if self.has_dbg("apw0"):
            S.dma("sp", self.dout["apw0"], self.apw[:, 0, :, :], reads=["apw"])
        if self.has_dbg("tab"):
            S.dma("sp", self.dout["tab"], self.tab[:, :, :, :], reads=["tab"])
        if self.has_dbg("btd"):
            S.dma("pool", self.dout["btd"], self.bt_d, reads=["bt_d"])
        S.barrier()

    def has_dbg(self, name):
        return any(nm == name for nm, _ in self.dbg)

    def rmsnorm(self, col0, Tn):
        S = self.S
        xT, hT = self.xT, self.hT
        for c in range(8):
            S.op("act", lambda e, c=c: e.activation(out=hT[:, c, :Tn], in_=xT[:, c, :Tn], func=AF.Square),
                 reads=[("xT", c)], writes=[("hT", c)])
        for c in range(8):
            S.op("pe", lambda e, c=c: e.matmul(self.ps[:, 0, :Tn], lhsT=self.ones[:, :], rhs=hT[:, c, :Tn], start=(c == 0), stop=(c == 7)),
                 reads=[("hT", c), "ones"], writes=[("ps", 0)])
        S.op("act", lambda e: e.activation(out=self.rstd[:, :Tn], in_=self.ps[:, 0, :Tn], func=AF.Sqrt, bias=self.eps[:, 0:1]),
             reads=[("ps", 0), "eps"], writes=["rstd"])
        S.op("dve", lambda e: e.reciprocal(out=self.rstd[:, :Tn], in_=self.rstd[:, :Tn]), reads=["rstd"], writes=["rstd"])
        for c in range(8):
            S.op("dve", lambda e, c=c: e.scalar_tensor_tensor(out=hT[:, c, :Tn], in0=xT[:, c, :Tn], scalar=self.vecs[:, col0 + c:col0 + c + 1],
                                                           in1=self.rstd[:, :Tn], op0=ALU.mult, op1=ALU.mult),
                 reads=[("xT", c), "rstd", "vecs"], writes=[("hT", c)])

    def ffn_g(self, nm, Tn, hid, sg, dbase=4):
        S = self.S
        d = self.din
        xT, hT, ps = self.xT, self.hT, self.ps
        wg_d, wu_d, wd_d = d["wg" + nm], d["wu" + nm], d["wd" + nm]
        for pc in range(NPF):
            bg, bu = (0, 1) if pc % 2 == 0 else (2, 0)
            bg = (2 * pc) % 4
            bu = (2 * pc + 1) % 4
            S.dma("pool", self.wA[bg][:, :, :], wg_d[pc], writes=[("wA", bg)], free=True)
            S.dma("pool", self.wA[bu][:, :, :], wu_d[pc], writes=[("wA", bu)], free=True)
            for j in range(2):
                hc = 2 * pc + j
                pb = (hc % 2) * 2
                for k in range(8):
                    S.op("pe", lambda e, k=k, bg=bg, j=j, pb=pb: e.matmul(ps[:, pb, :Tn], lhsT=self.wA[bg][:, k, j * 128:(j + 1) * 128],
                                                                         rhs=hT[:, k, :Tn], start=(k == 0), stop=(k == 7)),
                         reads=[("wA", bg), ("hT", k)], writes=[("ps", pb)])
                for k in range(8):
                    S.op("pe", lambda e, k=k, bu=bu, j=j, pb=pb: e.matmul(ps[:, pb + 1, :Tn], lhsT=self.wA[bu][:, k, j * 128:(j + 1) * 128],
                                                                         rhs=hT[:, k, :Tn], start=(k == 0), stop=(k == 7)),
                         reads=[("wA", bu), ("hT", k)], writes=[("ps", pb + 1)])
                sbi = hc % 2
                S.op("act", lambda e, pb=pb, sbi=sbi: e.activation(out=sg[sbi][:, :Tn], in_=ps[:, pb, :Tn], func=AF.Silu),
                     reads=[("ps", pb)], writes=[("sg", sbi)])
                S.op("dve", lambda e, pb=pb, sbi=sbi, hc=hc: e.tensor_tensor(out=hid[:, hc, :Tn], in0=sg[sbi][:, :Tn], in1=ps[:, pb + 1, :Tn], op=ALU.mult),
                     reads=[("ps", pb + 1), ("sg", sbi)], writes=[("hid", hc)])
            yield
        for half in range(2):
            for pc in range(NPF):
                b = (half * NPF + pc) % 7
                wdb = self.wD[b] if b < 3 else self.wKV[:, 2 * (b - 3):2 * (b - 3) + 2, :]
                S.dma("pool", wdb[:, :, :], wd_d[half, pc], writes=[("wD", b)], free=True)
                for j in range(2):
                    hc = 2 * pc + j
                    for o in range(4):
                        S.op("pe", lambda e, wdb=wdb, j=j, o=o, hc=hc: e.matmul(ps[:, dbase + o, :Tn], lhsT=wdb[:, j, o * 128:(o + 1) * 128],
                                                                           rhs=hid[:, hc, :Tn], start=(hc == 0), stop=(hc == 21)),
                             reads=[("wD", b), ("hid", hc)], writes=[("ps", dbase + o)])
                yield
            for o in range(4):
                c = half * 4 + o
                S.op("dve", lambda e, o=o, c=c: e.scalar_tensor_tensor(out=xT[:, c, :Tn], in0=ps[:, dbase + o, :Tn], scalar=0.5, in1=xT[:, c, :Tn],
                                                                     op0=ALU.mult, op1=ALU.add),
                     reads=[("ps", dbase + o), ("xT", c)], writes=[("xT", c)])

    def ffn(self, *a, **k):
        for _ in self.ffn_g(*a, **k):
            pass

    def ffn_block_g(self, nm, col0, Tn, base=0, dbase=4):
        f32, b16, off = self.carve()
        off[0] = base
        hid = b16(22 * T).rearrange("p (c t) -> p c t", c=22)
        sg = [f32(T), f32(T)]
        self.rmsnorm(col0, Tn)
        yield
        yield from self.ffn_g(nm, Tn, hid, sg, dbase=dbase)
        self.S.barrier()

    def ffn_block(self, *a, **k):
        for _ in self.ffn_block_g(*a, **k):
            pass

    def linear_g(self, w_d, npieces, in_buf, in_tag, nk, Tn, consumer, banks=(0, 1, 2, 3), wtag="wA", pieces=None):
        S = self.S
        for pc in range(npieces):
            if pieces is not None and pc not in pieces:
                continue
            b = self._wrot
            self._wrot = (self._wrot + 1) % 4
            S.dma("pool", self.wA[b][:, 0:nk, :], w_d[pc], writes=[("wA", b)], free=True)
            for j in range(2):
                oc = 2 * pc + j
                bank = banks[oc % len(banks)]
                for k in range(nk):
                    S.op("pe", lambda e, k=k, b=b, j=j, bank=bank: e.matmul(self.ps[:, bank, :Tn], lhsT=self.wA[b][:, k, j * 128:(j + 1) * 128],
                                                                           rhs=in_buf[:, k, :Tn], start=(k == 0), stop=(k == nk - 1)),
                         reads=[("wA", b), (in_tag, k)], writes=[("ps", bank)])
                consumer(oc, bank)
            yield

    def linear(self, *a, **k):
        for _ in self.linear_g(*a, **k):
            pass

    _wrot = 0

    def qknorm(self, bank, Tn, gcol, out_ap, out_tag, nb=7, bufs=None):
        S = self.S
        ps = self.ps
        if bufs is None:
            sqb, rstd, tg = self.sqb, self.rstd, ""
        else:
            i = self._qkrot % len(bufs)
            self._qkrot += 1
            sqb, rstd, nb = bufs[i]
            tg = "_%d" % i
        S.op("act", lambda e: e.activation(out=sqb[:, :Tn], in_=ps[:, bank, :Tn], func=AF.Square),
             reads=[("ps", bank)], writes=["sqb" + tg])
        S.op("pe", lambda e: e.matmul(ps[:, nb, :Tn], lhsT=self.onesblk[:, :], rhs=sqb[:, :Tn], start=True, stop=True),
             reads=["sqb" + tg, "onesblk"], writes=[("ps", nb)])
        S.op("act", lambda e: e.activation(out=rstd[:, :Tn], in_=ps[:, nb, :Tn], func=AF.Sqrt, bias=self.eps[:, 0:1]),
             reads=[("ps", nb), "eps"], writes=["rstd" + tg])
        S.op("dve", lambda e: e.reciprocal(out=rstd[:, :Tn], in_=rstd[:, :Tn]), reads=["rstd" + tg], writes=["rstd" + tg])
        S.op("dve", lambda e: e.scalar_tensor_tensor(out=out_ap, in0=ps[:, bank, :Tn], scalar=self.vecs[:, gcol:gcol + 1], in1=rstd[:, :Tn],
                                                     op0=ALU.mult, op1=ALU.mult),
             reads=[("ps", bank), "rstd" + tg, "vecs"], writes=[out_tag])

    _qkrot = 0

    def tile(self, ti):
        S = self.S
        d = self.din
        own = ti >= self.n_halo
        oi = ti - self.n_halo
        last = ti == self.NT - 1
        xT = self.xT
        if ti not in self.pre_ffn1:
            S.dma("sp", xT[:, :, :], d["xT"][:, ti * T:(ti + 1) * T].rearrange("(c p) t -> p c t", p=128),
                  writes=[("xT", c) for c in range(8)])
            if self.level < 1:
                return
            self.ffn_block("1", 0, T)
        if self.level < 2:
            return
        self.dbg_dump("x1T", ti, lambda dd: S.dma("sp", dd.rearrange("(c p) t -> p c t", p=128), xT[:, :, :], reads=[("xT", c) for c in range(8)]))
        f32, b16, off = self.carve()
        self.rmsnorm(8, T)
        u_bf = b16(8 * T).rearrange("p (c t) -> p c t", c=8)
        qT = b16(6 * T).rearrange("p (c t) -> p c t", c=6)
        def cons_u(oc, bank):
            S.op("act", lambda e: e.copy(out=u_bf[:, oc, :], in_=self.ps[:, bank, :]), reads=[("ps", bank)], writes=[("u", oc)])
        self.linear(d["w_u"], 4, self.hT, "hT", 8, T, cons_u)
        need_g = [g for g in range(3) if own or (self.n_halo - ti - 1) * T + 1 <= WIN[g]]
        self.need_g = need_g
        top = self.scr.shape[1] - 3 * (T + T // 2)
        qkb = []
        for i in range(3):
            o_ = top + i * (T + T // 2)
            qkb.append((self.scr[:, o_ + T:o_ + T + T // 2].bitcast(BF16), self.scr[:, o_:o_ + T], 5 + i))

        def cons_k(oc, bank):
            g, hh = oc // 2, oc % 2
            if g not in need_g:
                return
            if g < 2:
                dst = self.kT[g][:, hh, (ti % 2) * T:(ti % 2 + 1) * T]
            else:
                dst = self.kT[2][:, hh, ti * T:(ti + 1) * T]
            self.qknorm(bank, T, 33, dst, ("kT", g), bufs=qkb)
        self.linear(d["w_k"], 3, self.hT, "hT", 8, T, cons_k, pieces=need_g)
        if own:
            def cons_q(oc, bank):
                self.qknorm(bank, T, 32, qT[:, oc, :], ("qT", oc), bufs=qkb)
            self.linear(d["w_q"], 3, self.hT, "hT", 8, T, cons_q)
        ysm = b16(8 * T).rearrange("p (c t) -> p c t", c=8)
        self.ysm = ysm
        obT = b16(2 * T).rearrange("p (c t) -> p c t", c=2)
        mark = off[0]
        if self.level < 3:
            S.barrier(); return
        self.kv_tokmajor(ti, f32, b16)
        S.barrier()
        off[0] = mark
        if self.level < 4:
            return
        self.ssm_tile(ti, u_bf, f32, b16, own)
        if self.has_dbg("ysm_%d" % ti):
            S.dma("pool", self.dout["ysm_%d" % ti].rearrange("(c p) t -> p c t", p=128), ysm, reads=[("ysm", c) for c in range(8)])
        if self.has_dbg("u_%d" % ti):
            S.dma("pool", self.dout["u_%d" % ti].rearrange("(c p) t -> p c t", p=128), u_bf, reads=[("u", c) for c in range(8)])
        if self.level < 5:
            S.barrier(); return
        if own:
            S.barrier()
            off[0] = mark
            self.attention(ti, qT, obT, f32, b16)
            if self.has_dbg("obT_%d" % ti):
                S.dma("pool", self.dout["obT_%d" % ti].rearrange("(c p) t -> p c t", p=128), obT, reads=[("obT", c) for c in range(2)])
            if self.level < 6:
                S.barrier(); return
            self.mix_out(ti, obT, f32, b16)
        S.barrier()
        if own:
            self.ffn_block("2", 16, T)
            S.dma("sp", self.dout["yT"][:, oi * T:(oi + 1) * T].rearrange("(c p) t -> p c t", p=128), xT[:, :, :],
                  reads=[("xT", c) for c in range(8)])
        if last:
            f32, b16, off = self.carve()
            hr, hi_, t0, t1 = f32(32), f32(32), f32(32), f32(32)
            self.cmul_small(hr, hi_, self.Kc[:, 0, :], self.Kc[:, 1, :], self.ainv[:, 0, :], self.ainv[:, 1, :], t0, t1, "Kc", "ainv", "pstv")
            S.dma("sp", self.dout["pst"][:, 0, :], hr, reads=["pstv"])
            S.dma("sp", self.dout["pst"][:, 1, :], hi_, reads=["pstv"])
            S.barrier()

    HALO_ABASE = 4096 + 6448 + 816

    def haloA_g(self, ti, u_bf, par):
        S = self.S
        d = self.din
        abase = self.HALO_ABASE
        S.barrier()
        S.dma("sp", self.xT[:, :, :], d["xT"][:, ti * T:(ti + 1) * T].rearrange("(c p) t -> p c t", p=128),
              writes=[("xT", c) for c in range(8)])
        yield from self.ffn_block_g("1", 0, T, base=abase, dbase=0)
        self.rmsnorm(8, T)
        yield

        def cons_u(oc, bank):
            S.op("act", lambda e: e.copy(out=u_bf[:, oc, :], in_=self.ps[:, bank, :]), reads=[("ps", bank)], writes=[("u", par, oc)])
        yield from self.linear_g(d["w_u"], 4, self.hT, "hT", 8, T, cons_u, banks=(0, 1, 2))
        need_g = [g for g in range(3) if (self.n_halo - ti - 1) * T + 1 <= WIN[g]]
        self.need_g = need_g

        def cons_k(oc, bank):
            g, hh = oc // 2, oc % 2
            if g < 2:
                dst = self.kT[g][:, hh, (ti % 2) * T:(ti % 2 + 1) * T]
            else:
                dst = self.kT[2][:, hh, ti * T:(ti + 1) * T]
            self.qknorm(bank, T, 33, dst, ("kT", g), nb=3)
        yield from self.linear_g(d["w_k"], 3, self.hT, "hT", 8, T, cons_k, banks=(0, 1, 2), pieces=need_g)
        f32, b16, off = self.carve()
        off[0] = abase
        yield from self.kv_tokmajor_g(ti, f32, b16, b0_fixed=0)

    def ownpre_g(self, ti):
        S = self.S
        S.barrier()
        S.dma("sp", self.xT[:, :, :], self.din["xT"][:, ti * T:(ti + 1) * T].rearrange("(c p) t -> p c t", p=128),
              writes=[("xT", c) for c in range(8)])
        yield from self.ffn_block_g("1", 0, T, base=self.HALO_ABASE, dbase=0)

    def ssm_halo_g(self, ti, u_bf, par):
        S = self.S
        ps = self.ps
        tab = self.tab
        TT = lambda o, a, b, op: (lambda e: e.tensor_tensor(out=o, in0=a, in1=b, op=op))
        V = lambda fn, r, w: S.op("dve", fn, reads=r, writes=w)
        f32, b16, off = self.carve()
        off[0] = 4096
        T2 = 2 * T
        btb = b16(4 * 2 * 128).rearrange("p (a r c) -> p a r c", a=4, r=2)
        X = [f32(T2), f32(T2)]
        ta, tb = f32(T2), f32(T2)
        G1 = [b16(T2), b16(T2)]
        G = [G1] * 8
        NZ = NCH + 1
        Z = [[f32(16 * NZ).rearrange("p (a c) -> p a c", a=16) for _ in range(2)] for _ in range(2)]
        zt = [f32(16 * NZ).rearrange("p (a c) -> p a c", a=16) for _ in range(2)]
        assert off[0] <= self.HALO_ABASE, off[0]
        v4 = lambda a: a.rearrange("p (j c l) -> p j c l", j=2, l=TC)
        for cp in range(2):
            for sub in range(4):
                ch = 4 * cp + sub
                S.dma("sp", btb, self.bt_d[ch], reads=["bt_d"], writes=["h_btb"])
                for bt in range(2):
                    b4 = 2 * sub + bt
                    P0 = 4 * ch + 2 * bt
                    for j in range(2):
                        qq = 2 * bt + j
                        for ri in range(2):
                            bank = 4 + 2 * j + ri
                            S.op("pe", lambda e, qq=qq, ri=ri, bank=bank, ch=ch: e.matmul(ps[:, bank, :], lhsT=btb[:, qq, ri, :], rhs=u_bf[:, ch, :], start=True, stop=True),
                                 reads=["h_btb", ("u", par, ch)], writes=[("ps", bank)])
                            S.op("act", lambda e, ri=ri, j=j, bank=bank: e.copy(out=X[ri][:, j * T:(j + 1) * T], in_=ps[:, bank, :]),
                                 reads=[("ps", bank)], writes=[("hX", ri)])
                    ivr = tab[:, 2, P0:P0 + 2, :].unsqueeze(2).to_broadcast([128, 2, NCH, TC])
                    ivi = tab[:, 3, P0:P0 + 2, :].unsqueeze(2).to_broadcast([128, 2, NCH, TC])
                    gr, gi = G[b4]
                    V(TT(v4(ta), v4(X[0]), ivr, ALU.mult), [("hX", 0), "tab"], ["h_ta"])
                    V(TT(v4(tb), v4(X[1]), ivi, ALU.mult), [("hX", 1), "tab"], ["h_tb"])
                    V(TT(ta, ta, tb, ALU.subtract), ["h_ta", "h_tb"], ["h_ta"])
                    V(lambda e, gr=gr: e.tensor_tensor_scan(out=gr, data0=self.scanmask[:, :], data1=ta, initial=0.0, op0=ALU.mult, op1=ALU.add),
                      ["h_ta", "scanmask"], [("hG", 0)])
                    V(TT(v4(ta), v4(X[1]), ivr, ALU.mult), [("hX", 1), "tab"], ["h_ta"])
                    V(TT(v4(tb), v4(X[0]), ivi, ALU.mult), [("hX", 0), "tab"], ["h_tb"])
                    V(TT(ta, ta, tb, ALU.add), ["h_ta", "h_tb"], ["h_ta"])
                    V(lambda e, gi=gi: e.tensor_tensor_scan(out=gi, data0=self.scanmask[:, :], data1=ta, initial=0.0, op0=ALU.mult, op1=ALU.add),
                      ["h_ta", "scanmask"], [("hG", 1)])
                    for ri in range(2):
                        ge = G[b4][ri].rearrange("p (j c l) -> p j c l", j=2, l=TC)[:, :, :, TC - 1]
                        V(lambda e, b4=b4, ri=ri, ge=ge: e.tensor_copy(out=zt[ri][:, 2 * b4:2 * b4 + 2, 1:NZ], in_=ge), [("hG", ri)], ["h_zt"])
                    yield
            Pa = slice(16 * cp, 16 * cp + 16)
            s0 = int(math.log2(TC))
            zr, zi = Z[0]
            V(lambda e, Pa=Pa: e.tensor_copy(out=zr[:, :, 0:1], in_=self.Kc[:, 0, Pa].unsqueeze(2)), ["Kc"], ["hZ0"])
            V(lambda e, Pa=Pa: e.tensor_copy(out=zi[:, :, 0:1], in_=self.Kc[:, 1, Pa].unsqueeze(2)), ["Kc"], ["hZ0"])
            self.cmul_b(zr[:, :, 1:NZ], zi[:, :, 1:NZ], zt[0][:, :, 1:NZ], zt[1][:, :, 1:NZ], s0, Pa, NZ - 1, zt, "h_zt", "hZ0", add=None)
            yield
            cur = 0
            st = 1
            sidx = s0
            while st < NZ:
                a, b = Z[cur], Z[1 - cur]
                n = NZ - st
                V(lambda e, a=a, b=b, st=st: e.tensor_copy(out=b[0][:, :, 0:st], in_=a[0][:, :, 0:st]), ["hZ%d" % cur], ["hZ%d" % (1 - cur)])
                V(lambda e, a=a, b=b, st=st: e.tensor_copy(out=b[1][:, :, 0:st], in_=a[1][:, :, 0:st]), ["hZ%d" % cur], ["hZ%d" % (1 - cur)])
                self.cmul_b(b[0][:, :, st:NZ], b[1][:, :, st:NZ], a[0][:, :, 0:n], a[1][:, :, 0:n], sidx, Pa, n, zt, "hZ%d" % cur, "hZ%d" % (1 - cur),
                            add=(a[0][:, :, st:NZ], a[1][:, :, st:NZ]), ztg="h_zt")
                cur = 1 - cur
                st *= 2
                sidx += 1
            zf = Z[cur]
            ztag = "hZ%d" % cur
            V(lambda e, zf=zf, Pa=Pa: e.tensor_copy(out=self.Kc[:, 0, Pa].unsqueeze(2), in_=zf[0][:, :, NZ - 1:NZ]), [ztag], ["Kc"])
            V(lambda e, zf=zf, Pa=Pa: e.tensor_copy(out=self.Kc[:, 1, Pa].unsqueeze(2), in_=zf[1][:, :, NZ - 1:NZ]), [ztag], ["Kc"])
            yield

    def halo_phase(self):
        S = self.S
        f32, b16, off = self.carve()
        u_bufs = [b16(8 * T).rearrange("p (c t) -> p c t", c=8) for _ in range(2)]
        assert off[0] == 4096
        nh = self.n_halo
        rec = getattr(self, "setup_rec", [])
        ri_ = 0
        per = max(1, (len(rec) + 59) // 60)

        def replay(n):
            nonlocal ri_
            for _ in range(n):
                if ri_ >= len(rec):
                    return
                kind, a, k = rec[ri_]
                ri_ += 1
                if kind == "op":
                    S.op(*a, **k)
                elif kind == "dma":
                    S.dma(*a, **k)
                else:
                    S.barrier()
        for _ in self.haloA_g(0, u_bufs[0], 0):
            replay(per)
        replay(len(rec))
        for ti in range(nh):
            gs = self.ssm_halo_g(ti, u_bufs[ti % 2], ti % 2)
            if ti + 1 < nh:
                ga = self.haloA_g(ti + 1, u_bufs[(ti + 1) % 2], (ti + 1) % 2)
            elif self.n_own > 0:
                ga = self.ownpre_g(nh)
                self.pre_ffn1.add(nh)
            else:
                ga = None
            s_done = False
            a_done = ga is None
            while not (s_done and a_done):
                if not s_done:
                    try:
                        next(gs)
                    except StopIteration:
                        s_done = True
                for _ in range(3):
                    if not a_done:
                        try:
                            next(ga)
                        except StopIteration:
                            a_done = True
        S.barrier()

    def cmul_small(self, orr, oi, ar, ai, br, bi, t0, t1, ta, tb, tout):
        S = self.S
        TT = lambda o, a, b, op: (lambda e: e.tensor_tensor(out=o, in0=a, in1=b, op=op))
        S.op("dve", TT(t0, ar, br, ALU.mult), [ta, tb], ["cm_t0"])
        S.op("dve", TT(t1, ai, bi, ALU.mult), [ta, tb], ["cm_t1"])
        S.op("dve", TT(orr, t0, t1, ALU.subtract), ["cm_t0", "cm_t1"], [tout])
        S.op("dve", TT(t0, ar, bi, ALU.mult), [ta, tb, tout], ["cm_t0"])
        S.op("dve", TT(t1, ai, br, ALU.mult), [ta, tb, tout], ["cm_t1"])
        S.op("dve", TT(oi, t0, t1, ALU.add), ["cm_t0", "cm_t1"], [tout])

    def dbg_dump(self, name, ti, fn):
        for nm, shape in self.dbg:
            if nm == "%s_%d" % (name, ti):
                fn(self.dout[nm])

    def kv_tokmajor_g(self, ti, f32, b16, b0_fixed=None):
        S0 = self.S
        import os
        lim = int(os.environ.get("KVN", "100000"))
        cnt = [0]

        class _W:
            def op(self_, eng, fn, reads=(), writes=()):
                if eng != "pe":
                    cnt[0] += 1
                    if cnt[0] > lim:
                        return
                    if cnt[0] == lim:
                        print("LAST OP", eng, reads, writes)
                S0.op(eng, fn, reads, writes)

            def dma(self_, *a, **k):
                S0.dma(*a, **k)
        S = _W()
        d = self.din
        ps, hT = self.ps, self.hT
        own = ti >= self.n_halo
        oi = ti - self.n_halo
        kvf = [f32(4 * 512).rearrange("p (u c) -> p u c", u=4) for _ in range(2 if b0_fixed is None else 1)]
        sq = f32(4 * 256).rearrange("p (u c) -> p u c", u=4)
        ssq = f32(16)
        vst = b16(16 * 4 * 66).rearrange("p (r h c) -> p r h c", r=16, h=4)
        S.op("pool", lambda e: e.memset(vst[0:32, :, :, 64:66], 1.0), writes=["vst"])
        bi = [0]
        import os
        for g in self.need_g:
            dl = DIL[g]
            S.dma("pool", self.wKV[:, :, :], d["w_kv"][g], writes=["wKV"] + [("wD", 3), ("wD", 4), ("wD", 5), ("wD", 6)])
            if g < 2:
                M, batches = 128, [[0, 1], [2, 3]]
            else:
                M, batches = 32, [[0, 1, 2, 3], [4, 5, 6, 7], [8, 9, 10, 11], [12, 13, 14, 15]]
            for units in batches:
                nu = len(units)
                b0 = (4 if (bi[0] % 2 == 0) else 0) if b0_fixed is None else b0_fixed
                kb = kvf[bi[0] % len(kvf)]
                bi[0] += 1
                for ui, u in enumerate(units):
                    if g == 0:
                        cols = slice(u * 128, (u + 1) * 128)
                    else:
                        cols = slice(u, T, dl)
                    for k in range(8):
                        S.op("pe", lambda e, k=k, ui=ui, cols=cols, b0=b0, M=M: e.matmul(ps[0:M, b0 + ui, :], lhsT=hT[:, k, cols], rhs=self.wKV[:, k, :],
                                                                                        start=(k == 0), stop=(k == 7)),
                             reads=["wKV", ("hT", k)] + [("wD", 3), ("wD", 4), ("wD", 5), ("wD", 6)], writes=[("ps", b0 + ui)])
                rd = [("ps", b0 + ui) for ui in range(nu)]
                pk = ps[0:M, b0:b0 + nu, 0:256]
                pv = ps[0:M, b0:b0 + nu, 256:512]
                S.op("act", lambda e, pk=pk, M=M, nu=nu: e.activation(out=sq[0:M, 0:nu, :], in_=pk, func=AF.Square), reads=rd, writes=["kv_sq"])
                S.op("dve", lambda e, M=M, nu=nu: e.tensor_reduce(out=ssq[0:M, 0:nu * 4], in_=sq[0:M, 0:nu, :].rearrange("p u (h c) -> p (u h) c", h=4),
                                                                 axis=AX.X, op=ALU.add), reads=["kv_sq"], writes=["kv_ssq"])
                S.op("act", lambda e, M=M, nu=nu: e.activation(out=ssq[0:M, 0:nu * 4], in_=ssq[0:M, 0:nu * 4], func=AF.Sqrt, bias=self.eps[0:M, 0:1],
                                                               scale=1.0 / 64), reads=["kv_ssq", "eps"], writes=["kv_ssq"])
                S.op("dve", lambda e, M=M, nu=nu: e.reciprocal(out=ssq[0:M, 0:nu * 4], in_=ssq[0:M, 0:nu * 4]), reads=["kv_ssq"], writes=["kv_ssq"])
                kbk = kb[0:M, 0:nu, 0:256].rearrange("p u (h c) -> p u h c", h=4)
                S.op("dve", lambda e, pk=pk, M=M, nu=nu, kbk=kbk: e.tensor_tensor(
                    out=kbk, in0=pk.rearrange("p u (h c) -> p u h c", h=4),
                    in1=ssq[0:M, 0:nu * 4].rearrange("p (u h) -> p u h", h=4).unsqueeze(3).to_broadcast([M, nu, 4, 64]), op=ALU.mult),
                    reads=rd + ["kv_ssq"], writes=[("kvf", id(kb))])
                S.op("dve", lambda e, M=M, nu=nu, kb=kb: e.tensor_tensor(
                    out=kb[0:M, 0:nu, 0:256], in0=kb[0:M, 0:nu, 0:256], in1=self.gain_bc[0:M, 1:2, :].to_broadcast([M, nu, 256]), op=ALU.mult),
                    reads=[("kvf", id(kb)), "gain_bc"], writes=[("kvf", id(kb))])
                S.op("act", lambda e, pv=pv, M=M, nu=nu, kb=kb: e.copy(out=kb[0:M, 0:nu, 256:512], in_=pv), reads=rd, writes=[("kvf", id(kb))])
                for ui, u in enumerate(units):
                    src = ps[0:M, b0 + ui, 256:512].rearrange("p (h c) -> p h c", h=4)
                    if g == 0:
                        dst, tg = self.V[0][:, (4 * ti + u) % 8, :, 0:64], ("V", 0)
                    elif g == 1:
                        dst, tg = self.V[1][:, (ti % 2) * 4 + u, :, 0:64], ("V", 1)
                    else:
                        dst, tg = vst[0:32, u, :, 0:64], "vst"
                    S.op("act", lambda e, dst=dst, src=src: e.copy(out=dst, in_=src), reads=[("ps", b0 + ui)], writes=[tg])
                import os
                if own and not (int(os.environ.get("KVL", "0")) & 1):
                    for ui, u in enumerate(units):
                        if g == 0:
                            rows = self.dout["pkv"][0, oi * T + u * 128: oi * T + (u + 1) * 128, :]
                        else:
                            rows = self.dout["pkv"][g, oi * T + u: (oi + 1) * T: dl, :]
                        S.dma("sp", rows, kb[0:M, ui, :], reads=[("kvf", id(kb))])
                yield
            if g == 2 and not (int(os.environ.get("KVL", "0")) & 2):
                mt, q4 = ti // 4, ti % 4
                s0 = (mt % 2) * 16
                S.dma("sp", self.V[2][32 * q4:32 * q4 + 32, s0:s0 + 16, :, :], vst[0:32, :, :, :], reads=["vst"], writes=[("V", 2)])

    def kv_tokmajor(self, *a, **k):
        for _ in self.kv_tokmajor_g(*a, **k):
            pass

    def ssm_tile(self, ti, u_bf, f32, b16, own):
        S = self.S
        ps = self.ps
        tab = self.tab
        TT = lambda o, a, b, op: (lambda e: e.tensor_tensor(out=o, in0=a, in1=b, op=op))
        V = lambda fn, r, w: S.op("dve", fn, reads=r, writes=w)
        ysm = self.ysm
        T2 = 2 * T
        btb = b16(4 * 2 * 128).rearrange("p (a r c) -> p a r c", a=4, r=2)
        ctb = b16(2 * 4 * 128).rearrange("p (r a c) -> p r a c", r=2, a=4)
        X = [f32(T2), f32(T2)]
        ta, tb = f32(T2), f32(T2)
        G = [[b16(T2), b16(T2)] for _ in range(4)]
        hb = [[b16(T2), b16(T2)] for _ in range(2)]
        NZ = NCH + 1
        Z = [[f32(8 * NZ).rearrange("p (a c) -> p a c", a=8) for _ in range(2)] for _ in range(2)]
        zt = [f32(8 * NZ).rearrange("p (a c) -> p a c", a=8) for _ in range(2)]
        v4 = lambda a: a.rearrange("p (j c l) -> p j c l", j=2, l=TC)
        for cp in range(4):
            for sub in range(2):
                ch = 2 * cp + sub
                S.dma("sp", btb, self.bt_d[ch], reads=["bt_d"], writes=["btb"])
                for bt in range(2):
                    b4 = 2 * sub + bt
                    P0 = 4 * ch + 2 * bt
                    for j in range(2):
                        qq = 2 * bt + j
                        for ri in range(2):
                            bank = 4 + 2 * j + ri
                            S.op("pe", lambda e, qq=qq, ri=ri, bank=bank, ch=ch: e.matmul(ps[:, bank, :], lhsT=btb[:, qq, ri, :], rhs=u_bf[:, ch, :], start=True, stop=True),
                                 reads=["btb", ("u", ch)], writes=[("ps", bank)])
                            S.op("act", lambda e, ri=ri, j=j, bank=bank: e.copy(out=X[ri][:, j * T:(j + 1) * T], in_=ps[:, bank, :]),
                                 reads=[("ps", bank)], writes=[("X", ri)])
                    ivr = tab[:, 2, P0:P0 + 2, :].unsqueeze(2).to_broadcast([128, 2, NCH, TC])
                    ivi = tab[:, 3, P0:P0 + 2, :].unsqueeze(2).to_broadcast([128, 2, NCH, TC])
                    gr, gi = G[b4]
                    V(TT(v4(ta), v4(X[0]), ivr, ALU.mult), [("X", 0), "tab"], ["ta"])
                    V(TT(v4(tb), v4(X[1]), ivi, ALU.mult), [("X", 1), "tab"], ["tb"])
                    V(TT(ta, ta, tb, ALU.subtract), ["ta", "tb"], ["ta"])
                    V(lambda e, gr=gr: e.tensor_tensor_scan(out=gr, data0=self.scanmask[:, :], data1=ta, initial=0.0, op0=ALU.mult, op1=ALU.add),
                      ["ta", "scanmask"], [("G", b4, 0)])
                    V(TT(v4(ta), v4(X[1]), ivr, ALU.mult), [("X", 1), "tab"], ["ta"])
                    V(TT(v4(tb), v4(X[0]), ivi, ALU.mult), [("X", 0), "tab"], ["tb"])
                    V(TT(ta, ta, tb, ALU.add), ["ta", "tb"], ["ta"])
                    V(lambda e, gi=gi: e.tensor_tensor_scan(out=gi, data0=self.scanmask[:, :], data1=ta, initial=0.0, op0=ALU.mult, op1=ALU.add),
                      ["ta", "scanmask"], [("G", b4, 1)])
            Pa = slice(8 * cp, 8 * cp + 8)
            s0 = int(math.log2(TC))
            zr, zi = Z[0]
            V(lambda e, Pa=Pa: e.tensor_copy(out=zr[:, :, 0:1], in_=self.Kc[:, 0, Pa].unsqueeze(2)), ["Kc"], ["Z0"])
            V(lambda e, Pa=Pa: e.tensor_copy(out=zi[:, :, 0:1], in_=self.Kc[:, 1, Pa].unsqueeze(2)), ["Kc"], ["Z0"])
            for b4 in range(4):
                for ri in range(2):
                    ge = G[b4][ri].rearrange("p (j c l) -> p j c l", j=2, l=TC)[:, :, :, TC - 1]
                    V(lambda e, b4=b4, ri=ri, ge=ge: e.tensor_copy(out=zt[ri][:, 2 * b4:2 * b4 + 2, 1:NZ], in_=ge), [("G", b4, ri)], ["zt"])
            self.cmul_b(zr[:, :, 1:NZ], zi[:, :, 1:NZ], zt[0][:, :, 1:NZ], zt[1][:, :, 1:NZ], s0, Pa, NZ - 1, zt, "zt", "Z0", add=None)
            cur = 0
            st = 1
            sidx = s0
            while st < NZ:
                a, b = Z[cur], Z[1 - cur]
                n = NZ - st
                V(lambda e, a=a, b=b, st=st: e.tensor_copy(out=b[0][:, :, 0:st], in_=a[0][:, :, 0:st]), ["Z%d" % cur], ["Z%d" % (1 - cur)])
                V(lambda e, a=a, b=b, st=st: e.tensor_copy(out=b[1][:, :, 0:st], in_=a[1][:, :, 0:st]), ["Z%d" % cur], ["Z%d" % (1 - cur)])
                self.cmul_b(b[0][:, :, st:NZ], b[1][:, :, st:NZ], a[0][:, :, 0:n], a[1][:, :, 0:n], sidx, Pa, n, zt, "Z%d" % cur, "Z%d" % (1 - cur),
                            add=(a[0][:, :, st:NZ], a[1][:, :, st:NZ]))
                cur = 1 - cur
                st *= 2
                sidx += 1
            zf = Z[cur]
            ztag = "Z%d" % cur
            V(lambda e, zf=zf, Pa=Pa: e.tensor_copy(out=self.Kc[:, 0, Pa].unsqueeze(2), in_=zf[0][:, :, NZ - 1:NZ]), [ztag], ["Kc"])
            V(lambda e, zf=zf, Pa=Pa: e.tensor_copy(out=self.Kc[:, 1, Pa].unsqueeze(2), in_=zf[1][:, :, NZ - 1:NZ]), [ztag], ["Kc"])
            if not own:
                continue
            for sub in range(2):
                ch = 2 * cp + sub
                S.dma("pool", ctb, self.din["ssm_CT"][:, :, 4 * ch:4 * ch + 4, :], writes=["ctb"])
                for bt in range(2):
                    b4 = 2 * sub + bt
                    P0 = 4 * ch + 2 * bt
                    fr = tab[:, 0, P0:P0 + 2, :].unsqueeze(2).to_broadcast([128, 2, NCH, TC])
                    fi = tab[:, 1, P0:P0 + 2, :].unsqueeze(2).to_broadcast([128, 2, NCH, TC])
                    gr, gi = G[b4]
                    kr = zf[0][:, 2 * b4:2 * b4 + 2, 0:NCH].unsqueeze(3).to_broadcast([128, 2, NCH, TC])
                    ki = zf[1][:, 2 * b4:2 * b4 + 2, 0:NCH].unsqueeze(3).to_broadcast([128, 2, NCH, TC])
                    S.op("pool", TT(v4(gr), v4(gr), kr, ALU.add), [("G", b4, 0), ztag], [("G", b4, 0)])
                    S.op("pool", TT(v4(gi), v4(gi), ki, ALU.add), [("G", b4, 1), ztag], [("G", b4, 1)])
                    V(TT(v4(X[0]), v4(gr), fr, ALU.mult), [("G", b4, 0), "tab"], [("X", 0)])
                    V(TT(v4(X[1]), v4(gi), fi, ALU.mult), [("G", b4, 1), "tab"], [("X", 1)])
                    V(TT(hb[bt][0], X[0], X[1], ALU.subtract), [("X", 0), ("X", 1)], [("hb", bt, 0)])
                    V(TT(v4(X[0]), v4(gi), fr, ALU.mult), [("G", b4, 1), "tab"], [("X", 0)])
                    V(TT(v4(X[1]), v4(gr), fi, ALU.mult), [("G", b4, 0), "tab"], [("X", 1)])
                    V(lambda e, bt=bt: e.scalar_tensor_tensor(out=hb[bt][1], in0=X[0], scalar=-1.0, in1=X[1], op0=ALU.mult, op1=ALU.subtract),
                      [("X", 0), ("X", 1)], [("hb", bt, 1)])
                yb = 2 + ch % 2
                i = 0
                for qq in range(4):
                    bt, j = qq // 2, qq % 2
                    for ri in range(2):
                        S.op("pe", lambda e, qq=qq, ri=ri, yb=yb, i=i, bt=bt, j=j: e.matmul(ps[:, yb, :], lhsT=ctb[:, ri, qq, :], rhs=hb[bt][ri][:, j * T:(j + 1) * T],
                                                                                       start=(i == 0), stop=False),
                             reads=["ctb", ("hb", bt, ri)], writes=[("ps", yb)])
                        i += 1
                S.op("pe", lambda e, yb=yb, ch=ch: e.matmul(ps[:, yb, :], lhsT=self.diagD[:, ch, :], rhs=u_bf[:, ch, :], start=False, stop=True),
                     reads=["diagD", ("u", ch)], writes=[("ps", yb)])
                self.gelu_evict(yb, ysm[:, ch, :], ("ysm", ch), ta[:, 0:T], tb[:, 0:T])

    def gelu_evict(self, bank, out_ap, out_tag, t1, t2, Tn=T):
        S = self.S
        ps = self.ps
        S.op("act", lambda e: e.activation(out=t1, in_=ps[:, bank, 0:Tn], func=AF.Square), reads=[("ps", bank)], writes=["ta"])
        S.op("dve", lambda e: e.tensor_scalar(out=t1, in0=t1, scalar1=0.044715, scalar2=1.0, op0=ALU.mult, op1=ALU.add), reads=["ta"], writes=["ta"])
        S.op("dve", lambda e: e.tensor_tensor(out=t1, in0=t1, in1=ps[:, bank, 0:Tn], op=ALU.mult), reads=["ta", ("ps", bank)], writes=["ta"])
        S.op("act", lambda e: e.activation(out=t2, in_=t1, func=AF.Sigmoid, scale=1.5957691216057308), reads=["ta"], writes=["tb"])
        S.op("dve", lambda e: e.tensor_tensor(out=out_ap, in0=t2, in1=ps[:, bank, 0:Tn], op=ALU.mult), reads=["tb", ("ps", bank)], writes=[out_tag])

    def cmul_b(self, orr, oi, ar, ai, sidx, Pa, n, zt, tin, tout, add=None, ztg="zt"):
        S = self.S
        TT = lambda o, a, b, op: (lambda e: e.tensor_tensor(out=o, in0=a, in1=b, op=op))
        mr = self.apw[:, sidx, 0, Pa].unsqueeze(2).to_broadcast([128, orr.shape[1], n])
        mi = self.apw[:, sidx, 1, Pa].unsqueeze(2).to_broadcast([128, orr.shape[1], n])
        x0, x1 = zt[0][:, :, 0:n], zt[1][:, :, 0:n]
        if add is None:
            S.op("dve", TT(orr, ar, mr, ALU.mult), [tin, "apw"], [tout])
            S.op("dve", TT(oi, ai, mi, ALU.mult), [tin, "apw"], [tout])
            S.op("dve", TT(orr, orr, oi, ALU.subtract), [tout], [tout])
            S.op("dve", TT(oi, ar, mi, ALU.mult), [tin, "apw"], [tout])
            S.op("dve", TT(ar, ai, mr, ALU.mult), [tin, "apw"], [tin])
            S.op("dve", TT(oi, oi, ar, ALU.add), [tout, tin], [tout])
            return
        S.op("dve", TT(x0, ar, mr, ALU.mult), [tin, "apw"], [ztg])
        S.op("dve", TT(x1, ai, mi, ALU.mult), [tin, "apw"], [ztg])
        S.op("dve", TT(x0, x0, x1, ALU.subtract), [ztg], [ztg])
        S.op("dve", TT(orr, x0, add[0], ALU.add), [ztg, tin], [tout])
        S.op("dve", TT(x0, ar, mi, ALU.mult), [tin, "apw"], [ztg])
        S.op("dve", TT(x1, ai, mr, ALU.mult), [tin, "apw"], [ztg])
        S.op("dve", TT(x0, x0, x1, ALU.add), [ztg], [ztg])
        S.op("dve", TT(oi, x0, add[1], ALU.add), [ztg, tin], [tout])

    def attention(self, ti, qT, obT, f32, b16):
        S = self.S
        ps = self.ps
        NT, nh = self.NT, self.n_halo
        acc = [f32(T) for _ in range(4)]
        pt = [b16(T) for _ in range(4)]
        sc = [f32(T) for _ in range(2)]
        rz = f32(T)
        mt2, q4 = ti // 4, ti % 4
        for hs in range(4):
            hh, po = hs // 2, (hs % 2) * 64
            for g in range(3):
                dl = DIL[g]
                head = 4 * g + hs
                coef = -8.0 * SLOPES[head] * dl
                qc = 2 * g + hh
                if g < 2:
                    nun, nq = 4, 128
                else:
                    nun, nq = 16, 32
                plist = []
                for which in (0, 1):
                    bank = which
                    exists = []
                    for u in range(nun):
                        if g == 0:
                            blk = 4 * ti + u - (1 - which)
                            ok = blk >= 0
                            halo = blk < 4 * nh
                            kcols = slice((blk % 8) * 128, (blk % 8) * 128 + 128) if ok else None
                            qcols = slice(u * 128, (u + 1) * 128)
                            vslot = blk % 8
                        elif g == 1:
                            tt = ti - (1 - which)
                            ok = tt >= 0
                            halo = tt < nh
                            kcols = slice((tt % 2) * T + u, (tt % 2 + 1) * T, 4) if ok else None
                            qcols = slice(u, T, 4)
                            vslot = (tt % 2) * 4 + u
                        else:
                            m = mt2 - (1 - which)
                            ok = m >= 0
                            halo = (m * 4) < nh
                            kcols = slice(m * 4 * T + u, (m * 4 + 4) * T, 16) if ok else None
                            qcols = slice(u, T, 16)
                            vslot = (m % 2) * 16 + u
                        exists.append((ok, halo, kcols, qcols, vslot))
                    if not any(x[0] for x in exists):
                        continue
                    for u, (ok, halo, kcols, qcols, vslot) in enumerate(exists):
                        S.op("pe", lambda e, u=u, kcols=kcols, qcols=qcols, bank=bank, nq=nq, po=po, g=g, hh=hh, qc=qc, ok=ok: e.matmul(
                            ps[:, bank, u * nq:(u + 1) * nq],
                            lhsT=(self.kT[g][po:po + 64, hh, kcols] if ok else self.kT[g][po:po + 64, hh, 0:128]),
                            rhs=qT[po:po + 64, qc, qcols], start=True, stop=True),
                            reads=[("kT", g), ("qT", qc)], writes=[("ps", bank)])
                    scb = sc[which]
                    sv = scb.rearrange("p (u q) -> p u q", q=nq)
                    pv_ = ps[:, bank, :].rearrange("p (u q) -> p u q", q=nq)
                    if which == 1:
                        qsl = slice(0, 128) if g < 2 else slice(32 * q4, 32 * q4 + 32)
                        S.op("dve", lambda e, sv=sv, pv_=pv_, qsl=qsl, coef=coef, nun=nun, nq=nq: e.scalar_tensor_tensor(
                            out=sv, in0=self.dm[:, 0:1, qsl].to_broadcast([128, nun, nq]), scalar=coef, in1=pv_, op0=ALU.mult, op1=ALU.add),
                            reads=["dm", ("ps", bank)], writes=[("sc", which)])
                    else:
                        groups = {}
                        for u, x in enumerate(exists):
                            var = 2 if (x[1] or not x[0]) else 1
                            if not x[0]:
                                var = 3
                            groups.setdefault(var, []).append(u)
                        for var, us in groups.items():
                            u0, u1 = us[0], us[-1] + 1
                            qsl = slice(0, 128) if g < 2 else slice(32 * q4, 32 * q4 + 32)
                            if var == 3:
                                S.op("dve", lambda e, sv=sv, u0=u0, u1=u1: e.memset(sv[:, u0:u1, :], -8.0 * MASKV), reads=[("ps", bank)], writes=[("sc", which)])
                            else:
                                S.op("dve", lambda e, sv=sv, pv_=pv_, qsl=qsl, coef=coef, u0=u0, u1=u1, nq=nq, var=var: e.scalar_tensor_tensor(
                                    out=sv[:, u0:u1, :], in0=self.dm[:, var:var + 1, qsl].to_broadcast([128, u1 - u0, nq]), scalar=coef,
                                    in1=pv_[:, u0:u1, :], op0=ALU.mult, op1=ALU.add),
                                    reads=["dm", ("ps", bank)], writes=[("sc", which)])
                    pbuf = pt[(g % 2) * 2 + which]
                    ptag = ("pt", (g % 2) * 2 + which)
                    S.op("act", lambda e, scb=scb, pbuf=pbuf: e.activation(out=pbuf, in_=scb, func=AF.Exp, scale=0.125),
                         reads=[("sc", which)], writes=[ptag])
                    plist.append((which, exists, pbuf, ptag))
                ob = 2 + g % 2
                for u in range(nun):
                    for pi_, (which, exists, pbuf, ptag) in enumerate(plist):
                        ok, halo, kcols, qcols, vslot = exists[u]
                        vs = vslot if ok else 0
                        S.op("pe", lambda e, u=u, vs=vs, ob=ob, nq=nq, pbuf=pbuf, g=g, hs=hs, st_=(pi_ == 0), sp_=(pi_ == len(plist) - 1): e.matmul(
                            ps[0:65, ob, u * nq:(u + 1) * nq], lhsT=self.V[g][:, vs, hs, 0:65], rhs=pbuf[:, u * nq:(u + 1) * nq],
                            start=st_, stop=sp_),
                            reads=[("V", g), ptag], writes=[("ps", ob)])
                ob = 2 + g % 2
                a = acc[hs]
                if g == 0:
                    S.op("act", lambda e, a=a, ob=ob: e.copy(out=a[0:65, :], in_=ps[0:65, ob, :]), reads=[("ps", ob)], writes=[("acc", hs)])
                else:
                    av = a[0:65, :].rearrange("p (m r) -> p r m", r=dl)
                    pvw = ps[0:65, ob, :].rearrange("p (r m) -> p r m", r=dl)
                    S.op("dve", lambda e, av=av, pvw=pvw: e.tensor_tensor(out=av, in0=av, in1=pvw, op=ALU.add),
                         reads=[("ps", ob), ("acc", hs)], writes=[("acc", hs)])
            a = acc[hs]
            S.op("dve", lambda e, a=a: e.reciprocal(out=rz[64:65, :], in_=a[64:65, :]), reads=[("acc", hs)], writes=["rz"])
            S.op("pe", lambda e: e.matmul(ps[0:64, 4, :], lhsT=self.ones1[64:65, 0:64], rhs=rz[64:65, :], start=True, stop=True),
                 reads=["rz", "ones1"], writes=[("ps", 4)])
            S.op("dve", lambda e, a=a, hs=hs, hh=hh, po=po: e.tensor_tensor(out=obT[po:po + 64, hh, :], in0=a[0:64, :], in1=ps[0:64, 4, :], op=ALU.mult)
                 if po == 0 else e.tensor_tensor(out=obT[po:po + 64, hh, :], in0=a[0:64, :], in1=ps[0:64, 4, :], op=ALU.mult),
                 reads=[("acc", hs), ("ps", 4)], writes=[("obT", hh)])

    def _prev_exists(self, g, ti):
        if g == 0:
            return [(4 * ti + u - 1 >= 0,) for u in range(4)]
        if g == 1:
            return [(ti - 1 >= 0,)] * 4
        return [(ti // 4 - 1 >= 0,)] * 16

    def mix_out(self, ti, obT, f32, b16, Tn=T):
        S = self.S
        d = self.din
        ps = self.ps
        ysm = self.ysm
        ya = b16(8 * T).rearrange("p (c t) -> p c t", c=8)
        mg = b16(8 * T).rearrange("p (c t) -> p c t", c=8)
        sgt = [f32(T) for _ in range(2)]
        m1 = [f32(T) for _ in range(2)]

        def cons_glu(oc, bank):
            sb_ = sgt[oc % 2]
            S.op("act", lambda e: e.activation(out=sb_[:, :Tn], in_=ps[:, bank, :Tn], func=AF.Sigmoid), reads=[("ps", bank)], writes=[("sgt", oc % 2)])
            S.op("dve", lambda e: e.tensor_tensor(out=ya[:, oc, :Tn], in0=sb_[:, :Tn], in1=ysm[:, oc, :Tn], op=ALU.mult),
                 reads=[("sgt", oc % 2), ("ysm", oc)], writes=[("ya", oc)])
        self.linear(d["w_glu"], 4, ysm, "ysm", 8, Tn, cons_glu)

        def cons_ga(oc, bank):
            S.op("act", lambda e: e.activation(out=mg[:, oc, :Tn], in_=ps[:, bank, :Tn], func=AF.Sigmoid), reads=[("ps", bank)], writes=[("mg", oc)])
        self.linear(d["w_ga"], 4, self.hT, "hT", 8, Tn, cons_ga)

        def cons_pa(oc, bank):
            S.op("dve", lambda e: e.tensor_tensor(out=mg[:, oc, :Tn], in0=mg[:, oc, :Tn], in1=ps[:, bank, :Tn], op=ALU.mult),
                 reads=[("ps", bank), ("mg", oc)], writes=[("mg", oc)])
        self.linear(d["w_pa"], 4, ya, "ya", 8, Tn, cons_pa)
        gbuf = ya

        def cons_gb(oc, bank):
            S.op("act", lambda e: e.activation(out=gbuf[:, oc, :Tn], in_=ps[:, bank, :Tn], func=AF.Sigmoid), reads=[("ps", bank)], writes=[("ya", oc)])
        self.linear(d["w_gb"], 4, self.hT, "hT", 8, Tn, cons_gb)

        def cons_pb(oc, bank):
            mm = m1[oc % 2]
            S.op("dve", lambda e: e.tensor_tensor(out=mm[:, :Tn], in0=gbuf[:, oc, :Tn], in1=ps[:, bank, :Tn], op=ALU.mult),
                 reads=[("ps", bank), ("ya", oc)], writes=[("m1", oc % 2)])
            S.op("dve", lambda e: e.tensor_tensor(out=mg[:, oc, :Tn], in0=mg[:, oc, :Tn], in1=mm[:, :Tn], op=ALU.add),
                 reads=[("m1", oc % 2), ("mg", oc)], writes=[("mg", oc)])
        self.linear(d["w_pb"], 4, obT, "obT", 2, Tn, cons_pb)

        def cons_out(oc, bank):
            S.op("dve", lambda e: e.tensor_tensor(out=self.xT[:, oc, :Tn], in0=self.xT[:, oc, :Tn], in1=ps[:, bank, :Tn], op=ALU.add),
                 reads=[("ps", bank), ("xT", oc)], writes=[("xT", oc)])
        self.linear(d["w_out"], 4, mg, "mg", 8, Tn, cons_out)

    def sample_pass(self):
        S = self.S
        d = self.din
        ps = self.ps
        xT, hT = self.xT, self.hT
        Tn = 4
        TT = lambda o, a, b, op: (lambda e: e.tensor_tensor(out=o, in0=a, in1=b, op=op))
        V = lambda fn, r, w: S.op("dve", fn, reads=r, writes=w)
        for g, nm in enumerate(("0", "1", "2")):
            W = WIN[g]
            for b in range(4):
                for r0 in range(0, W - 1, 256):
                    r1 = min(W - 1, r0 + 256)
                    S.dma("sp", self.dout["skv" + nm][b, r0:r1, :], d["c" + nm][b, r0 + 1:r1 + 1, :])
        S.dma("sp", xT[:, :, 0:Tn], d["xsT"].rearrange("(c p) t -> p c t", p=128), writes=[("xT", c) for c in range(8)])
        self.ffn_block("1", 0, Tn)
        f32, b16, off = self.carve()
        self.rmsnorm(8, Tn)
        u_bf = b16(8 * T).rearrange("p (c t) -> p c t", c=8)
        ysm = b16(8 * T).rearrange("p (c t) -> p c t", c=8)
        self.ysm = ysm
        obT = b16(2 * T).rearrange("p (c t) -> p c t", c=2)

        def cons_u(oc, bank):
            S.op("act", lambda e: e.copy(out=u_bf[:, oc, 0:Tn], in_=ps[:, bank, 0:Tn]), reads=[("ps", bank)], writes=[("u", oc)])
        self.linear(d["w_u"], 4, hT, "hT", 8, Tn, cons_u)
        qs = f32(768).rearrange("p (g c) -> p g c", g=3)
        knv = f32(3 * 512).rearrange("p (g c) -> p g c", g=3)
        sq = f32(256)
        ssq = f32(4)
        for g in range(3):
            S.dma("pool", self.wKV[:, :, :], d["w_kv"][g], writes=["wKV"] + [("wD", 3), ("wD", 4), ("wD", 5), ("wD", 6)])
            b = self._wrot
            self._wrot = (self._wrot + 1) % 4
            S.dma("pool", self.wA[b][:, :, :], d["w_qs"][g], writes=[("wA", b)])
            for k in range(8):
                S.op("pe", lambda e, k=k: e.matmul(ps[0:Tn, 0, :], lhsT=hT[:, k, 0:Tn], rhs=self.wKV[:, k, :], start=(k == 0), stop=(k == 7)),
                     reads=["wKV", ("hT", k)] + [("wD", 3), ("wD", 4), ("wD", 5), ("wD", 6)], writes=[("ps", 0)])
            for k in range(8):
                S.op("pe", lambda e, k=k, b=b: e.matmul(ps[0:Tn, 1, 0:256], lhsT=hT[:, k, 0:Tn], rhs=self.wA[b][:, k, :], start=(k == 0), stop=(k == 7)),
                     reads=[("wA", b), ("hT", k)], writes=[("ps", 1)])
            for (bank, gi, dst) in ((0, 1, knv[0:Tn, g, 0:256]), (1, 0, qs[0:Tn, g, :])):
                src = ps[0:Tn, bank, 0:256]
                S.op("act", lambda e, src=src: e.activation(out=sq[0:Tn, :], in_=src, func=AF.Square), reads=[("ps", bank)], writes=["s_sq"])
                V(lambda e: e.tensor_reduce(out=ssq[0:Tn, :], in_=sq[0:Tn, :].rearrange("p (h c) -> p h c", h=4), axis=AX.X, op=ALU.add), ["s_sq"], ["s_ssq"])
                S.op("act", lambda e: e.activation(out=ssq[0:Tn, :], in_=ssq[0:Tn, :], func=AF.Sqrt, bias=self.eps[0:Tn, 0:1], scale=1.0 / 64),
                     reads=["s_ssq", "eps"], writes=["s_ssq"])
                V(lambda e: e.reciprocal(out=ssq[0:Tn, :], in_=ssq[0:Tn, :]), ["s_ssq"], ["s_ssq"])
                V(lambda e, src=src, dst=dst: e.tensor_tensor(out=dst.rearrange("p (h c) -> p h c", h=4), in0=src.rearrange("p (h c) -> p h c", h=4),
                                                               in1=ssq[0:Tn, :].unsqueeze(2).to_broadcast([Tn, 4, 64]), op=ALU.mult),
                  [("ps", bank), "s_ssq"], ["qkv_s"])
                V(lambda e, dst=dst, gi=gi: e.tensor_tensor(out=dst, in0=dst, in1=self.gain_bc[0:Tn, gi, :], op=ALU.mult), ["qkv_s", "gain_bc"], ["qkv_s"])
            S.op("act", lambda e, g=g: e.copy(out=knv[0:Tn, g, 256:512], in_=ps[0:Tn, 0, 256:512]), reads=[("ps", 0)], writes=["qkv_s"])
            S.dma("sp", self.dout["skv%d" % g][:, WIN[g] - 1, :], knv[0:Tn, g, :], reads=["qkv_s"])
        mark = off[0]
        S.barrier()
        h0 = f32(2 * 32 * 4).rearrange("p (r a t) -> p r a t", r=2, a=32)
        h1 = f32(2 * 32 * 4).rearrange("p (r a t) -> p r a t", r=2, a=32)
        hb = [b16(32 * 4).rearrange("p (a t) -> p a t", a=32) for _ in range(2)]
        t1 = f32(T)
        t2 = f32(T)
        s1 = f32(128).rearrange("p (a t) -> p a t", a=32)
        s2 = f32(128).rearrange("p (a t) -> p a t", a=32)
        btb = [b16(4 * 2 * 128).rearrange("p (a r c) -> p a r c", a=4, r=2) for _ in range(2)]
        ctb = [b16(2 * 4 * 128).rearrange("p (r a c) -> p r a c", r=2, a=4) for _ in range(2)]
        S.dma("sp", h0, d["h0"], writes=["h0"])
        for ch in range(8):
            bb = btb[ch % 2]
            S.dma("sp", bb, self.bt_d[ch], reads=["bt_d"], writes=[("btb", ch % 2)])
            for qq in range(4):
                P = 4 * ch + qq
                for ri in range(2):
                    S.op("pe", lambda e, bb=bb, qq=qq, ri=ri, P=P, ch=ch: e.matmul(ps[:, ri, P * 4:(P + 1) * 4], lhsT=bb[:, qq, ri, :], rhs=u_bf[:, ch, 0:Tn],
                                                                              start=True, stop=True),
                         reads=[("btb", ch % 2), ("u", ch)], writes=[("ps", ri)])
        ar = self.apw[:, 0, 0, :].unsqueeze(2).to_broadcast([128, 32, 4])
        ai = self.apw[:, 0, 1, :].unsqueeze(2).to_broadcast([128, 32, 4])
        bur = ps[:, 0, 0:128].rearrange("p (a t) -> p a t", a=32)
        bui = ps[:, 1, 0:128].rearrange("p (a t) -> p a t", a=32)
        V(TT(s1, h0[:, 0], ar, ALU.mult), ["h0", "apw"], ["s1"])
        V(TT(s2, h0[:, 1], ai, ALU.mult), ["h0", "apw"], ["s2"])
        V(TT(s1, s1, s2, ALU.subtract), ["s1", "s2"], ["s1"])
        V(TT(h1[:, 0], s1, bur, ALU.add), ["s1", ("ps", 0)], ["h1"])
        V(TT(s1, h0[:, 0], ai, ALU.mult), ["h0", "apw"], ["s1"])
        V(TT(s2, h0[:, 1], ar, ALU.mult), ["h0", "apw"], ["s2"])
        V(TT(s1, s1, s2, ALU.add), ["s1", "s2"], ["s1"])
        V(TT(h1[:, 1], s1, bui, ALU.add), ["s1", ("ps", 1)], ["h1"])
        S.dma("sp", self.dout["sst"], h1, reads=["h1"])
        V(lambda e: e.tensor_copy(out=hb[0], in_=h1[:, 0]), ["h1"], ["hb_s"])
        V(lambda e: e.tensor_scalar(out=hb[1], in0=h1[:, 1], scalar1=-1.0, scalar2=None, op0=ALU.mult), ["h1"], ["hb_s"])
        for ch in range(8):
            cb = ctb[ch % 2]
            S.dma("pool", cb, d["ssm_CT"][:, :, 4 * ch:4 * ch + 4, :], writes=[("ctb", ch % 2)])
            yb = 4 + ch % 2
            i = 0
            for qq in range(4):
                for ri in range(2):
                    S.op("pe", lambda e, cb=cb, qq=qq, ri=ri, yb=yb, i=i, ch=ch: e.matmul(ps[:, yb, 0:Tn], lhsT=cb[:, ri, qq, :], rhs=hb[ri][:, 4 * ch + qq, :],
                                                                                     start=(i == 0), stop=False),
                         reads=[("ctb", ch % 2), "hb_s"], writes=[("ps", yb)])
                    i += 1
            S.op("pe", lambda e, yb=yb, ch=ch: e.matmul(ps[:, yb, 0:Tn], lhsT=self.diagD[:, ch, :], rhs=u_bf[:, ch, 0:Tn], start=False, stop=True),
                 reads=["diagD", ("u", ch)], writes=[("ps", yb)])
            self.gelu_evict(yb, ysm[:, ch, 0:Tn], ("ysm", ch), t1[:, 0:Tn], t2[:, 0:Tn], Tn)
        S.barrier()
        off[0] = mark
        pens = f32(12)
        sel4 = f32(4 * 128).rearrange("p (b m) -> p b m", b=4)
        selc = f32(16).rearrange("p (b m) -> p b m", b=4)
        kc = [f32(512) for _ in range(3)]
        prod = f32(256)
        sc = f32(12)
        stg = f32(780)
        OZ = f32(780)
        pn = f32(12)
        zs = f32(4)
        os_ = f32(256)
        S.dma("sp", pens, d["pens"], writes=["pens"])
        S.dma("sp", sel4[0:4], d["sel4"], writes=["sel4"])
        S.dma("sp", selc, d["selc"], writes=["selc"])
        for b in range(4):
            S.op("pe", lambda e, b=b: e.matmul(ps[:, 0, :], lhsT=sel4[0:4, b, :], rhs=qs[0:4, :, :].rearrange("p g c -> p (g c)")[:, 0:512], start=True, stop=True),
                 reads=["sel4", "qkv_s"], writes=[("ps", 0)])
            S.op("pe", lambda e, b=b: e.matmul(ps[:, 1, 0:256], lhsT=sel4[0:4, b, :], rhs=qs[0:4, :, :].rearrange("p g c -> p (g c)")[:, 512:768], start=True, stop=True),
                 reads=["sel4", "qkv_s"], writes=[("ps", 1)])
            for g in range(3):
                W, dl = WIN[g], DIL[g]
                S.dma("sp", kc[g], d["c%d" % g][b, 0:W:dl, :], writes=[("kc", g)])
                qb = ps[:, 0, g * 256:(g + 1) * 256] if g < 2 else ps[:, 1, 0:256]
                V(TT(prod, kc[g][:, 0:256], qb, ALU.mult), [("kc", g), ("ps", 0), ("ps", 1)], ["prod"])
                V(lambda e, g=g: e.tensor_reduce(out=sc[:, 4 * g:4 * g + 4], in_=prod.rearrange("p (h c) -> p h c", h=4), axis=AX.X, op=ALU.add),
                  ["prod"], ["sc_s"])
            V(TT(sc, sc, pens, ALU.add), ["sc_s", "pens"], ["sc_s"])
            S.op("act", lambda e: e.activation(out=stg[:, 768:780], in_=sc, func=AF.Exp, scale=0.125), reads=["sc_s"], writes=["stg"])
            for g in range(3):
                V(lambda e, g=g: e.tensor_tensor(out=stg[:, g * 256:(g + 1) * 256].rearrange("p (h c) -> p h c", h=4),
                                                  in0=kc[g][:, 256:512].rearrange("p (h c) -> p h c", h=4),
                                                  in1=stg[:, 768 + 4 * g:772 + 4 * g].unsqueeze(2).to_broadcast([128, 4, 64]), op=ALU.mult),
                  [("kc", g), "stg"], ["stg"])
            S.op("pe", lambda e, b=b: e.matmul(ps[0:4, 2, :], lhsT=selc[:, b, :], rhs=stg[:, 0:512], start=(b == 0), stop=(b == 3)),
                 reads=["selc", "stg"], writes=[("ps", 2)])
            S.op("pe", lambda e, b=b: e.matmul(ps[0:4, 3, 0:268], lhsT=selc[:, b, :], rhs=stg[:, 512:780], start=(b == 0), stop=(b == 3)),
                 reads=["selc", "stg"], writes=[("ps", 3)])
        S.op("act", lambda e: e.copy(out=OZ[0:4, 0:512], in_=ps[0:4, 2, :]), reads=[("ps", 2)], writes=["OZ"])
        S.op("act", lambda e: e.copy(out=OZ[0:4, 512:780], in_=ps[0:4, 3, 0:268]), reads=[("ps", 3)], writes=["OZ"])
        for g in range(3):
            V(TT(prod[0:4, :], qs[0:4, g, :], knv[0:4, g, 0:256], ALU.mult), ["qkv_s"], ["prod"])
            V(lambda e, g=g: e.tensor_reduce(out=pn[0:4, 4 * g:4 * g + 4], in_=prod[0:4, :].rearrange("p (h c) -> p h c", h=4), axis=AX.X, op=ALU.add),
              ["prod"], ["pn"])
        S.op("act", lambda e: e.activation(out=pn[0:4, :], in_=pn[0:4, :], func=AF.Exp, scale=0.125), reads=["pn"], writes=["pn"])
        V(TT(OZ[0:4, 768:780], OZ[0:4, 768:780], pn[0:4, :], ALU.add), ["OZ", "pn"], ["OZ"])
        for g in range(3):
            V(lambda e, g=g: e.tensor_tensor(out=prod[0:4, :].rearrange("p (h c) -> p h c", h=4), in0=knv[0:4, g, 256:512].rearrange("p (h c) -> p h c", h=4),
                                              in1=pn[0:4, 4 * g:4 * g + 4].unsqueeze(2).to_broadcast([4, 4, 64]), op=ALU.mult), ["qkv_s", "pn"], ["prod"])
            V(TT(OZ[0:4, g * 256:(g + 1) * 256], OZ[0:4, g * 256:(g + 1) * 256], prod[0:4, :], ALU.add), ["OZ", "prod"], ["OZ"])
        V(TT(zs[0:4, :], OZ[0:4, 768:772], OZ[0:4, 772:776], ALU.add), ["OZ"], ["zs"])
        V(TT(zs[0:4, :], zs[0:4, :], OZ[0:4, 776:780], ALU.add), ["OZ", "zs"], ["zs"])
        V(lambda e: e.reciprocal(out=zs[0:4, :], in_=zs[0:4, :]), ["zs"], ["zs"])
        V(TT(os_[0:4, :], OZ[0:4, 0:256], OZ[0:4, 256:512], ALU.add), ["OZ"], ["os"])
        V(TT(os_[0:4, :], os_[0:4, :], OZ[0:4, 512:768], ALU.add), ["OZ", "os"], ["os"])
        V(lambda e: e.tensor_tensor(out=os_[0:4, :].rearrange("p (h c) -> p h c", h=4), in0=os_[0:4, :].rearrange("p (h c) -> p h c", h=4),
                                    in1=zs[0:4, :].unsqueeze(2).to_broadcast([4, 4, 64]), op=ALU.mult), ["os", "zs"], ["os"])
        for hh in range(2):
            S.op("pe", lambda e, hh=hh: e.matmul(ps[:, 4 + hh, 0:4], lhsT=os_[0:4, hh * 128:(hh + 1) * 128], rhs=self.identf[0:4, 0:4], start=True, stop=True),
                 reads=["os", "identf"], writes=[("ps", 4 + hh)])
            S.op("act", lambda e, hh=hh: e.copy(out=obT[:, hh, 0:Tn], in_=ps[:, 4 + hh, 0:4]), reads=[("ps", 4 + hh)], writes=[("obT", hh)])
        S.barrier()
        off[0] = mark
        self.mix_out(0, obT, f32, b16, Tn)
        S.barrier()
        self.ffn_block("2", 16, Tn)
        S.dma("sp", self.dout["ysT"].rearrange("(c p) t -> p c t", p=128), xT[:, :, 0:Tn], reads=[("xT", c) for c in range(8)])


_NC_CACHE = {}


def _get_nc():
    if "nc" not in _NC_CACHE:
        kk = K(4, 4, sample=True)
        _NC_CACHE["nc"] = kk.build()
    return _NC_CACHE["nc"]


def kernel(**inp):
    inp = {k: np.asarray(v) for k, v in inp.items()}
    nc = _get_nc()
    shared = prep_shared(inp)
    consts = [host_consts(True), host_consts(False)]
    xp = inp["x_prompt"]
    in_maps = []
    for c in range(8):
        b, half = c // 2, c % 2
        m = dict(shared)
        m.update(consts[half])
        own = xp[b, half * 2048:(half + 1) * 2048]
        halo = xp[b, 0:2048] if half == 1 else np.zeros_like(own)
        m["xT"] = np.ascontiguousarray(np.concatenate([halo, own], axis=0).T)
        sl = slice(4 * c, 4 * c + 4)
        m["xsT"] = np.ascontiguousarray(inp["x_sample"][sl, 0, :].T)
        h0 = np.stack([np.stack([lay_gp(inp["state_ssm_re"][0, 4 * c + t]) for t in range(4)], axis=-1),
                       np.stack([lay_gp(inp["state_ssm_im"][0, 4 * c + t]) for t in range(4)], axis=-1)], axis=1)
        m["h0"] = np.ascontiguousarray(h0)
        m["c0"] = np.ascontiguousarray(inp["cache_kv_w128"][0, sl].reshape(4, 128, 512))
        m["c1"] = np.ascontiguousarray(inp["cache_kv_w512"][0, sl].reshape(4, 512, 512))
        m["c2"] = np.ascontiguousarray(inp["cache_kv_w2048"][0, sl].reshape(4, 2048, 512))
        in_maps.append({k: np.ascontiguousarray(v, dtype=np.float32) for k, v in m.items()})
    res = run_bass_kernel_spmd(nc, in_maps, core_ids=list(range(8)))
    R = res.results
    unlay = lambda a: a.reshape(2, 64, 32).transpose(2, 0, 1).reshape(64, 64)
    y_prompt = np.zeros((4, 4096, 1024), np.float32)
    y_sample = np.zeros((32, 1, 1024), np.float32)
    p_re = np.zeros((1, 4, 64, 64), np.float32)
    p_im = np.zeros((1, 4, 64, 64), np.float32)
    s_re = np.zeros((1, 32, 64, 64), np.float32)
    s_im = np.zeros((1, 32, 64, 64), np.float32)
    pkv = [np.zeros((1, 4, w, 2, 4, 64), np.float32) for w in WIN]
    skv = [np.zeros((1, 32, w, 2, 4, 64), np.float32) for w in WIN]
    for c in range(8):
        b, half = c // 2, c % 2
        r = R[c]
        y_prompt[b, half * 2048:(half + 1) * 2048] = r["yT"].T
        y_sample[4 * c:4 * c + 4, 0, :] = r["ysT"].T
        if half == 1:
            p_re[0, b] = unlay(r["pst"][:, 0, :])
            p_im[0, b] = unlay(r["pst"][:, 1, :])
            for g, w in enumerate(WIN):
                pkv[g][0, b] = r["pkv"][g][2048 - w:].reshape(w, 2, 4, 64)
        for t in range(4):
            s_re[0, 4 * c + t] = unlay(r["sst"][:, 0, :, t])
            s_im[0, 4 * c + t] = unlay(r["sst"][:, 1, :, t])
        for g, w in enumerate(WIN):
            skv[g][0, 4 * c:4 * c + 4] = r["skv%d" % g].reshape(4, w, 2, 4, 64)
    return (y_prompt, y_sample, p_re, p_im, pkv[0], pkv[1], pkv[2], s_re, s_im, skv[0], skv[1], skv[2])
```
